# Optimizing a Trainium2 kernel written in Bass

```python
import jax, jax.numpy as jnp
from jax import lax
import numpy as np

D_MODEL = 1024
BATCH = 16
SEQ = 2048
DEPTH = 1
DEC_BATCH = 8
DEC_SEQ = 64
PAST_LEN = 4096

CHUNK = 64
GLA_HEADS = 4
GLA_DK = 128
GLA_DV = 256
GLA_K_WIDTH = GLA_HEADS * GLA_DK
GLA_V_WIDTH = GLA_HEADS * GLA_DV
GLA_GATE_RANK = 16
GLA_TAU = 16.0
ATT_HEADS = 16
ATT_DH = 64
ATT_WIDTH = ATT_HEADS * ATT_DH
BAND_PAST_CHUNKS = 8
ATT_PAST = BAND_PAST_CHUNKS * CHUNK
REL_CLIP = 128
D_FF = 2816
EPS = 1e-6
NEG_INF = -1e30
PROJ_SPLITS = (GLA_K_WIDTH, GLA_K_WIDTH, GLA_V_WIDTH, GLA_V_WIDTH, GLA_GATE_RANK,
               ATT_WIDTH, ATT_WIDTH, ATT_WIDTH, D_MODEL, D_MODEL)
PROJ_WIDTH = sum(PROJ_SPLITS)

kernel_name = "gla_chunkband_macaron_stream_step"


def rms_norm(x, g):
    x32 = x.astype(jnp.float32)
    y = x32 * lax.rsqrt(jnp.mean(x32 * x32, axis=-1, keepdims=True) + EPS)
    return (y * g.astype(jnp.float32)).astype(x.dtype)


def swiglu_ffn(x, w_in, w_out):
    gate, up = jnp.split(x @ w_in, 2, axis=-1)
    return (jax.nn.silu(gate) * up) @ w_out


def split_projection(n, w_in):
    idx = np.cumsum(PROJ_SPLITS)[:-1].tolist()
    return jnp.split(n @ w_in, idx, axis=-1)


def gla_chunked(q, k, v, log_a, s0):
    B, T, H, dk = q.shape
    dv = v.shape[-1]
    L = min(CHUNK, T)
    N = T // L
    f32 = jnp.float32
    qc = q.astype(f32).reshape(B, N, L, H, dk)
    kc = k.astype(f32).reshape(B, N, L, H, dk)
    vc = v.astype(f32).reshape(B, N, L, H, dv)
    b = jnp.cumsum(log_a.astype(f32).reshape(B, N, L, H, dk), axis=2)
    b_last = b[:, :, -1]
    q_dec = qc * jnp.exp(b)
    k_inv = kc * jnp.exp(-b)
    k_end = kc * jnp.exp(b_last[:, :, None] - b)
    causal = jnp.tril(jnp.ones((L, L), dtype=bool))
    scores = jnp.where(causal, jnp.einsum('bnihd,bnjhd->bnhij', q_dec, k_inv), 0.0)
    o_intra = jnp.einsum('bnhij,bnjhv->bnihv', scores, vc)
    incr = jnp.einsum('bnjhd,bnjhv->bnhdv', k_end, vc)
    decay = jnp.exp(b_last)

    def step(s, inp):
        d, u = inp
        return d[..., None] * s + u, s

    s_final, s_before = lax.scan(step, s0.astype(f32),
                                 (jnp.moveaxis(decay, 1, 0), jnp.moveaxis(incr, 1, 0)))
    o_inter = jnp.einsum('bnihd,nbhdv->bnihv', q_dec, s_before)
    return (o_intra + o_inter).reshape(B, T, H, dv), s_final


def gla_branch(q_a, k_a, v_a, r_a, f_a, w_gla_gate, b_gla_gate, gla_norm, s0):
    B, T, _ = q_a.shape
    q = q_a.reshape(B, T, GLA_HEADS, GLA_DK) * (GLA_DK ** -0.5)
    k = k_a.reshape(B, T, GLA_HEADS, GLA_DK)
    v = v_a.reshape(B, T, GLA_HEADS, GLA_DV)
    log_a = jax.nn.log_sigmoid((f_a @ w_gla_gate + b_gla_gate).astype(jnp.float32)) / GLA_TAU
    log_a = log_a.reshape(B, T, GLA_HEADS, GLA_DK)
    o, s_new = gla_chunked(q, k, v, log_a, s0)
    o = o * lax.rsqrt(jnp.mean(o * o, axis=-1, keepdims=True) + EPS)
    o = o * gla_norm.astype(jnp.float32).reshape(GLA_HEADS, GLA_DV)
    o = o.reshape(B, T, GLA_V_WIDTH) * jax.nn.silu(r_a.astype(jnp.float32))
    return o.astype(q_a.dtype), s_new.astype(q_a.dtype)


def rel_bias(table, q_pos, k_pos):
    rel = jnp.clip(q_pos[:, None] - k_pos[None, :], -REL_CLIP, REL_CLIP) + REL_CLIP
    return table[:, rel].astype(jnp.float32)


def attend(q, k, v, bias, valid):
    s = jnp.einsum('bqhd,bkhd->bhqk', q, k).astype(jnp.float32) * (ATT_DH ** -0.5) + bias
    s = jnp.where(valid, s, NEG_INF)
    p = jax.nn.softmax(s, axis=-1).astype(v.dtype)
    return jnp.einsum('bhqk,bkhd->bqhd', p, v)


def band_attention_prompt(q, k, v, table):
    B, T, H, d = q.shape
    N = T // CHUNK
    band = ATT_PAST + CHUNK
    kp = jnp.pad(k, ((0, 0), (ATT_PAST, 0), (0, 0), (0, 0)))
    vp = jnp.pad(v, ((0, 0), (ATT_PAST, 0), (0, 0), (0, 0)))
    bias = rel_bias(table, jnp.arange(CHUNK) + ATT_PAST, jnp.arange(band))

    def one_chunk(c):
        start = c * CHUNK
        qc = lax.dynamic_slice_in_dim(q, start, CHUNK, axis=1)
        kc = lax.dynamic_slice_in_dim(kp, start, band, axis=1)
        vc = lax.dynamic_slice_in_dim(vp, start, band, axis=1)
        valid = (start - ATT_PAST + jnp.arange(band)) >= 0
        return attend(qc, kc, vc, bias, valid)

    out = lax.map(one_chunk, jnp.arange(N))
    return jnp.moveaxis(out, 0, 1).reshape(B, T, H, d)


def band_attention_step(q, k, v, cache_k, cache_v, table):
    T = q.shape[1]
    C = cache_k.shape[1]
    keys = jnp.concatenate([cache_k.astype(k.dtype), k], axis=1)
    vals = jnp.concatenate([cache_v.astype(v.dtype), v], axis=1)
    bias = rel_bias(table, C + jnp.arange(T), jnp.arange(C + T))
    valid = jnp.ones((C + T,), dtype=bool)
    return attend(q, keys, vals, bias, valid)


def encoder_layer(x, cache_k, cache_v, state_gla, norm_ffn1, w_ffn1_in, w_ffn1_out,
                  norm_mix, w_in, w_gla_gate, b_gla_gate, gla_norm, attn_rel_bias,
                  w_branch_gla, w_branch_att, w_out, norm_ffn2, w_ffn2_in, w_ffn2_out):
    B, T, _ = x.shape
    h = x + 0.5 * swiglu_ffn(rms_norm(x, norm_ffn1), w_ffn1_in, w_ffn1_out)
    n = rms_norm(h, norm_mix)
    q_a, k_a, v_a, r_a, f_a, q_b, k_b, v_b, g_a, g_b = split_projection(n, w_in)
    if state_gla is None:
        s0 = jnp.zeros((B, GLA_HEADS, GLA_DK, GLA_DV), jnp.float32)
    else:
        s0 = state_gla
    o_a, s_new = gla_branch(q_a, k_a, v_a, r_a, f_a, w_gla_gate, b_gla_gate, gla_norm, s0)
    qh = q_b.reshape(B, T, ATT_HEADS, ATT_DH)
    kh = k_b.reshape(B, T, ATT_HEADS, ATT_DH)
    vh = v_b.reshape(B, T, ATT_HEADS, ATT_DH)
    if cache_k is None:
        o_b = band_attention_prompt(qh, kh, vh, attn_rel_bias)
        k_keep, v_keep = kh[:, -ATT_PAST:], vh[:, -ATT_PAST:]
    else:
        o_b = band_attention_step(qh, kh, vh, cache_k, cache_v, attn_rel_bias)
        k_keep, v_keep = kh, vh
    mixed = (jax.nn.sigmoid(g_a) * (o_a @ w_branch_gla)
             + jax.nn.sigmoid(g_b) * (o_b.reshape(B, T, ATT_WIDTH) @ w_branch_att))
    h = h + mixed @ w_out
    h = h + 0.5 * swiglu_ffn(rms_norm(h, norm_ffn2), w_ffn2_in, w_ffn2_out)
    return h, k_keep, v_keep, s_new


def setup_inputs(seed: int = 0) -> dict:
    key = jax.random.key(seed)
    ks = jax.random.split(key, 24)
    f32 = jnp.float32

    def nrm(k, shape, scale):
        return jax.random.normal(k, shape, f32) * scale

    def gain(k, shape):
        return 1.0 + 0.05 * jax.random.normal(k, shape, f32)

    att_cache_len = min(ATT_PAST, PAST_LEN)
    return {
        "x_prompt": nrm(ks[0], (BATCH, SEQ, D_MODEL), 1.0),
        "x_sample": nrm(ks[1], (DEC_BATCH, DEC_SEQ, D_MODEL), 1.0),
        "cache_att_k": nrm(ks[2], (DEPTH, DEC_BATCH, att_cache_len, ATT_HEADS, ATT_DH), 1.0),
        "cache_att_v": nrm(ks[3], (DEPTH, DEC_BATCH, att_cache_len, ATT_HEADS, ATT_DH), 1.0),
        "state_gla": nrm(ks[4], (DEPTH, DEC_BATCH, GLA_HEADS, GLA_DK, GLA_DV), 0.5),
        "norm_ffn1": gain(ks[5], (DEPTH, D_MODEL)),
        "w_ffn1_in": nrm(ks[6], (DEPTH, D_MODEL, 2 * D_FF), D_MODEL ** -0.5),
        "w_ffn1_out": nrm(ks[7], (DEPTH, D_FF, D_MODEL), D_FF ** -0.5),
        "norm_mix": gain(ks[8], (DEPTH, D_MODEL)),
        "w_in": nrm(ks[9], (DEPTH, D_MODEL, PROJ_WIDTH), D_MODEL ** -0.5),
        "w_gla_gate": nrm(ks[10], (DEPTH, GLA_GATE_RANK, GLA_K_WIDTH), GLA_GATE_RANK ** -0.5),
        "b_gla_gate": nrm(ks[11], (DEPTH, GLA_K_WIDTH), 0.1),
        "gla_norm": gain(ks[12], (DEPTH, GLA_V_WIDTH)),
        "attn_rel_bias": nrm(ks[13], (DEPTH, ATT_HEADS, 2 * REL_CLIP + 1), 0.1),
        "w_branch_gla": nrm(ks[14], (DEPTH, GLA_V_WIDTH, D_MODEL), GLA_V_WIDTH ** -0.5),
        "w_branch_att": nrm(ks[15], (DEPTH, ATT_WIDTH, D_MODEL), ATT_WIDTH ** -0.5),
        "w_out": nrm(ks[16], (DEPTH, D_MODEL, D_MODEL), D_MODEL ** -0.5),
        "norm_ffn2": gain(ks[17], (DEPTH, D_MODEL)),
        "w_ffn2_in": nrm(ks[18], (DEPTH, D_MODEL, 2 * D_FF), D_MODEL ** -0.5),
        "w_ffn2_out": nrm(ks[19], (DEPTH, D_FF, D_MODEL), D_FF ** -0.5),
        "norm_final": gain(ks[20], (D_MODEL,)),
    }


def reference(x_prompt, x_sample, cache_att_k, cache_att_v, state_gla, norm_ffn1, w_ffn1_in,
              w_ffn1_out, norm_mix, w_in, w_gla_gate, b_gla_gate, gla_norm, attn_rel_bias,
              w_branch_gla, w_branch_att, w_out, norm_ffn2, w_ffn2_in, w_ffn2_out, norm_final):
    hp, hs = x_prompt, x_sample
    kp_list, vp_list, sp_list, ks_list, vs_list, ss_list = [], [], [], [], [], []
    for l in range(DEPTH):
        layer_w = (norm_ffn1[l], w_ffn1_in[l], w_ffn1_out[l], norm_mix[l], w_in[l],
                   w_gla_gate[l], b_gla_gate[l], gla_norm[l], attn_rel_bias[l],
                   w_branch_gla[l], w_branch_att[l], w_out[l], norm_ffn2[l],
                   w_ffn2_in[l], w_ffn2_out[l])
        hp, kp, vp, sp = encoder_layer(hp, None, None, None, *layer_w)
        hs, kn, vn, sn = encoder_layer(hs, cache_att_k[l], cache_att_v[l], state_gla[l], *layer_w)
        kp_list.append(kp); vp_list.append(vp); sp_list.append(sp)
        ks_list.append(kn); vs_list.append(vn); ss_list.append(sn)
    y_prompt = rms_norm(hp, norm_final)
    y_sample = rms_norm(hs, norm_final)
    new_att_k_prompt = jnp.stack(kp_list)
    new_att_v_prompt = jnp.stack(vp_list)
    new_gla_prompt = jnp.stack(sp_list)
    new_att_k_sample = jnp.stack(ks_list)
    new_att_v_sample = jnp.stack(vs_list)
    new_gla_sample = jnp.stack(ss_list)
    return (y_prompt, y_sample, new_att_k_prompt, new_att_v_prompt, new_gla_prompt,
            new_att_k_sample, new_att_v_sample, new_gla_sample)
```

```python
import contextlib
import numpy as np
import concourse.bass as bass
import concourse.mybir as mybir
from concourse.bass_utils import run_bass_kernel_spmd

F32 = mybir.dt.float32
BF16 = mybir.dt.bfloat16
AF = mybir.ActivationFunctionType
ALU = mybir.AluOpType

D = 1024
DFF = 2816
PW = 8208
EPS = 1e-6
NCORES = 8

COMPUTE = ("pe", "act", "dve")
QUEUES = ("sp", "pool")
NDMASEM = 12

OVERLAPS = {
    "gT": ["ebT", "enbT", "la", "oaT", "obT", "qdT", "kiT"],
    "ebT": ["gT", "oaT"], "oaT": ["gT", "ebT"],
    "enbT": ["gT", "obT"], "obT": ["gT", "enbT"],
    "la": ["gT", "qdT", "kiT"], "qdT": ["gT", "la"], "kiT": ["gT", "la"],
    "dke": ["oa", "h2"], "oa": ["dke", "h2"], "va": ["ob", "h2"], "ob": ["va", "h2"], "h2": ["dke", "oa", "va", "ob"],
    "hout": ["rr", "qbT"], "rr": ["hout"], "qbT": ["hout"],
    "Pb": ["Pbg"], "Pbg": ["Pb"],
}


def kname(k):
    return k[0] if isinstance(k, tuple) else k


class Sched:
    def __init__(self):
        self.ops = []
        self.last_w = {}
        self.readers = {}
        self.nseq = {e: 0 for e in COMPUTE + QUEUES}
        self.pending = {}

    def enter(self, name):
        deps = set()
        others = set(OVERLAPS[name])
        for k, v in self.last_w.items():
            if kname(k) in others:
                deps.add(v)
        for k, v in self.readers.items():
            if kname(k) in others:
                deps.update(v)
        best = {}
        keep = set()
        for d in deps:
            o = self.ops[d]
            if o["dma"]:
                keep.add(d)
            else:
                if o["eng"] not in best or self.ops[best[o["eng"]]]["seq"] < o["seq"]:
                    best[o["eng"]] = d
        keep.update(best.values())
        self.pending[name] = keep

    def op(self, eng, fn, reads=(), writes=(), dma=False):
        idx = len(self.ops)
        raw = set()
        war = set()
        for r in reads:
            if r in self.last_w:
                raw.add(self.last_w[r])
            if kname(r) == "ps":
                for rd in self.readers.get(r, ()):
                    war.add(rd)
        for w in writes:
            if w in self.last_w:
                raw.add(self.last_w[w])
            for rd in self.readers.get(w, ()):
                war.add(rd)
            p = self.pending.get(kname(w))
            if p:
                raw.update(p)
        for r in reads:
            self.readers.setdefault(r, []).append(idx)
        for w in writes:
            self.last_w[w] = idx
            self.readers[w] = []
        seq = self.nseq[eng]
        self.nseq[eng] += 1
        self.ops.append(dict(eng=eng, fn=fn, raw=raw, war=war - raw, dma=dma, seq=seq,
                             sig=False, waits=[], idx=idx))
        return idx

    def barrier(self, eng, deps):
        idx = len(self.ops)
        seq = self.nseq[eng]
        self.nseq[eng] += 1
        self.ops.append(dict(eng=eng, fn=None, raw=set(deps), war=set(), dma=False, seq=seq,
                             sig=False, waits=[], idx=idx))
        return idx

    def analyze(self):
        ops = self.ops
        seen = {e: {x: -1 for x in COMPUTE} for e in COMPUTE + QUEUES}
        seen_dma = {e: set() for e in COMPUTE + QUEUES}
        dma_count = {q: 0 for q in QUEUES}
        dma_hist = {q: [] for q in QUEUES}
        for o in ops:
            e = o["eng"]
            deps = [(d, True) for d in sorted(o["raw"])] + [(d, False) for d in sorted(o["war"])]
            if o["dma"]:
                n = dma_count[e]
                dma_count[e] += 1
                o["dsem"] = n % NDMASEM
                o["dval"] = 16 * (n // NDMASEM + 1)
                if n >= NDMASEM:
                    deps.append((dma_hist[e][n - NDMASEM], True))
                dma_hist[e].append(o["idx"])
            for d, is_raw in deps:
                a = ops[d]
                if a["dma"]:
                    if d in seen_dma[e]:
                        continue
                    o["waits"].append(d)
                    seen_dma[e].add(d)
                else:
                    ae = a["eng"]
                    if a["fn"] is None:
                        continue
                    if ae == e and ae == "pe":
                        continue
                    if seen[e][ae] >= a["seq"]:
                        continue
                    o["waits"].append(d)
                    a["sig"] = True
                for x in COMPUTE:
                    if a["vc"][x] > seen[e][x]:
                        seen[e][x] = a["vc"][x]
            vc = dict(seen[e])
            if e in COMPUTE:
                vc[e] = o["seq"]
            o["vc"] = vc
        cnt = {e: 0 for e in COMPUTE}
        for o in ops:
            if o["eng"] in COMPUTE and o["sig"]:
                cnt[o["eng"]] += 1
                o["sval"] = cnt[o["eng"]]

    def emit(self, sems, dsems, block):
        ops = self.ops
        per = {e: [o for o in ops if o["eng"] == e] for e in COMPUTE + QUEUES}

        def run(engname, eng):
            for o in per[engname]:
                for d in o["waits"]:
                    a = ops[d]
                    if a["dma"]:
                        eng.wait_ge(dsems[a["eng"]][a["dsem"]], a["dval"])
                    else:
                        eng.wait_ge(sems[a["eng"]], a["sval"])
                if o["fn"] is None:
                    continue
                ins = o["fn"](eng)
                if o["dma"]:
                    ins.then_inc(dsems[engname][o["dsem"]], 16)
                elif o["sig"]:
                    ins.then_inc(sems[engname], 1)

        block.tensor(lambda eng: run("pe", eng))
        block.scalar(lambda eng: run("act", eng))
        block.vector(lambda eng: run("dve", eng))
        block.gpsimd(lambda eng: run("pool", eng))
        block.sync(lambda eng: run("sp", eng))


class NullSched:
    def op(self, *a, **k):
        return 0

    def enter(self, name):
        pass


NSLOT = 3
LOOKAHEAD = 2
DIRECT0 = False


class WStream:
    def __init__(self, ring):
        self.ring = ring
        self.plan = []
        self.rearr = []
        self.planning = True
        self.issued = 0
        self.taken = 0
        self.S = None

    def view(self, slot, shape):
        a, b = shape
        return self.ring[:, slot, 0:a * b].rearrange("p (a b) -> p a b", a=a)


class StopEmit(Exception):
    pass


def build(NSEQ, SEQLEN, debug=(), stop_at=None):
    nc = bass.Bass("TRN2", target_bir_lowering=False)
    NTOK = NSEQ * SEQLEN
    NT = SEQLEN // 512

    def din(name, shape, dt=F32):
        return nc.dram_tensor(name, shape, dt, kind="ExternalInput").ap()

    def dout(name, shape):
        return nc.dram_tensor(name, shape, F32, kind="ExternalOutput").ap()

    def dscr(name, shape):
        return nc.dram_tensor(name, shape, BF16, kind="Internal").ap()

    xp = din("xp", [NTOK, D])
    xsm = din("xs", [64, D])
    ck = din("ck", [512, D])
    cv = din("cv", [512, D])
    st = din("st", [4, 128, 256])
    wsrc = dict(
        w1i=din("w1i", [D, 2 * DFF]), w1o=din("w1o", [DFF, D]), win=din("win", [D, PW]),
        wbg=din("wbg", [D, D]), wba=din("wba", [D, D]), wo=din("wo", [D, D]),
        w2i=din("w2i", [D, 2 * DFF]), w2o=din("w2o", [DFF, D]), wgate=din("wgate", [17, 512]))
    gains_d = din("gains", [128, 4, 8])
    gnf_d = din("gnf", [1, D])
    biasT_d = din("biasT", [128, 16, 2, 128])
    cb_d = din("cb", [128, 16])
    consts_d = din("consts", [128, 4, 128])

    yp = dout("yp", [NTOK, D])
    ys = dout("ys", [64, D])
    kpo = dout("kpo", [NSEQ * 512, D])
    vpo = dout("vpo", [NSEQ * 512, D])
    spo = dout("spo", [NSEQ, 4, 128, 256])
    kso = dout("kso", [64, D])
    vso = dout("vso", [64, D])
    sso = dout("sso", [4, 128, 256])
    dbg_out = {}

    wfa_b = dscr("wfa_b", [D, 16])
    wgate_b = dscr("wgate_b", [17, 512])
    wsc_holder = {}

    def sb(name, shape, dt):
        return nc.alloc_sbuf_tensor("s_" + name, shape, dt)

    h = sb("h", [128, 4, D], F32)
    xnT = sb("xnT", [128, 8, 512], BF16)
    mixT = sb("mixT", [128, 8, 512], BF16)
    UG = sb("UG", [128, 12288], BF16)
    gT = UG[:, 0:22 * 512].rearrange("p (j t) -> p j t", j=22)
    ebT = UG[:, 0:4096].bitcast(F32).rearrange("p (h t) -> p h t", h=4)
    oaT = UG[:, 0:4096].rearrange("p (k t) -> p k t", k=8)
    enbT = UG[:, 4096:8192].bitcast(F32).rearrange("p (h t) -> p h t", h=4)
    obT = UG[:, 4096:8192].rearrange("p (k t) -> p k t", k=8)
    la = UG[:, 8192:12288].bitcast(F32).rearrange("p (s d) -> p s d", s=4)
    qdT = UG[:, 8192:10240].rearrange("p (h t) -> p h t", h=4)
    kiT = UG[:, 10240:12288].rearrange("p (h t) -> p h t", h=4)
    U45 = sb("U45", [128, 8192], BF16)
    dke = U45[:, 0:4096].bitcast(F32).rearrange("p (s d) -> p s d", s=4)
    oa = U45[:, 0:4096].rearrange("p (s d) -> p s d", s=4)
    va = U45[:, 4096:8192].rearrange("p (s d) -> p s d", s=4)
    ob = U45[:, 4096:8192].rearrange("p (s d) -> p s d", s=4)
    h2 = U45[:, :].bitcast(F32).rearrange("p (s d) -> p s d", s=4)
    U67 = sb("U67", [128, 8192], BF16)
    rr = U67[:, 0:4096].rearrange("p (s d) -> p s d", s=4)
    qbT = U67[:, 4096:8192].rearrange("p (k t) -> p k t", k=8)
    hout = U67[:, :].bitcast(F32).rearrange("p (s d) -> p s d", s=4)
    kend = sb("kend", [128, 4, 512], BF16)
    faT = sb("faT", [32, 512], BF16)
    Pb = sb("Pb", [128, 4, 5, 128], BF16)
    tmpf = sb("tmpf", [128, 2, 512], F32)
    xb = sb("xb", [128, 2, D], BF16)
    junk = sb("junk", [128, 2, D], BF16)
    ystage = sb("ystage", [128, 2, 512], F32)
    kbT = sb("kbT", [128, 8, 1024], BF16)
    vaug = sb("vaug", [128, 8, 16, 65], BF16)
    Sf = sb("Sf", [128, 4, 256], F32)
    Sb = sb("Sb", [128, 3, 4, 256], BF16)
    dec = sb("dec", [128, 4, 8], F32)
    biasT = sb("biasT", [128, 16, 2, 128], F32)
    cbs = sb("cbs", [128, 16], F32)
    constf = sb("constf", [128, 4, 128], F32)
    identb = sb("identb", [128, 128], BF16)
    gains = sb("gains", [128, 4, 8], F32)
    gnfb = sb("gnfb", [128, D], F32)
    epsb = sb("epsb", [128, 1], F32)
    stat = sb("stat", [128, 32], F32)
    rec = sb("rec", [128, 2, 8], F32)
    wfa = sb("wfa", [128, 8, 16], BF16)
    wgate = sb("wgate", [32, 512], BF16)
    ring = sb("ring", [128, NSLOT, 4096], BF16)
    ps = nc.alloc_psum_tensor("ps", [128, 8, 512], F32)

    TRI = constf[:, 1, :]
    TRIC = constf[:, 2, :]
    CMASK = constf[:, 3, :]

    ws = WStream(ring)
    POOLS = {"A": [0, 1, 2, 3], "B": [4, 5, 6, 7], "C": [0, 1, 2, 3, 4, 5], "D": [6, 7]}
    psrr = {k: 0 for k in POOLS}

    def ps_alloc(pool):
        i = psrr[pool]
        psrr[pool] = (i + 1) % len(POOLS[pool])
        return POOLS[pool][i]

    def bank(b):
        return ps[:, b, :]

    def bankb(b):
        return ps[:, b, :].bitcast(BF16)

    dumps = []

    def emit_all(S):
        for k_ in psrr:
            psrr[k_] = 0

        marks = {}

        def mark(name):
            marks[name] = marks.get(name, 0) + 1
            if stop_at is not None and stop_at == "%s:%d" % (name, marks[name]):
                raise StopEmit()

        def dump(name, ap, key):
            if name not in debug or isinstance(S, NullSched) or name in dbg_out:
                return
            shape = list(ap.shape)
            t = nc.dram_tensor("dbg_" + name, shape, ap.dtype, kind="ExternalOutput").ap()
            dbg_out[name] = t
            S.op("pool", lambda e: e.dma_start(out=t, in_=ap), reads=key, dma=True)

        S.op("pool", lambda e: e.dma_start(out=wfa_b, in_=wsrc["win"][:, 3072:3088]), writes=["wfa_b"], dma=True)
        S.op("pool", lambda e: e.dma_start(out=wgate_b, in_=wsrc["wgate"]), writes=["wgate_b"], dma=True)
        if not ws.planning and not DIRECT0:
            wsc = wsc_holder["t"]
            for j in range(ws.nbt):
                parts_j, shape_j = ws.plan[j]
                for (name_j, reg_j, fn_j), rearr_j in zip(parts_j, ws.rearr[j]):
                    src_j = reg_j[1](wsrc[name_j]).rearrange(rearr_j, p=128)
                    dst_j = fn_j(wsc[j])
                    S.op("pool", (lambda e, d=dst_j, s_=src_j: e.dma_start(out=d, in_=s_)), writes=[("wsc", j)], dma=True)

        S.op("sp", lambda e: e.dma_start(out=constf[:], in_=consts_d), writes=["constf"], dma=True)
        S.op("sp", lambda e: e.dma_start(out=gains[:], in_=gains_d), writes=["gains"], dma=True)
        S.op("sp", lambda e: e.dma_start(out=gnfb[:], in_=gnf_d.partition_broadcast(128)), writes=["gnfb"], dma=True)
        S.op("sp", lambda e: e.dma_start(out=biasT[:], in_=biasT_d), writes=["biasT"], dma=True)
        S.op("sp", lambda e: e.dma_start(out=cbs[:], in_=cb_d), writes=["cbs"], dma=True)
        S.op("dve", lambda e: e.tensor_copy(out=identb[:], in_=constf[:, 0, :]), reads=["constf"], writes=["identb"])
        S.op("dve", lambda e: e.memset(epsb[:], EPS), writes=["epsb"])
        S.op("dve", lambda e: e.memset(faT[:], 1.0), writes=["faT"])
        S.op("dve", lambda e: e.memset(vaug[:], 1.0), writes=[("vaug", sl_, hf_) for sl_ in range(8) for hf_ in range(2)])
        S.op("sp", lambda e: e.dma_start(out=wfa[:], in_=wfa_b.rearrange("(k p) n -> p k n", p=128)),
             reads=["wfa_b"], writes=["wfa"], dma=True)
        S.op("sp", lambda e: e.dma_start(out=wgate[0:17, :], in_=wgate_b), reads=["wgate_b"], writes=["wgate"], dma=True)

        mark("init")
        def getw(parts, shape):
            if ws.planning:
                ws.plan.append(([(n_, r_, f_) for (n_, r_, f_, _) in parts], shape))
                ws.rearr.append([x[3] for x in parts])
                return ws.view(0, shape), ("wring", 0)
            i = ws.taken
            assert ws.plan[i][1] == shape, (i, ws.plan[i][1], shape)
            wsc = wsc_holder["t"]
            while ws.issued < min(len(ws.plan), i + LOOKAHEAD + 1):
                j = ws.issued
                parts_j, shape_j = ws.plan[j]
                slot = j % NSLOT
                jb = j % ws.nbt
                n_el = shape_j[0] * shape_j[1]
                if j < ws.nbt and DIRECT0:
                    for (name_j, reg_j, fn_j), rearr_j in zip(parts_j, ws.rearr[j]):
                        src_j = reg_j[1](wsrc[name_j]).rearrange(rearr_j, p=128)
                        dst_j = fn_j(ring[:, slot, :])
                        S.op("pool", (lambda e, d=dst_j, s_=src_j: e.dma_start(out=d, in_=s_)), writes=[("wring", slot)], dma=True)
                    S.op("sp", (lambda e, slot=slot, jb=jb, n_el=n_el: e.dma_start(out=wsc[jb][:, 0:n_el], in_=ring[:, slot, 0:n_el])),
                         reads=[("wring", slot)], writes=[("wsc", jb)], dma=True)
                else:
                    S.op("sp", (lambda e, slot=slot, jb=jb, n_el=n_el: e.dma_start(out=ring[:, slot, 0:n_el], in_=wsc[jb][:, 0:n_el])),
                         reads=[("wsc", jb)], writes=[("wring", slot)], dma=True)
                ws.issued += 1
            ws.taken += 1
            slot = i % NSLOT
            return ws.view(slot, shape), ("wring", slot)

        def v3(base, a_, b_):
            return base[:, 0:a_ * b_].rearrange("p (a b) -> p a b", a=a_)

        def colreg(c0, n):
            return ((0, c0), lambda t: t[:, c0:c0 + n])

        def wk(name, c0, ncols):
            return getw([(name, colreg(c0, ncols), lambda base: v3(base, 8, ncols), "(k p) n -> p k n")], (8, ncols))

        def wk2(name1, c1, name2, c2, n):
            return getw([(name1, colreg(c1, n), lambda base: v3(base, 8, 2 * n)[:, :, 0:n], "(k p) n -> p k n"),
                         (name2, colreg(c2, n), lambda base: v3(base, 8, 2 * n)[:, :, n:2 * n], "(k p) n -> p k n")],
                        (8, 2 * n))

        def wj(name, j0, nj, c0):
            reg = ((j0, c0), lambda t: t[j0 * 128:(j0 + nj) * 128, c0:c0 + 512])
            return getw([(name, reg, lambda base: v3(base, nj, 512), "(j p) n -> p j n")], (nj, 512))

        def mmgroup(out_ap, pairs, reads, writes, start=True, stop=True):
            def fn(e):
                n = len(pairs)
                ins = None
                for i, (l, r) in enumerate(pairs):
                    ins = e.matmul(out_ap, lhsT=l, rhs=r, start=(start and i == 0), stop=(stop and i == n - 1))
                return ins
            return S.op("pe", fn, reads=reads, writes=writes)

        def rstd_of(src_ap, TP, n, col, junk_key, junk_ap, reads):
            S.op("act", lambda e: e.activation(out=junk_ap, in_=src_ap, func=AF.Square, accum_out=stat[:TP, col:col + 1]),
                 reads=reads, writes=[junk_key, ("stat", col)])
            S.op("act", lambda e: e.activation(out=stat[:TP, col + 1:col + 2], in_=stat[:TP, col:col + 1], func=AF.Sqrt,
                                               scale=1.0 / n, bias=epsb[:TP, 0:1]),
                 reads=[("stat", col), "epsb"], writes=[("stat", col + 1)])
            S.op("dve", lambda e: e.reciprocal(out=stat[:TP, col + 1:col + 2], in_=stat[:TP, col + 1:col + 2]),
                 reads=[("stat", col + 1)], writes=[("stat", col + 1)])

        def sqrt_table_prefetch():
            S.op("act", lambda e: e.activation(out=stat[0:1, 21:22], in_=epsb[0:1, 0:1], func=AF.Sqrt),
                 reads=["epsb"], writes=[("stat", 21)])

        def transpose_sub(src_ap, TP, dstT, s, gidx, reads, dstkey, pool="A"):
            b = ps_alloc(pool)
            pb_ = bankb(b)

            def fn(e):
                ins = None
                for k in range(8):
                    ins = e.transpose(out=pb_[:, k * 128:k * 128 + TP], in_=src_ap[:, k * 128:(k + 1) * 128],
                                      identity=identb[:TP, :TP])
                return ins
            S.op("pe", fn, reads=list(reads) + ["identb"], writes=[("ps", b)])
            src3 = pb_.rearrange("p (k t) -> p k t", k=8)[:, :, 0:TP]
            dst3 = dstT[:, :, s * 128:s * 128 + TP]
            if gidx is None:
                S.op("act", lambda e: e.activation(out=dst3, in_=src3, func=AF.Copy),
                     reads=[("ps", b)], writes=[dstkey])
            else:
                g3 = gains[:, gidx, :].unsqueeze(2).broadcast_to([128, 8, TP])
                S.op("dve", lambda e: e.tensor_tensor(out=dst3, in0=src3, in1=g3, op=ALU.mult),
                     reads=[("ps", b), "gains"], writes=[dstkey])

        def norm_stats(NS, TP, src, srcname):
            for s in range(NS):
                if s < 2:
                    S.op("act", (lambda e, s=s: e.activation(out=junk[:TP, 0, :], in_=src[:TP, s, :], func=AF.Square,
                                                             accum_out=stat[:TP, s:s + 1])),
                         reads=[(srcname, s)], writes=[("junk", 0), ("stat", s)])
                else:
                    S.op("dve", (lambda e, s=s: e.scalar_tensor_tensor(
                        out=junk[:TP, 1, :], in0=src[:TP, s, :], scalar=1.0, in1=src[:TP, s, :],
                        op0=ALU.mult, op1=ALU.mult, accum_out=stat[:TP, s:s + 1])),
                        reads=[(srcname, s)], writes=[("junk", 1), ("stat", s)])
            S.op("act", lambda e: e.activation(out=stat[:TP, 4:4 + NS], in_=stat[:TP, 0:NS], func=AF.Sqrt,
                                               scale=1.0 / D, bias=epsb[:TP, 0:1]),
                 reads=[("stat", s) for s in range(NS)] + ["epsb"], writes=[("stat", 4 + s) for s in range(NS)])
            S.op("dve", lambda e: e.reciprocal(out=stat[:TP, 4:4 + NS], in_=stat[:TP, 4:4 + NS]),
                 reads=[("stat", 4 + s) for s in range(NS)], writes=[("stat", 4 + s) for s in range(NS)])

        def norm_apply(NS, TP, gidx, dstT, dstname, src, srcname, scol=4):
            for s in range(NS):
                pbuf = s % 2
                S.op("act", (lambda e, s=s, pbuf=pbuf: e.activation(
                    out=xb[:TP, pbuf, :], in_=src[:TP, s, :], func=AF.Copy, scale=stat[:TP, scol + s:scol + s + 1])),
                    reads=[(srcname, s), ("stat", scol + s)], writes=[("xb", pbuf)])
                transpose_sub(xb[:TP, pbuf, :], TP, dstT, s, gidx, [("xb", pbuf)], (dstname, s))

        def norm_T(T, NS, TP, gidx, dstT, dstname, src=None, srcname="h"):
            src = h if src is None else src
            norm_stats(NS, TP, src, srcname)
            norm_apply(NS, TP, gidx, dstT, dstname, src, srcname)

        def ffn(T, NS, TP, gidx, win_name, wout_name, dst=None, dstname="h", res=None, resname="h", norm_done=False,
                mid_hook=None, pre_hook=None):
            dst = h if dst is None else dst
            res = h if res is None else res
            S.enter("gT")
            if not norm_done:
                norm_T(T, NS, TP, gidx, xnT, "xnT")
            if pre_hook is not None:
                pre_hook()
            xkeys = [("xnT", s) for s in range(NS)]
            dump("xnT_" + win_name, xnT[:, :, 0:T], xkeys)
            for blk in range(11):
                if pending_final and blk >= 1:
                    pending_final.pop(0)()
                w, kw = wk2(win_name, blk * 256, win_name, DFF + blk * 256, 256)
                for c in range(2):
                    j = blk * 2 + c
                    bg = ps_alloc("A")
                    bu = ps_alloc("A")
                    mmgroup(bank(bg)[:, 0:T], [(w[:, k, c * 128:(c + 1) * 128], xnT[:, k, 0:T]) for k in range(8)],
                            xkeys + [kw], [("ps", bg)])
                    mmgroup(bank(bu)[:, 0:T], [(w[:, k, 256 + c * 128:256 + (c + 1) * 128], xnT[:, k, 0:T]) for k in range(8)],
                            xkeys + [kw], [("ps", bu)])
                    tb = j % 2
                    S.op("act", (lambda e, bg=bg, tb=tb: e.activation(out=tmpf[:, tb, 0:T], in_=bank(bg)[:, 0:T], func=AF.Silu)),
                         reads=[("ps", bg)], writes=[("tmpf", tb)])
                    S.op("dve", (lambda e, bu=bu, tb=tb, j=j: e.tensor_tensor(out=gT[:, j, 0:T], in0=bank(bu)[:, 0:T],
                                                                              in1=tmpf[:, tb, 0:T], op=ALU.mult)),
                         reads=[("ps", bu), ("tmpf", tb)], writes=[("gT", j)])
            sqrt_table_prefetch()
            dump("gT_" + win_name, gT[:, :, 0:T], [("gT", j) for j in range(22)])
            for half in range(2):
                if half == 1 and mid_hook is not None:
                    mid_hook()
                banks = [ps_alloc("B" if half == 0 else "A") for _ in range(NS)]
                for jb in range(3):
                    j0 = jb * 8
                    nj = 8 if jb < 2 else 6
                    w, kw = wj(wout_name, j0, nj, half * 512)
                    for s in range(NS):
                        mmgroup(bank(banks[s])[:TP, :],
                                [(gT[:, j0 + jj, s * 128:s * 128 + TP], w[:, jj, :]) for jj in range(nj)],
                                [("gT", j0 + jj) for jj in range(nj)] + [kw], [("ps", banks[s])],
                                start=(jb == 0), stop=(jb == 2))
                for s in range(NS):
                    hv = res[:TP, s, half * 512:(half + 1) * 512]
                    dv = dst[:TP, s, half * 512:(half + 1) * 512]
                    S.op("dve", (lambda e, hv=hv, dv=dv, b=banks[s]: e.scalar_tensor_tensor(
                        out=dv, in0=bank(b)[:TP, :], scalar=0.5, in1=hv, op0=ALU.mult, op1=ALU.add)),
                        reads=[("ps", banks[s]), (resname, s)], writes=[(dstname, s)])

        def tile(kind, q, t):
            if kind == "p":
                T, NS, TP = 512, 4, 128
                tok0 = q * SEQLEN + t * 512
                xsrc = xp[tok0:tok0 + 512, :].rearrange("(s p) d -> p s d", p=128)
                first, last = (t == 0), (t == NT - 1)
                gp0 = 4 * t
                ydst = yp
            else:
                T, NS, TP = 64, 1, 64
                tok0 = 0
                xsrc = xsm.rearrange("(s p) d -> p s d", p=64)
                first, last = False, True
                gp0 = 4
                ydst = ys
            if not prefetched:
                prefetch_x(kind, q, t)
            prefetched.clear()

            if kind == "p" and first:
                S.op("dve", lambda e: e.memset(Sf[:], 0.0), writes=[("Sf", hh) for hh in range(4)])
                S.op("dve", lambda e: e.memset(Sb[:, 0, :, :], 0.0), writes=[("Sb", 0, hh) for hh in range(4)])
            if kind == "s":
                S.op("sp", lambda e: e.dma_start(out=Sf[:], in_=st.rearrange("h d v -> d h v")),
                     writes=[("Sf", hh) for hh in range(4)], dma=True)
                S.op("act", lambda e: e.activation(out=Sb[:, 0, :, :], in_=Sf[:], func=AF.Copy),
                     reads=[("Sf", hh) for hh in range(4)], writes=[("Sb", 0, hh) for hh in range(4)])
                for blk in range(4):
                    S.op("pool", (lambda e, blk=blk: e.dma_start(out=vaug[:, blk, :, 0:64],
                                                                 in_=cv[blk * 128:(blk + 1) * 128, :].rearrange("p (h d) -> p h d", h=16))),
                         writes=[("vaug", blk, 0), ("vaug", blk, 1)], dma=True)
                    pbuf = blk % 2
                    S.op("pool", (lambda e, blk=blk, pbuf=pbuf: e.dma_start(out=xb[:, pbuf, :], in_=ck[blk * 128:(blk + 1) * 128, :])),
                         writes=[("xb", pbuf)], dma=True)
                    b = ps_alloc("A")
                    pb_ = bankb(b)

                    def fn(e, pb_=pb_, pbuf=pbuf):
                        ins = None
                        for k in range(8):
                            ins = e.transpose(out=pb_[:, k * 128:(k + 1) * 128], in_=xb[:, pbuf, k * 128:(k + 1) * 128],
                                              identity=identb[:, :])
                        return ins
                    S.op("pe", fn, reads=[("xb", pbuf), "identb"], writes=[("ps", b)])
                    S.op("act", (lambda e, pb_=pb_, blk=blk: e.activation(
                        out=kbT[:, :, blk * 128:(blk + 1) * 128], in_=pb_.rearrange("p (k t) -> p k t", k=8), func=AF.Copy)),
                        reads=[("ps", b)], writes=[("kbT", blk, c_) for c_ in range(8)])

            mark("load")
            ffn(T, NS, TP, 0, "w1i", "w1o", res=h2, resname="h2", norm_done=True)

            dump("h1", h[:TP, 0:NS, :], [("h", s) for s in range(NS)])
            mark("ffn1")
            norm_T(T, NS, TP, 1, xnT, "xnT")
            nkeys = [("xnT", s) for s in range(NS)]
            S.enter("ebT")
            S.enter("enbT")
            S.enter("la")
            S.enter("dke")
            b = ps_alloc("A")
            mmgroup(bank(b)[0:16, 0:T], [(wfa[:, k, :], xnT[:, k, 0:T]) for k in range(8)], nkeys + ["wfa"], [("ps", b)])
            S.op("act", (lambda e, b=b: e.activation(out=faT[0:16, 0:T], in_=bank(b)[0:16, 0:T], func=AF.Copy)),
                 reads=[("ps", b)], writes=["faT"])
            def m1_A(s):
                b = ps_alloc("A")
                mmgroup(bank(b)[:TP, :], [(faT[0:17, s * 128:s * 128 + TP], wgate[0:17, :])], ["faT", "wgate"], [("ps", b)])
                tb = s % 2
                S.op("act", (lambda e, b=b, tb=tb: e.activation(out=tmpf[:TP, tb, :], in_=bank(b)[:TP, :], func=AF.Exp, scale=-1.0)),
                     reads=[("ps", b)], writes=[("tmpf", tb)])
                S.op("act", (lambda e, s=s, tb=tb: e.activation(out=la[:TP, s, :], in_=tmpf[:TP, tb, :], func=AF.Ln, bias=1.0)),
                     reads=[("tmpf", tb)], writes=[("la", s)])
            def m1_B(s):
                b1 = ps_alloc("A")

                def fnc(e, s=s, b1=b1):
                    ins = None
                    for hh in range(4):
                        ins = e.matmul(bank(b1)[:, hh * 128:hh * 128 + TP], lhsT=la[:TP, s, hh * 128:(hh + 1) * 128],
                                       rhs=TRI[:TP, :TP], start=True, stop=True)
                    return ins
                S.op("pe", fnc, reads=[("la", s), "constf"], writes=[("ps", b1)])
                src3 = bank(b1).rearrange("p (h t) -> p h t", h=4)[:, :, 0:TP]
                S.op("act", (lambda e, s=s, src3=src3: e.activation(out=ebT[:, :, s * 128:s * 128 + TP], in_=src3,
                                                                    func=AF.Exp, scale=-1.0 / 16)),
                     reads=[("ps", b1)], writes=[("ebT", s)])
                S.op("act", (lambda e, s=s, src3=src3: e.activation(out=enbT[:, :, s * 128:s * 128 + TP], in_=src3,
                                                                    func=AF.Exp, scale=1.0 / 16)),
                     reads=[("ps", b1)], writes=[("enbT", s)])
                b2 = ps_alloc("A")
                mmgroup(bank(b2)[:TP, :], [(TRIC[:TP, :TP], la[:TP, s, :])], [("la", s), "constf"], [("ps", b2)])
                S.op("act", (lambda e, s=s, b2=b2: e.activation(out=dke[:TP, s, :], in_=bank(b2)[:TP, :], func=AF.Exp,
                                                                scale=-1.0 / 16)),
                     reads=[("ps", b2)], writes=[("dke", s)])
                srcd = bank(b1).rearrange("p (h t) -> p h t", h=4)[:, :, TP - 1:TP]
                S.op("act", (lambda e, s=s, srcd=srcd: e.activation(out=dec[:, :, s:s + 1], in_=srcd,
                                                                    func=AF.Exp, scale=-1.0 / 16)),
                     reads=[("ps", b1)], writes=[("dec", s)])
            S.enter("va")
            S.enter("rr")
            S.enter("qbT")
            ebkeys = [("ebT", s) for s in range(NS)]
            enkeys = [("enbT", s) for s in range(NS)]
            def proj_qa():
                w, kw = wk("win", 0, 512)
                for hh in range(4):
                    b = ps_alloc("A")
                    mmgroup(bank(b)[:, 0:T], [(w[:, k, hh * 128:(hh + 1) * 128], xnT[:, k, 0:T]) for k in range(8)],
                            nkeys + [kw], [("ps", b)])
                    S.op("dve", (lambda e, b=b, hh=hh: e.scalar_tensor_tensor(
                        out=qdT[:, hh, 0:T], in0=bank(b)[:, 0:T], scalar=128.0 ** -0.5, in1=ebT[:, hh, 0:T],
                        op0=ALU.mult, op1=ALU.mult)), reads=[("ps", b)] + ebkeys, writes=[("qdT", hh)])
            def proj_ka():
                w, kw = wk("win", 512, 512)
                for hh in range(4):
                    b = ps_alloc("A")
                    mmgroup(bank(b)[:, 0:T], [(w[:, k, hh * 128:(hh + 1) * 128], xnT[:, k, 0:T]) for k in range(8)],
                            nkeys + [kw], [("ps", b)])
                    S.op("dve", (lambda e, b=b, hh=hh: e.tensor_tensor(out=kiT[:, hh, 0:T], in0=bank(b)[:, 0:T],
                                                                       in1=enbT[:, hh, 0:T], op=ALU.mult)),
                         reads=[("ps", b)] + enkeys, writes=[("kiT", hh)])
                for s in range(NS):
                    b = ps_alloc("A")
                    mmgroup(bank(b)[:TP, :], [(xnT[:, k, s * 128:s * 128 + TP], w[:, k, :]) for k in range(8)],
                            [("xnT", s), kw], [("ps", b)])
                    S.op("dve", (lambda e, b=b, s=s: e.tensor_tensor(
                        out=kend[:TP, s, :], in0=bank(b)[:TP, :], in1=dke[:TP, s, :], op=ALU.mult)),
                        reads=[("ps", b), ("dke", s)], writes=[("kend", s)])
            def proj_va(half):
                w, kw = wk("win", 1024 + half * 512, 512)
                for s in range(NS):
                    b = ps_alloc("A")
                    mmgroup(bank(b)[:TP, :], [(xnT[:, k, s * 128:s * 128 + TP], w[:, k, :]) for k in range(8)],
                            [("xnT", s), kw], [("ps", b)])
                    S.op("dve", (lambda e, b=b, s=s, half=half: e.tensor_copy(
                        out=va[:TP, s, half * 512:(half + 1) * 512], in_=bank(b)[:TP, :])),
                        reads=[("ps", b)], writes=[("va", s, half)])
            def proj_ra(half):
                w, kw = wk("win", 2048 + half * 512, 512)
                for s in range(NS):
                    b = ps_alloc("A")
                    mmgroup(bank(b)[:TP, :], [(xnT[:, k, s * 128:s * 128 + TP], w[:, k, :]) for k in range(8)],
                            [("xnT", s), kw], [("ps", b)])
                    S.op("act", (lambda e, b=b, s=s, half=half: e.activation(
                        out=rr[:TP, s, half * 512:(half + 1) * 512], in_=bank(b)[:TP, :], func=AF.Silu)),
                        reads=[("ps", b)], writes=[("rr", s, half)])
            def proj_qb_kb_vb():
                for half in range(2):
                    w, kw = wk("win", 3088 + half * 512, 512)
                    for c in range(4):
                        b = ps_alloc("A")
                        mmgroup(bank(b)[:, 0:T], [(w[:, k, c * 128:(c + 1) * 128], xnT[:, k, 0:T]) for k in range(8)],
                                nkeys + [kw], [("ps", b)])
                        S.op("act", (lambda e, b=b, c=c, half=half: e.activation(out=qbT[:, half * 4 + c, 0:T], in_=bank(b)[:, 0:T],
                                                                                 func=AF.Copy)),
                             reads=[("ps", b)], writes=[("qbT", half * 4 + c)])
                mark("m2d")
                slots = [(gp0 + s) % 8 for s in range(NS)]
                kpos0 = slots[0] * 128
                for half in range(2):
                    w, kw = wk("win", 4112 + half * 512, 512)
                    for c in range(4):
                        b = ps_alloc("A")
                        mmgroup(bank(b)[:, 0:T], [(w[:, k, c * 128:(c + 1) * 128], xnT[:, k, 0:T]) for k in range(8)],
                                nkeys + [kw], [("ps", b)])
                        S.op("act", (lambda e, b=b, c=c, half=half: e.activation(out=kbT[:, half * 4 + c, kpos0:kpos0 + T],
                                                                                 in_=bank(b)[:, 0:T], func=AF.Copy)),
                             reads=[("ps", b)], writes=[("kbT", sl, half * 4 + c) for sl in slots])
                    if last:
                        for s in range(NS):
                            b = ps_alloc("A")
                            mmgroup(bank(b)[:TP, :], [(xnT[:, k, s * 128:s * 128 + TP], w[:, k, :]) for k in range(8)],
                                    [("xnT", s), kw], [("ps", b)])
                            ob_ = (2 * s + half) % 2
                            S.op("act", (lambda e, b=b, ob_=ob_: e.activation(out=ystage[:TP, ob_, :], in_=bank(b)[:TP, :], func=AF.Copy)),
                                 reads=[("ps", b)], writes=[("ystage", ob_)])
                            if kind == "p":
                                dst = kpo[q * 512 + s * 128:q * 512 + s * 128 + 128, half * 512:(half + 1) * 512]
                            else:
                                dst = kso[:, half * 512:(half + 1) * 512]
                            S.op("pool", (lambda e, dst=dst, ob_=ob_: e.dma_start(out=dst, in_=ystage[:TP, ob_, :])),
                                 reads=[("ystage", ob_)], dma=True)
                mark("m2e")
                for half in range(2):
                    w, kw = wk("win", 5136 + half * 512, 512)
                    for s in range(NS):
                        b = ps_alloc("A")
                        mmgroup(bank(b)[:TP, :], [(xnT[:, k, s * 128:s * 128 + TP], w[:, k, :]) for k in range(8)],
                                [("xnT", s), kw], [("ps", b)])
                        sl = slots[s]
                        S.op("act", (lambda e, b=b, sl=sl, half=half: e.activation(
                            out=vaug[:TP, sl, half * 8:(half + 1) * 8, 0:64],
                            in_=bank(b)[:TP, :].rearrange("p (h d) -> p h d", h=8), func=AF.Copy)),
                            reads=[("ps", b)], writes=[("vaug", sl, half)])
                        if last:
                            ob_ = (2 * s + half) % 2
                            S.op("dve", (lambda e, b=b, ob_=ob_: e.tensor_copy(out=ystage[:TP, ob_, :], in_=bank(b)[:TP, :])),
                                 reads=[("ps", b), ("vaug", sl, half)], writes=[("ystage", ob_)])
                            if kind == "p":
                                dst = vpo[q * 512 + s * 128:q * 512 + s * 128 + 128, half * 512:(half + 1) * 512]
                            else:
                                dst = vso[:, half * 512:(half + 1) * 512]
                            S.op("pool", (lambda e, dst=dst, ob_=ob_: e.dma_start(out=dst, in_=ystage[:TP, ob_, :])),
                                 reads=[("ystage", ob_)], dma=True)
            for s in range(NS):
                m1_A(s)
            proj_va(0)
            for s in range(0, min(2, NS)):
                m1_B(s)
            proj_va(1)
            for s in range(2, NS):
                m1_B(s)
            proj_ra(0)
            proj_ra(1)
            proj_qb_kb_vb()
            S.enter("qdT")
            S.enter("kiT")
            proj_qa()
            proj_ka()
            dump("qdT", qdT[:, :, 0:T], [("qdT", hh) for hh in range(4)])
            dump("kiT", kiT[:, :, 0:T], [("kiT", hh) for hh in range(4)])

            mark("m2")
            S.enter("oa")
            S.enter("oaT")
            gla_tail = []
            gla_epi = []
            S.enter("Pbg")
            sqrt_table_prefetch()
            sbase = (4 * t) if kind == "p" else 0
            info = {}

            def partA_all(s):
                rd = (sbase + s) % 3
                wr = (sbase + s + 1) % 3
                for hh in range(4):
                    bi = ps_alloc("A")
                    mmgroup(bank(bi)[:, 0:256], [(kend[:TP, s, hh * 128:(hh + 1) * 128], va[:TP, s, hh * 256:(hh + 1) * 256])],
                            [("kend", s), ("va", s, hh // 2)], [("ps", bi)])
                    bs = ps_alloc("A")
                    mmgroup(bank(bs)[:TP, 0:TP], [(kiT[:, hh, s * 128:s * 128 + TP], qdT[:, hh, s * 128:s * 128 + TP])],
                            [("kiT", hh), ("qdT", hh)], [("ps", bs)])
                    pslot = (hh, 1 + s % 2)
                    S.op("dve", (lambda e, bs=bs, pslot=pslot: e.tensor_tensor(
                        out=Pb[:TP, pslot[0], pslot[1], 0:TP], in0=bank(bs)[:TP, 0:TP], in1=CMASK[:TP, :TP], op=ALU.mult)),
                        reads=[("ps", bs), "constf"], writes=[("Pbg", pslot)])
                    S.op("dve", (lambda e, bi=bi, hh=hh: e.scalar_tensor_tensor(
                        out=Sf[:, hh, :], in0=Sf[:, hh, :], scalar=dec[:, hh, s:s + 1],
                        in1=bank(bi)[:, 0:256], op0=ALU.mult, op1=ALU.add)),
                        reads=[("Sf", hh), ("dec", s), ("ps", bi)], writes=[("Sf", hh)])
                    S.op("act", (lambda e, hh=hh, wr=wr: e.activation(out=Sb[:, wr, hh, :], in_=Sf[:, hh, :], func=AF.Copy)),
                         reads=[("Sf", hh)], writes=[("Sb", wr, hh)])
                info[s] = rd

            def partB_all(s):
                rd = info[s]
                bo = [ps_alloc("B"), ps_alloc("B")]
                for hh in range(4):
                    obank = bank(bo[hh // 2])
                    ocol = (hh % 2) * 256
                    pslot = (hh, 1 + s % 2)

                    def fno(e, obank=obank, ocol=ocol, hh=hh, pslot=pslot):
                        e.matmul(obank[:TP, ocol:ocol + 256], lhsT=qdT[:, hh, s * 128:s * 128 + TP],
                                 rhs=Sb[:, rd, hh, :], start=True, stop=False)
                        return e.matmul(obank[:TP, ocol:ocol + 256], lhsT=Pb[:TP, pslot[0], pslot[1], 0:TP],
                                        rhs=va[:TP, s, hh * 256:(hh + 1) * 256], start=False, stop=True)
                    S.op("pe", fno, reads=[("qdT", hh), ("Pbg", pslot), ("va", s, hh // 2), ("Sb", rd, hh)],
                         writes=[("ps", bo[hh // 2])])
                for hh in range(4):
                    obank = bank(bo[hh // 2])
                    ocol = (hh % 2) * 256
                    tb = hh % 2
                    S.op("act", (lambda e, hh=hh, obank=obank, ocol=ocol, tb=tb: e.activation(
                        out=tmpf[:TP, tb, 0:256], in_=obank[:TP, ocol:ocol + 256], func=AF.Square,
                        accum_out=stat[:TP, 8 + hh:9 + hh])),
                        reads=[("ps", bo[hh // 2])], writes=[("tmpf", tb), ("stat", 8 + hh)])
                S.op("act", lambda e: e.activation(out=stat[:TP, 12:16], in_=stat[:TP, 8:12], func=AF.Sqrt,
                                                   scale=1.0 / 256, bias=epsb[:TP, 0:1]),
                     reads=[("stat", 8 + hh) for hh in range(4)] + ["epsb"], writes=[("stat", 12 + hh) for hh in range(4)])
                S.op("dve", lambda e: e.reciprocal(out=stat[:TP, 12:16], in_=stat[:TP, 12:16]),
                     reads=[("stat", 12 + hh) for hh in range(4)], writes=[("stat", 12 + hh) for hh in range(4)])
                for hh in range(4):
                    obank = bank(bo[hh // 2])
                    ocol = (hh % 2) * 256
                    S.op("dve", (lambda e, hh=hh, obank=obank, ocol=ocol: e.scalar_tensor_tensor(
                        out=oa[:TP, s, hh * 256:(hh + 1) * 256], in0=obank[:TP, ocol:ocol + 256],
                        scalar=stat[:TP, 12 + hh:13 + hh], in1=rr[:TP, s, hh * 256:(hh + 1) * 256],
                        op0=ALU.mult, op1=ALU.mult)),
                        reads=[("ps", bo[hh // 2]), ("stat", 12 + hh), ("rr", s, hh // 2)], writes=[("oa", s, hh)])

            def gtail(s):
                transpose_sub(oa[:TP, s, :], TP, oaT, s, 3, [("oa", s, hh) for hh in range(4)], ("oaT", s), pool="A")

            partA_all(0)
            for s in range(1, NS):
                partA_all(s)
                partB_all(s - 1)
                if s >= 2:
                    gtail(s - 2)
            partB_all(NS - 1)
            if NS >= 2:
                gtail(NS - 2)
            gla_tail.append(lambda: gtail(NS - 1))

            if last:
                sdst = spo[q] if kind == "p" else sso
                S.op("pool", lambda e: e.dma_start(out=sdst.rearrange("h d v -> d h v"), in_=Sf[:]),
                     reads=[("Sf", hh) for hh in range(4)], dma=True)

            mark("m3")
            S.enter("ob")
            S.enter("obT")
            S.enter("Pb")
            accs = {}

            def att_sc(s, hd):
                gp = gp0 + s
                blks = [i for i in range(5) if gp - 4 + i >= 0]
                c = hd // 2
                pb0 = (hd % 2) * 64
                pbuf = (s * 16 + hd) % 4
                bA = ps_alloc("C")
                bB = ps_alloc("C")
                cst = [i for i in blks if i < 3]

                def fns(e):
                    ins = None
                    for i in blks:
                        sl = (gp - 4 + i) % 8
                        nk = TP if i == 4 else 128
                        outp = bank(bA)[:nk, i * 128:i * 128 + TP] if i < 3 else bank(bB)[:nk, (i - 3) * 128:(i - 3) * 128 + TP]
                        ins = e.matmul(outp, lhsT=kbT[pb0:pb0 + 64, c, sl * 128:sl * 128 + nk],
                                       rhs=qbT[pb0:pb0 + 64, c, s * 128:s * 128 + TP], start=True, stop=True)
                    return ins
                S.op("pe", fns, reads=[("qbT", c)] + [("kbT", (gp - 4 + i) % 8, c) for i in blks],
                     writes=[("ps", bA), ("ps", bB)])
                tb = (s * 16 + hd) % 2
                S.op("dve", lambda e: e.scalar_tensor_tensor(
                    out=tmpf[:, tb, 0:256].rearrange("p (b q) -> p b q", b=2)[:, :, 0:TP],
                    in0=bank(bB)[:, 0:256].rearrange("p (b q) -> p b q", b=2)[:, :, 0:TP], scalar=0.125,
                    in1=biasT[:, hd, :, 0:TP], op0=ALU.mult, op1=ALU.add),
                    reads=[("ps", bB), "biasT"], writes=[("tmpf", tb)])
                if cst:
                    i0 = cst[0]
                    ncst = len(cst)
                    S.op("act", lambda e: e.activation(
                        out=Pb[:, pbuf, i0:i0 + ncst, 0:TP],
                        in_=bank(bA)[:, i0 * 128:(i0 + ncst) * 128].rearrange("p (b q) -> p b q", b=ncst)[:, :, 0:TP],
                        func=AF.Exp, scale=0.125, bias=cbs[:, hd:hd + 1]),
                        reads=[("ps", bA), "cbs"], writes=[("Pb", pbuf, 0)])
                i3 = 0 if 3 in blks else 1
                S.op("act", lambda e: e.activation(
                    out=Pb[:, pbuf, 3 + i3:5, 0:TP],
                    in_=tmpf[:, tb, 0:256].rearrange("p (b q) -> p b q", b=2)[:, i3:2, 0:TP], func=AF.Exp),
                    reads=[("tmpf", tb)], writes=[("Pb", pbuf, 1)])
                if 0 in blks and TP == 128:
                    S.op("dve", lambda e: e.memset(Pb[0:64, pbuf, 0, 64:128], 0.0), writes=[("Pb", pbuf, 0)])

            GRP = [(0, 7), (7, 7), (14, 2)]

            def att_pv(s, hd):
                gp = gp0 + s
                blks = [i for i in range(5) if gp - 4 + i >= 0]
                pbuf = (s * 16 + hd) % 4
                g = 0 if hd < 7 else (1 if hd < 14 else 2)
                hd0, nh = GRP[g]
                if hd == hd0:
                    accs[(s, g)] = ps_alloc("D")
                ab = accs[(s, g)]
                acc = bank(ab)
                acol = (hd - hd0) * 65

                def fnv(e):
                    ins = None
                    for n_, i in enumerate(blks):
                        sl = (gp - 4 + i) % 8
                        nk = TP if i == 4 else 128
                        ins = e.matmul(acc[:TP, acol:acol + 65], lhsT=Pb[:nk, pbuf, i, 0:TP], rhs=vaug[:nk, sl, hd, :],
                                       start=(n_ == 0), stop=(n_ == len(blks) - 1))
                    return ins
                S.op("pe", fnv, reads=[("Pb", pbuf, 0), ("Pb", pbuf, 1)] + [("vaug", (gp - 4 + i) % 8, hd // 8) for i in blks],
                     writes=[("ps", ab)])
                if hd == hd0 + nh - 1:
                    a3 = acc[:TP, 0:nh * 65].rearrange("p (h d) -> p h d", h=nh)
                    rb = g % 2
                    S.op("dve", lambda e: e.reciprocal(out=rec[:TP, rb, 0:nh], in_=a3[:, :, 64]),
                         reads=[("ps", ab)], writes=[("rec", rb)])
                    S.op("dve", lambda e: e.tensor_tensor(
                        out=ob[:TP, s, hd0 * 64:(hd0 + nh) * 64].rearrange("p (h d) -> p h d", h=nh), in0=a3[:, :, 0:64],
                        in1=rec[:TP, rb, 0:nh].unsqueeze(2).broadcast_to([TP, nh, 64]), op=ALU.mult),
                        reads=[("ps", ab), ("rec", rb)], writes=[("ob", s, g)])

            items = [(s, hd) for s in range(NS) for hd in range(16)]
            DEPTH = 2
            if gla_epi:
                gla_epi.pop()()
            for i, (s, hd) in enumerate(items):
                att_sc(s, hd)
                if i == 6 and gla_tail:
                    gla_tail.pop()()
                if i >= DEPTH:
                    ps_, ph_ = items[i - DEPTH]
                    att_pv(ps_, ph_)
                    if ph_ == 15 and False:
                        pass
                if hd == 3 and s > 0:
                    transpose_sub(ob[:TP, s - 1, :], TP, obT, s - 1, None, [("ob", s - 1, g) for g in range(3)], ("obT", s - 1))
            for j in range(len(items) - DEPTH, len(items)):
                att_pv(*items[j])
            if gla_tail:
                gla_tail.pop()()
            ob_tail = [lambda: transpose_sub(ob[:TP, NS - 1, :], TP, obT, NS - 1, None, [("ob", NS - 1, g) for g in range(3)],
                                             ("obT", NS - 1))]

            mark("m4")
            oakeys = [("oaT", s) for s in range(NS)]
            obkeys = [("obT", s) for s in range(NS)]
            for mp in range(4):
                w, kw = wk2("win", 6160 + mp * 256, "wbg", mp * 256, 256)
                if mp == 0:
                    oakeys_ = list(oakeys)
                for c in range(2):
                    m = mp * 2 + c
                    b1 = ps_alloc("A")
                    b2 = ps_alloc("A")
                    mmgroup(bank(b1)[:, 0:T], [(w[:, k, c * 128:(c + 1) * 128], xnT[:, k, 0:T]) for k in range(8)],
                            nkeys + [kw], [("ps", b1)])
                    mmgroup(bank(b2)[:, 0:T], [(w[:, k, 256 + c * 128:256 + (c + 1) * 128], oaT[:, k, 0:T]) for k in range(8)],
                            oakeys + [kw], [("ps", b2)])
                    tb = m % 2
                    S.op("act", (lambda e, b1=b1, tb=tb: e.activation(out=tmpf[:, tb, 0:T], in_=bank(b1)[:, 0:T], func=AF.Sigmoid)),
                         reads=[("ps", b1)], writes=[("tmpf", tb)])
                    S.op("dve", (lambda e, b2=b2, tb=tb, m=m: e.tensor_tensor(out=mixT[:, m, 0:T], in0=bank(b2)[:, 0:T],
                                                                              in1=tmpf[:, tb, 0:T], op=ALU.mult)),
                         reads=[("ps", b2), ("tmpf", tb)], writes=[("mixT", m)])
                if ob_tail:
                    ob_tail.pop()()
                w, kw = wk2("win", 7184 + mp * 256, "wba", mp * 256, 256)
                for c in range(2):
                    m = mp * 2 + c
                    b1 = ps_alloc("A")
                    b2 = ps_alloc("A")
                    mmgroup(bank(b1)[:, 0:T], [(w[:, k, c * 128:(c + 1) * 128], xnT[:, k, 0:T]) for k in range(8)],
                            nkeys + [kw], [("ps", b1)])
                    mmgroup(bank(b2)[:, 0:T], [(w[:, k, 256 + c * 128:256 + (c + 1) * 128], obT[:, k, 0:T]) for k in range(8)],
                            obkeys + [kw], [("ps", b2)])
                    tb = m % 2
                    S.op("act", (lambda e, b1=b1, tb=tb: e.activation(out=tmpf[:, tb, 0:T], in_=bank(b1)[:, 0:T], func=AF.Sigmoid)),
                         reads=[("ps", b1)], writes=[("tmpf", tb)])
                    S.op("dve", (lambda e, b2=b2, tb=tb: e.tensor_tensor(out=tmpf[:, tb, 0:T], in0=bank(b2)[:, 0:T],
                                                                         in1=tmpf[:, tb, 0:T], op=ALU.mult)),
                         reads=[("ps", b2), ("tmpf", tb)], writes=[("tmpf", tb)])
                    S.op("dve", (lambda e, tb=tb, m=m: e.tensor_tensor(out=mixT[:, m, 0:T], in0=tmpf[:, tb, 0:T],
                                                                       in1=mixT[:, m, 0:T], op=ALU.add)),
                         reads=[("tmpf", tb), ("mixT", m)], writes=[("mixT", m)])
            sqrt_table_prefetch()
            mkeys = [("mixT", m) for m in range(8)]
            for half in range(2):
                w, kw = wk("wo", half * 512, 512)
                for s in range(NS):
                    b = ps_alloc("A")
                    mmgroup(bank(b)[:TP, :], [(mixT[:, k, s * 128:s * 128 + TP], w[:, k, :]) for k in range(8)],
                            mkeys + [kw], [("ps", b)])
                    hv = h[:TP, s, half * 512:(half + 1) * 512]
                    S.op("dve", (lambda e, hv=hv, b=b: e.tensor_tensor(out=hv, in0=bank(b)[:TP, :], in1=hv, op=ALU.add)),
                         reads=[("ps", b), ("h", s)], writes=[("h", s)])
            dump("hmix", h[:TP, 0:NS, :], [("h", s) for s in range(NS)])

            mark("m5")
            S.enter("hout")
            nxt = next_tile.get((kind, q, t))
            hookA = (lambda: prefetch_x(*nxt, part="load")) if nxt is not None else None
            hookB = (lambda: (prefetch_x(*nxt, part="apply"), prefetched.append(1))) if nxt is not None else None
            ffn(T, NS, TP, 2, "w2i", "w2o", dst=hout, dstname="hout", mid_hook=hookB, pre_hook=hookA)
            mark("ffn2")

            def final_sub(s, TP=TP, ydst=ydst, tok0=tok0):
                col = 16 + 2 * (s % 2)
                rstd_of(hout[:TP, s, :], TP, D, col, ("tmpf", s % 2), tmpf[:TP, s % 2, :].bitcast(BF16), [("hout", s)])
                for half in range(2):
                    yb_ = half
                    S.op("dve", (lambda e, half=half, yb_=yb_: e.scalar_tensor_tensor(
                        out=ystage[:TP, yb_, :], in0=hout[:TP, s, half * 512:(half + 1) * 512], scalar=stat[:TP, col + 1:col + 2],
                        in1=gnfb[:TP, half * 512:(half + 1) * 512], op0=ALU.mult, op1=ALU.mult)),
                        reads=[("hout", s), ("stat", col + 1), "gnfb"], writes=[("ystage", yb_)])
                    dst_ = ydst[tok0 + s * 128:tok0 + s * 128 + TP, half * 512:(half + 1) * 512]
                    S.op("pool", (lambda e, dst_=dst_, yb_=yb_: e.dma_start(out=dst_, in_=ystage[:TP, yb_, :])),
                         reads=[("ystage", yb_)], dma=True)
            for s in range(NS):
                pending_final.append(lambda s=s, f=final_sub: f(s))

        pending_final = []
        prefetched = []
        order = [("p", q_, t_) for q_ in range(NSEQ) for t_ in range(NT)] + [("s", 0, 0)]
        next_tile = {order[i]: order[i + 1] for i in range(len(order) - 1)}

        def prefetch_x(kind, q, t, part="all"):
            if kind == "p":
                T, NS, TP = 512, 4, 128
                tok0 = q * SEQLEN + t * 512
                xsrc = xp[tok0:tok0 + 512, :].rearrange("(s p) d -> p s d", p=128)
            else:
                T, NS, TP = 64, 1, 64
                xsrc = xsm.rearrange("(s p) d -> p s d", p=64)
            if part in ("all", "load"):
                S.enter("h2")
                for s in range(NS):
                    S.op("sp", (lambda e, s=s: e.dma_start(out=h2[:TP, s, :], in_=xsrc[:, s, :])), writes=[("h2", s)], dma=True)
                for s in range(NS):
                    S.op("dve", (lambda e, s=s: e.scalar_tensor_tensor(
                        out=junk[:TP, 1, :], in0=h2[:TP, s, :], scalar=1.0, in1=h2[:TP, s, :],
                        op0=ALU.mult, op1=ALU.mult, accum_out=stat[:TP, 24 + s:25 + s])),
                        reads=[("h2", s)], writes=[("junk", 1), ("stat", 24 + s)])
                S.op("act", lambda e: e.activation(out=stat[:TP, 28:28 + NS], in_=stat[:TP, 24:24 + NS], func=AF.Sqrt,
                                                   scale=1.0 / D, bias=epsb[:TP, 0:1]),
                     reads=[("stat", 24 + s) for s in range(NS)] + ["epsb"], writes=[("stat", 28 + s) for s in range(NS)])
                S.op("dve", lambda e: e.reciprocal(out=stat[:TP, 28:28 + NS], in_=stat[:TP, 28:28 + NS]),
                     reads=[("stat", 28 + s) for s in range(NS)], writes=[("stat", 28 + s) for s in range(NS)])
            if part in ("all", "apply"):
                norm_apply(NS, TP, 0, xnT, "xnT", h2, "h2", scol=28)
        for q in range(NSEQ):
            for t in range(NT):
                tile("p", q, t)
        tile("s", 0, 0)
        while pending_final:
            pending_final.pop(0)()

    ws.planning = True
    try:
        emit_all(NullSched())
    except StopEmit:
        pass
    ws.planning = False
    ntiles = NSEQ * NT + 1
    if stop_at is None:
        assert len(ws.plan) % ntiles == 0
        ws.nbt = len(ws.plan) // ntiles
        for j in range(len(ws.plan)):
            assert ws.plan[j][1] == ws.plan[j % ws.nbt][1]
    else:
        ws.nbt = len(ws.plan)
    wsc_holder["t"] = nc.dram_tensor("wsc", [ws.nbt, 128, 4096], BF16, kind="Internal").ap()
    S = Sched()
    try:
        emit_all(S)
    except StopEmit:
        pass
    outs = [i for i, o in enumerate(S.ops) if o["dma"] and o["eng"] == "pool"]
    S.barrier("pool", outs)
    S.analyze()
    with contextlib.ExitStack() as stk:
        sems = {e: stk.enter_context(nc.semaphore("s_" + e)) for e in COMPUTE}
        dsems = {q_: [stk.enter_context(nc.semaphore("d_%s%d" % (q_, i))) for i in range(NDMASEM)] for q_ in QUEUES}
        block = stk.enter_context(nc.Block())
        S.emit(sems, dsems, block)
    return nc, dbg_out, S


def host_consts(attn_rel_bias):
    tab = np.asarray(attn_rel_bias, dtype=np.float32)[0]
    ext = np.concatenate([tab, np.repeat(tab[:, 256:257], 127, axis=1)], axis=1)
    p = np.arange(128)[:, None]
    qq = np.arange(128)[None, :]
    biasT = np.empty((128, 16, 2, 128), np.float32)
    biasT[:, :, 0, :] = np.transpose(ext[:, 256 + qq - p], (1, 0, 2))
    blkA = np.transpose(ext[:, 128 + qq - p], (1, 0, 2)).copy()
    blkA[64:128, :, 0:64] = -30000.0
    biasT[:, :, 1, :] = blkA
    cb = np.broadcast_to(tab[None, :, 256], (128, 16)).astype(np.float32).copy()
    consts = np.zeros((128, 4, 128), np.float32)
    consts[:, 0, :] = np.eye(128, dtype=np.float32)
    j = np.arange(128)[:, None]
    i = np.arange(128)[None, :]
    same = np.ones((128, 128), dtype=bool)
    consts[:, 1, :] = (same & (j <= i)).astype(np.float32)
    consts[:, 2, :] = (same & (j > i)).astype(np.float32)
    consts[:, 3, :] = (same & (j <= i)).astype(np.float32)
    return biasT, cb, consts


def make_common(norm_ffn1, w_ffn1_in, w_ffn1_out, norm_mix, w_in, w_gla_gate, b_gla_gate, gla_norm, attn_rel_bias,
                w_branch_gla, w_branch_att, w_out, norm_ffn2, w_ffn2_in, w_ffn2_out, norm_final):
    f = lambda a: np.ascontiguousarray(np.asarray(a, dtype=np.float32))
    biasT, cb, consts = host_consts(attn_rel_bias)
    gains = np.stack([f(norm_ffn1)[0], f(norm_mix)[0], f(norm_ffn2)[0], f(gla_norm)[0]], axis=0)
    gains = np.ascontiguousarray(gains.reshape(4, 8, 128).transpose(2, 0, 1))
    return dict(
        w1i=f(w_ffn1_in)[0], w1o=f(w_ffn1_out)[0], win=f(w_in)[0], wbg=f(w_branch_gla)[0], wba=f(w_branch_att)[0],
        wo=f(w_out)[0], w2i=f(w_ffn2_in)[0], w2o=f(w_ffn2_out)[0],
        wgate=np.ascontiguousarray(np.concatenate([f(w_gla_gate)[0], f(b_gla_gate)], axis=0)),
        gains=gains, gnf=f(norm_final).reshape(1, D), biasT=biasT, cb=cb, consts=consts)


_CACHE = {}


def kernel(x_prompt, x_sample, cache_att_k, cache_att_v, state_gla, norm_ffn1, w_ffn1_in, w_ffn1_out, norm_mix, w_in,
           w_gla_gate, b_gla_gate, gla_norm, attn_rel_bias, w_branch_gla, w_branch_att, w_out, norm_ffn2, w_ffn2_in,
           w_ffn2_out, norm_final):
    x_prompt = np.asarray(x_prompt, dtype=np.float32)
    x_sample = np.asarray(x_sample, dtype=np.float32)
    cache_att_k = np.asarray(cache_att_k, dtype=np.float32)
    cache_att_v = np.asarray(cache_att_v, dtype=np.float32)
    state_gla = np.asarray(state_gla, dtype=np.float32)
    B, SEQ, _ = x_prompt.shape
    NB = x_sample.shape[0]
    n = NB
    NSEQ = B // n
    key = (NSEQ, SEQ)
    if key not in _CACHE:
        _CACHE[key] = build(NSEQ, SEQ)[0]
    nc = _CACHE[key]
    common = make_common(norm_ffn1, w_ffn1_in, w_ffn1_out, norm_mix, w_in, w_gla_gate, b_gla_gate, gla_norm, attn_rel_bias,
                         w_branch_gla, w_branch_att, w_out, norm_ffn2, w_ffn2_in, w_ffn2_out, norm_final)
    in_maps = []
    for c in range(n):
        m = dict(common)
        m["xp"] = np.ascontiguousarray(x_prompt[c * NSEQ:(c + 1) * NSEQ].reshape(NSEQ * SEQ, D))
        m["xs"] = np.ascontiguousarray(x_sample[c])
        m["ck"] = np.ascontiguousarray(cache_att_k[0, c].reshape(512, D))
        m["cv"] = np.ascontiguousarray(cache_att_v[0, c].reshape(512, D))
        m["st"] = np.ascontiguousarray(state_gla[0, c])
        in_maps.append(m)
    res = run_bass_kernel_spmd(nc, in_maps, core_ids=list(range(n)))
    R = res.results
    y_prompt = np.stack([R[c]["yp"].reshape(NSEQ, SEQ, D) for c in range(n)]).reshape(B, SEQ, D)
    y_sample = np.stack([R[c]["ys"] for c in range(n)])
    kp = np.stack([R[c]["kpo"].reshape(NSEQ, 512, 16, 64) for c in range(n)]).reshape(1, B, 512, 16, 64)
    vp = np.stack([R[c]["vpo"].reshape(NSEQ, 512, 16, 64) for c in range(n)]).reshape(1, B, 512, 16, 64)
    sp = np.stack([R[c]["spo"] for c in range(n)]).reshape(1, B, 4, 128, 256)
    ks = np.stack([R[c]["kso"].reshape(64, 16, 64) for c in range(n)])[None]
    vs = np.stack([R[c]["vso"].reshape(64, 16, 64) for c in range(n)])[None]
    ss = np.stack([R[c]["sso"] for c in range(n)])[None]
    f = lambda a: np.ascontiguousarray(a, dtype=np.float32)
    return (f(y_prompt), f(y_sample), f(kp), f(vp), f(sp), f(ks), f(vs), f(ss))
```

```python
import contextlib
import numpy as np
import concourse.bass as bass
import concourse.mybir as mybir
from concourse.bass_utils import run_bass_kernel_spmd

F32 = mybir.dt.float32
BF16 = mybir.dt.bfloat16
AF = mybir.ActivationFunctionType
ALU = mybir.AluOpType

D = 1024
DFF = 2816
PW = 8208
EPS = 1e-6
NCORES = 8

COMPUTE = ("pe", "act", "dve")
QUEUES = ("sp", "pool")
NDMASEM = 12

OVERLAPS = {
    "gT": ["ebT", "enbT", "la", "oaT", "obT", "qdT", "kiT"],
    "ebT": ["gT", "oaT"], "oaT": ["gT", "ebT"],
    "enbT": ["gT", "obT"], "obT": ["gT", "enbT"],
    "la": ["gT", "qdT", "kiT"], "qdT": ["gT", "la"], "kiT": ["gT", "la"],
    "dke": ["oa", "h2"], "oa": ["dke", "h2"], "va": ["ob", "h2"], "ob": ["va", "h2"], "h2": ["dke", "oa", "va", "ob"],
    "hout": ["rr", "qbT"], "rr": ["hout"], "qbT": ["hout"],
    "Pb": ["Pbg"], "Pbg": ["Pb"],
}


def kname(k):
    return k[0] if isinstance(k, tuple) else k


class Sched:
    def __init__(self):
        self.ops = []
        self.last_w = {}
        self.readers = {}
        self.nseq = {e: 0 for e in COMPUTE + QUEUES}
        self.pending = {}

    def enter(self, name):
        deps = set()
        others = set(OVERLAPS[name])
        for k, v in self.last_w.items():
            if kname(k) in others:
                deps.add(v)
        for k, v in self.readers.items():
            if kname(k) in others:
                deps.update(v)
        best = {}
        keep = set()
        for d in deps:
            o = self.ops[d]
            if o["dma"]:
                keep.add(d)
            else:
                if o["eng"] not in best or self.ops[best[o["eng"]]]["seq"] < o["seq"]:
                    best[o["eng"]] = d
        keep.update(best.values())
        self.pending[name] = keep

    def op(self, eng, fn, reads=(), writes=(), dma=False):
        idx = len(self.ops)
        raw = set()
        war = set()
        for r in reads:
            if r in self.last_w:
                raw.add(self.last_w[r])
            if kname(r) == "ps":
                for rd in self.readers.get(r, ()):
                    war.add(rd)
        for w in writes:
            if w in self.last_w:
                raw.add(self.last_w[w])
            for rd in self.readers.get(w, ()):
                war.add(rd)
            p = self.pending.get(kname(w))
            if p:
                raw.update(p)
        for r in reads:
            self.readers.setdefault(r, []).append(idx)
        for w in writes:
            self.last_w[w] = idx
            self.readers[w] = []
        seq = self.nseq[eng]
        self.nseq[eng] += 1
        self.ops.append(dict(eng=eng, fn=fn, raw=raw, war=war - raw, dma=dma, seq=seq,
                             sig=False, waits=[], idx=idx))
        return idx

    def barrier(self, eng, deps):
        idx = len(self.ops)
        seq = self.nseq[eng]
        self.nseq[eng] += 1
        self.ops.append(dict(eng=eng, fn=None, raw=set(deps), war=set(), dma=False, seq=seq,
                             sig=False, waits=[], idx=idx))
        return idx

    def analyze(self):
        ops = self.ops
        seen = {e: {x: -1 for x in COMPUTE} for e in COMPUTE + QUEUES}
        seen_dma = {e: set() for e in COMPUTE + QUEUES}
        dma_count = {q: 0 for q in QUEUES}
        dma_hist = {q: [] for q in QUEUES}
        for o in ops:
            e = o["eng"]
            deps = [(d, True) for d in sorted(o["raw"])] + [(d, False) for d in sorted(o["war"])]
            if o["dma"]:
                n = dma_count[e]
                dma_count[e] += 1
                o["dsem"] = n % NDMASEM
                o["dval"] = 16 * (n // NDMASEM + 1)
                if n >= NDMASEM:
                    deps.append((dma_hist[e][n - NDMASEM], True))
                dma_hist[e].append(o["idx"])
            for d, is_raw in deps:
                a = ops[d]
                if a["dma"]:
                    if d in seen_dma[e]:
                        continue
                    o["waits"].append(d)
                    seen_dma[e].add(d)
                else:
                    ae = a["eng"]
                    if a["fn"] is None:
                        continue
                    if ae == e and ae == "pe":
                        continue
                    if seen[e][ae] >= a["seq"]:
                        continue
                    o["waits"].append(d)
                    a["sig"] = True
                for x in COMPUTE:
                    if a["vc"][x] > seen[e][x]:
                        seen[e][x] = a["vc"][x]
            vc = dict(seen[e])
            if e in COMPUTE:
                vc[e] = o["seq"]
            o["vc"] = vc
        cnt = {e: 0 for e in COMPUTE}
        for o in ops:
            if o["eng"] in COMPUTE and o["sig"]:
                cnt[o["eng"]] += 1
                o["sval"] = cnt[o["eng"]]

    def emit(self, sems, dsems, block):
        ops = self.ops
        per = {e: [o for o in ops if o["eng"] == e] for e in COMPUTE + QUEUES}

        def run(engname, eng):
            for o in per[engname]:
                for d in o["waits"]:
                    a = ops[d]
                    if a["dma"]:
                        eng.wait_ge(dsems[a["eng"]][a["dsem"]], a["dval"])
                    else:
                        eng.wait_ge(sems[a["eng"]], a["sval"])
                if o["fn"] is None:
                    continue
                ins = o["fn"](eng)
                if o["dma"]:
                    ins.then_inc(dsems[engname][o["dsem"]], 16)
                elif o["sig"]:
                    ins.then_inc(sems[engname], 1)

        block.tensor(lambda eng: run("pe", eng))
        block.scalar(lambda eng: run("act", eng))
        block.vector(lambda eng: run("dve", eng))
        block.gpsimd(lambda eng: run("pool", eng))
        block.sync(lambda eng: run("sp", eng))


class NullSched:
    def op(self, *a, **k):
        return 0

    def enter(self, name):
        pass


NSLOT = 3
LOOKAHEAD = 2
DIRECT0 = False


class WStream:
    def __init__(self, ring):
        self.ring = ring
        self.plan = []
        self.rearr = []
        self.planning = True
        self.issued = 0
        self.taken = 0
        self.S = None

    def view(self, slot, shape):
        a, b = shape
        return self.ring[:, slot, 0:a * b].rearrange("p (a b) -> p a b", a=a)


class StopEmit(Exception):
    pass


def build(NSEQ, SEQLEN, debug=(), stop_at=None):
    nc = bass.Bass("TRN2", target_bir_lowering=False)
    NTOK = NSEQ * SEQLEN
    NT = SEQLEN // 512

    def din(name, shape, dt=F32):
        return nc.dram_tensor(name, shape, dt, kind="ExternalInput").ap()

    def dout(name, shape):
        return nc.dram_tensor(name, shape, F32, kind="ExternalOutput").ap()

    def dscr(name, shape):
        return nc.dram_tensor(name, shape, BF16, kind="Internal").ap()

    xp = din("xp", [NTOK, D])
    xsm = din("xs", [64, D])
    ck = din("ck", [512, D])
    cv = din("cv", [512, D])
    st = din("st", [4, 128, 256])
    wsrc = dict(
        w1i=din("w1i", [D, 2 * DFF]), w1o=din("w1o", [DFF, D]), win=din("win", [D, PW]),
        wbg=din("wbg", [D, D]), wba=din("wba", [D, D]), wo=din("wo", [D, D]),
        w2i=din("w2i", [D, 2 * DFF]), w2o=din("w2o", [DFF, D]), wgate=din("wgate", [17, 512]))
    gains_d = din("gains", [128, 4, 8])
    gnf_d = din("gnf", [1, D])
    biasT_d = din("biasT", [128, 16, 2, 128])
    cb_d = din("cb", [128, 16])
    consts_d = din("consts", [128, 4, 128])

    yp = dout("yp", [NTOK, D])
    ys = dout("ys", [64, D])
    kpo = dout("kpo", [NSEQ * 512, D])
    vpo = dout("vpo", [NSEQ * 512, D])
    spo = dout("spo", [NSEQ, 4, 128, 256])
    kso = dout("kso", [64, D])
    vso = dout("vso", [64, D])
    sso = dout("sso", [4, 128, 256])
    dbg_out = {}

    wfa_b = dscr("wfa_b", [D, 16])
    wgate_b = dscr("wgate_b", [17, 512])
    wsc_holder = {}

    def sb(name, shape, dt):
        return nc.alloc_sbuf_tensor("s_" + name, shape, dt)

    h = sb("h", [128, 4, D], F32)
    xnT = sb("xnT", [128, 8, 512], BF16)
    mixT = sb("mixT", [128, 8, 512], BF16)
    UG = sb("UG", [128, 12288], BF16)
    gT = UG[:, 0:22 * 512].rearrange("p (j t) -> p j t", j=22)
    ebT = UG[:, 0:4096].bitcast(F32).rearrange("p (h t) -> p h t", h=4)
    oaT = UG[:, 0:4096].rearrange("p (k t) -> p k t", k=8)
    enbT = UG[:, 4096:8192].bitcast(F32).rearrange("p (h t) -> p h t", h=4)
    obT = UG[:, 4096:8192].rearrange("p (k t) -> p k t", k=8)
    la = UG[:, 8192:12288].bitcast(F32).rearrange("p (s d) -> p s d", s=4)
    qdT = UG[:, 8192:10240].rearrange("p (h t) -> p h t", h=4)
    kiT = UG[:, 10240:12288].rearrange("p (h t) -> p h t", h=4)
    U45 = sb("U45", [128, 8192], BF16)
    dke = U45[:, 0:4096].bitcast(F32).rearrange("p (s d) -> p s d", s=4)
    oa = U45[:, 0:4096].rearrange("p (s d) -> p s d", s=4)
    va = U45[:, 4096:8192].rearrange("p (s d) -> p s d", s=4)
    ob = U45[:, 4096:8192].rearrange("p (s d) -> p s d", s=4)
    h2 = U45[:, :].bitcast(F32).rearrange("p (s d) -> p s d", s=4)
    U67 = sb("U67", [128, 8192], BF16)
    rr = U67[:, 0:4096].rearrange("p (s d) -> p s d", s=4)
    qbT = U67[:, 4096:8192].rearrange("p (k t) -> p k t", k=8)
    hout = U67[:, :].bitcast(F32).rearrange("p (s d) -> p s d", s=4)
    kend = sb("kend", [128, 4, 512], BF16)
    faT = sb("faT", [32, 512], BF16)
    Pb = sb("Pb", [128, 4, 5, 128], BF16)
    tmpf = sb("tmpf", [128, 2, 512], F32)
    xb = sb("xb", [128, 2, D], BF16)
    junk = sb("junk", [128, 2, D], BF16)
    ystage = sb("ystage", [128, 2, 512], F32)
    kbT = sb("kbT", [128, 8, 1024], BF16)
    vaug = sb("vaug", [128, 8, 16, 65], BF16)
    Sf = sb("Sf", [128, 4, 256], F32)
    Sb = sb("Sb", [128, 3, 4, 256], BF16)
    dec = sb("dec", [128, 4, 8], F32)
    biasT = sb("biasT", [128, 16, 2, 128], F32)
    cbs = sb("cbs", [128, 16], F32)
    constf = sb("constf", [128, 4, 128], F32)
    identb = sb("identb", [128, 128], BF16)
    gains = sb("gains", [128, 4, 8], F32)
    gnfb = sb("gnfb", [128, D], F32)
    epsb = sb("epsb", [128, 1], F32)
    stat = sb("stat", [128, 32], F32)
    rec = sb("rec", [128, 2, 8], F32)
    wfa = sb("wfa", [128, 8, 16], BF16)
    wgate = sb("wgate", [32, 512], BF16)
    ring = sb("ring", [128, NSLOT, 4096], BF16)
    ps = nc.alloc_psum_tensor("ps", [128, 8, 512], F32)

    TRI = constf[:, 1, :]
    TRIC = constf[:, 2, :]
    CMASK = constf[:, 3, :]

    ws = WStream(ring)
    POOLS = {"A": [0, 1, 2, 3], "B": [4, 5, 6, 7], "C": [0, 1, 2, 3, 4, 5], "D": [6, 7]}
    psrr = {k: 0 for k in POOLS}

    def ps_alloc(pool):
        i = psrr[pool]
        psrr[pool] = (i + 1) % len(POOLS[pool])
        return POOLS[pool][i]

    def bank(b):
        return ps[:, b, :]

    def bankb(b):
        return ps[:, b, :].bitcast(BF16)

    dumps = []

    def emit_all(S):
        for k_ in psrr:
            psrr[k_] = 0

        marks = {}
        late_init = []

        def mark(name):
            marks[name] = marks.get(name, 0) + 1
            if stop_at is not None and stop_at == "%s:%d" % (name, marks[name]):
                raise StopEmit()

        def dump(name, ap, key):
            if name not in debug or isinstance(S, NullSched) or name in dbg_out:
                return
            shape = list(ap.shape)
            t = nc.dram_tensor("dbg_" + name, shape, ap.dtype, kind="ExternalOutput").ap()
            dbg_out[name] = t
            S.op("pool", lambda e: e.dma_start(out=t, in_=ap), reads=key, dma=True)

        S.op("pool", lambda e: e.dma_start(out=wfa_b, in_=wsrc["win"][:, 3072:3088]), writes=["wfa_b"], dma=True)
        S.op("pool", lambda e: e.dma_start(out=wgate_b, in_=wsrc["wgate"]), writes=["wgate_b"], dma=True)
        if not ws.planning and not DIRECT0:
            wsc = wsc_holder["t"]
            for j in range(ws.nbt):
                parts_j, shape_j = ws.plan[j]
                for (name_j, reg_j, fn_j), rearr_j in zip(parts_j, ws.rearr[j]):
                    src_j = reg_j[1](wsrc[name_j]).rearrange(rearr_j, p=128)
                    dst_j = fn_j(wsc[j])
                    S.op("pool", (lambda e, d=dst_j, s_=src_j: e.dma_start(out=d, in_=s_)), writes=[("wsc", j)], dma=True)

        S.op("sp", lambda e: e.dma_start(out=constf[:], in_=consts_d), writes=["constf"], dma=True)
        S.op("sp", lambda e: e.dma_start(out=gains[:], in_=gains_d), writes=["gains"], dma=True)
        S.op("sp", lambda e: e.dma_start(out=gnfb[:], in_=gnf_d.partition_broadcast(128)), writes=["gnfb"], dma=True)
        late_init.append(lambda: S.op("sp", lambda e: e.dma_start(out=biasT[:], in_=biasT_d), writes=["biasT"], dma=True))
        late_init.append(lambda: S.op("sp", lambda e: e.dma_start(out=cbs[:], in_=cb_d), writes=["cbs"], dma=True))
        S.op("dve", lambda e: e.tensor_copy(out=identb[:], in_=constf[:, 0, :]), reads=["constf"], writes=["identb"])
        S.op("dve", lambda e: e.memset(epsb[:], EPS), writes=["epsb"])
        S.op("dve", lambda e: e.memset(faT[:], 1.0), writes=["faT"])
        S.op("dve", lambda e: e.memset(vaug[:], 1.0), writes=[("vaug", sl_, hf_) for sl_ in range(8) for hf_ in range(2)])
        S.op("sp", lambda e: e.dma_start(out=wfa[:], in_=wfa_b.rearrange("(k p) n -> p k n", p=128)),
             reads=["wfa_b"], writes=["wfa"], dma=True)
        S.op("sp", lambda e: e.dma_start(out=wgate[0:17, :], in_=wgate_b), reads=["wgate_b"], writes=["wgate"], dma=True)

        mark("init")
        def getw(parts, shape):
            if ws.planning:
                ws.plan.append(([(n_, r_, f_) for (n_, r_, f_, _) in parts], shape))
                ws.rearr.append([x[3] for x in parts])
                return ws.view(0, shape), ("wring", 0)
            i = ws.taken
            assert ws.plan[i][1] == shape, (i, ws.plan[i][1], shape)
            wsc = wsc_holder["t"]
            while ws.issued < min(len(ws.plan), i + LOOKAHEAD + 1):
                j = ws.issued
                parts_j, shape_j = ws.plan[j]
                slot = j % NSLOT
                jb = j % ws.nbt
                n_el = shape_j[0] * shape_j[1]
                if j < ws.nbt and DIRECT0:
                    for (name_j, reg_j, fn_j), rearr_j in zip(parts_j, ws.rearr[j]):
                        src_j = reg_j[1](wsrc[name_j]).rearrange(rearr_j, p=128)
                        dst_j = fn_j(ring[:, slot, :])
                        S.op("pool", (lambda e, d=dst_j, s_=src_j: e.dma_start(out=d, in_=s_)), writes=[("wring", slot)], dma=True)
                    S.op("sp", (lambda e, slot=slot, jb=jb, n_el=n_el: e.dma_start(out=wsc[jb][:, 0:n_el], in_=ring[:, slot, 0:n_el])),
                         reads=[("wring", slot)], writes=[("wsc", jb)], dma=True)
                else:
                    S.op("sp", (lambda e, slot=slot, jb=jb, n_el=n_el: e.dma_start(out=ring[:, slot, 0:n_el], in_=wsc[jb][:, 0:n_el])),
                         reads=[("wsc", jb)], writes=[("wring", slot)], dma=True)
                ws.issued += 1
            ws.taken += 1
            slot = i % NSLOT
            return ws.view(slot, shape), ("wring", slot)

        def v3(base, a_, b_):
            return base[:, 0:a_ * b_].rearrange("p (a b) -> p a b", a=a_)

        def colreg(c0, n):
            return ((0, c0), lambda t: t[:, c0:c0 + n])

        def wk(name, c0, ncols):
            return getw([(name, colreg(c0, ncols), lambda base: v3(base, 8, ncols), "(k p) n -> p k n")], (8, ncols))

        def wk2(name1, c1, name2, c2, n):
            return getw([(name1, colreg(c1, n), lambda base: v3(base, 8, 2 * n)[:, :, 0:n], "(k p) n -> p k n"),
                         (name2, colreg(c2, n), lambda base: v3(base, 8, 2 * n)[:, :, n:2 * n], "(k p) n -> p k n")],
                        (8, 2 * n))

        def wj(name, j0, nj, c0):
            reg = ((j0, c0), lambda t: t[j0 * 128:(j0 + nj) * 128, c0:c0 + 512])
            return getw([(name, reg, lambda base: v3(base, nj, 512), "(j p) n -> p j n")], (nj, 512))

        def mmgroup(out_ap, pairs, reads, writes, start=True, stop=True):
            def fn(e):
                n = len(pairs)
                ins = None
                for i, (l, r) in enumerate(pairs):
                    ins = e.matmul(out_ap, lhsT=l, rhs=r, start=(start and i == 0), stop=(stop and i == n - 1))
                return ins
            return S.op("pe", fn, reads=reads, writes=writes)

        def rstd_of(src_ap, TP, n, col, junk_key, junk_ap, reads):
            S.op("act", lambda e: e.activation(out=junk_ap, in_=src_ap, func=AF.Square, accum_out=stat[:TP, col:col + 1]),
                 reads=reads, writes=[junk_key, ("stat", col)])
            S.op("act", lambda e: e.activation(out=stat[:TP, col + 1:col + 2], in_=stat[:TP, col:col + 1], func=AF.Sqrt,
                                               scale=1.0 / n, bias=epsb[:TP, 0:1]),
                 reads=[("stat", col), "epsb"], writes=[("stat", col + 1)])
            S.op("dve", lambda e: e.reciprocal(out=stat[:TP, col + 1:col + 2], in_=stat[:TP, col + 1:col + 2]),
                 reads=[("stat", col + 1)], writes=[("stat", col + 1)])

        def sqrt_table_prefetch():
            S.op("act", lambda e: e.activation(out=stat[0:1, 21:22], in_=epsb[0:1, 0:1], func=AF.Sqrt),
                 reads=["epsb"], writes=[("stat", 21)])

        def transpose_sub(src_ap, TP, dstT, s, gidx, reads, dstkey, pool="A"):
            b = ps_alloc(pool)
            pb_ = bankb(b)

            def fn(e):
                ins = None
                for k in range(8):
                    ins = e.transpose(out=pb_[:, k * 128:k * 128 + TP], in_=src_ap[:, k * 128:(k + 1) * 128],
                                      identity=identb[:TP, :TP])
                return ins
            S.op("pe", fn, reads=list(reads) + ["identb"], writes=[("ps", b)])
            src3 = pb_.rearrange("p (k t) -> p k t", k=8)[:, :, 0:TP]
            dst3 = dstT[:, :, s * 128:s * 128 + TP]
            if gidx is None:
                S.op("act", lambda e: e.activation(out=dst3, in_=src3, func=AF.Copy),
                     reads=[("ps", b)], writes=[dstkey])
            else:
                g3 = gains[:, gidx, :].unsqueeze(2).broadcast_to([128, 8, TP])
                S.op("dve", lambda e: e.tensor_tensor(out=dst3, in0=src3, in1=g3, op=ALU.mult),
                     reads=[("ps", b), "gains"], writes=[dstkey])

        def norm_stats(NS, TP, src, srcname):
            for s in range(NS):
                if s < 2:
                    S.op("act", (lambda e, s=s: e.activation(out=junk[:TP, 0, :], in_=src[:TP, s, :], func=AF.Square,
                                                             accum_out=stat[:TP, s:s + 1])),
                         reads=[(srcname, s)], writes=[("junk", 0), ("stat", s)])
                else:
                    S.op("dve", (lambda e, s=s: e.scalar_tensor_tensor(
                        out=junk[:TP, 1, :], in0=src[:TP, s, :], scalar=1.0, in1=src[:TP, s, :],
                        op0=ALU.mult, op1=ALU.mult, accum_out=stat[:TP, s:s + 1])),
                        reads=[(srcname, s)], writes=[("junk", 1), ("stat", s)])
            S.op("act", lambda e: e.activation(out=stat[:TP, 4:4 + NS], in_=stat[:TP, 0:NS], func=AF.Sqrt,
                                               scale=1.0 / D, bias=epsb[:TP, 0:1]),
                 reads=[("stat", s) for s in range(NS)] + ["epsb"], writes=[("stat", 4 + s) for s in range(NS)])
            S.op("dve", lambda e: e.reciprocal(out=stat[:TP, 4:4 + NS], in_=stat[:TP, 4:4 + NS]),
                 reads=[("stat", 4 + s) for s in range(NS)], writes=[("stat", 4 + s) for s in range(NS)])

        def norm_apply(NS, TP, gidx, dstT, dstname, src, srcname, scol=4):
            for s in range(NS):
                pbuf = s % 2
                S.op("act", (lambda e, s=s, pbuf=pbuf: e.activation(
                    out=xb[:TP, pbuf, :], in_=src[:TP, s, :], func=AF.Copy, scale=stat[:TP, scol + s:scol + s + 1])),
                    reads=[(srcname, s), ("stat", scol + s)], writes=[("xb", pbuf)])
                transpose_sub(xb[:TP, pbuf, :], TP, dstT, s, gidx, [("xb", pbuf)], (dstname, s))

        def norm_T(T, NS, TP, gidx, dstT, dstname, src=None, srcname="h"):
            src = h if src is None else src
            norm_stats(NS, TP, src, srcname)
            norm_apply(NS, TP, gidx, dstT, dstname, src, srcname)

        def ffn(T, NS, TP, gidx, win_name, wout_name, dst=None, dstname="h", res=None, resname="h", norm_done=False,
                mid_hook=None, pre_hook=None):
            dst = h if dst is None else dst
            res = h if res is None else res
            S.enter("gT")
            if not norm_done:
                norm_T(T, NS, TP, gidx, xnT, "xnT")
            if pre_hook is not None:
                pre_hook()
            xkeys = [("xnT", s) for s in range(NS)]
            dump("xnT_" + win_name, xnT[:, :, 0:T], xkeys)
            for blk in range(11):
                if pending_final and blk >= 1:
                    pending_final.pop(0)()
                w, kw = wk2(win_name, blk * 256, win_name, DFF + blk * 256, 256)
                for c in range(2):
                    j = blk * 2 + c
                    bg = ps_alloc("A")
                    bu = ps_alloc("A")
                    mmgroup(bank(bg)[:, 0:T], [(w[:, k, c * 128:(c + 1) * 128], xnT[:, k, 0:T]) for k in range(8)],
                            xkeys + [kw], [("ps", bg)])
                    mmgroup(bank(bu)[:, 0:T], [(w[:, k, 256 + c * 128:256 + (c + 1) * 128], xnT[:, k, 0:T]) for k in range(8)],
                            xkeys + [kw], [("ps", bu)])
                    tb = j % 2
                    S.op("act", (lambda e, bg=bg, tb=tb: e.activation(out=tmpf[:, tb, 0:T], in_=bank(bg)[:, 0:T], func=AF.Silu)),
                         reads=[("ps", bg)], writes=[("tmpf", tb)])
                    S.op("dve", (lambda e, bu=bu, tb=tb, j=j: e.tensor_tensor(out=gT[:, j, 0:T], in0=bank(bu)[:, 0:T],
                                                                              in1=tmpf[:, tb, 0:T], op=ALU.mult)),
                         reads=[("ps", bu), ("tmpf", tb)], writes=[("gT", j)])
            sqrt_table_prefetch()
            dump("gT_" + win_name, gT[:, :, 0:T], [("gT", j) for j in range(22)])
            for half in range(2):
                if half == 1 and mid_hook is not None:
                    mid_hook()
                banks = [ps_alloc("B" if half == 0 else "A") for _ in range(NS)]
                for jb in range(3):
                    j0 = jb * 8
                    nj = 8 if jb < 2 else 6
                    w, kw = wj(wout_name, j0, nj, half * 512)
                    for s in range(NS):
                        mmgroup(bank(banks[s])[:TP, :],
                                [(gT[:, j0 + jj, s * 128:s * 128 + TP], w[:, jj, :]) for jj in range(nj)],
                                [("gT", j0 + jj) for jj in range(nj)] + [kw], [("ps", banks[s])],
                                start=(jb == 0), stop=(jb == 2))
                for s in range(NS):
                    hv = res[:TP, s, half * 512:(half + 1) * 512]
                    dv = dst[:TP, s, half * 512:(half + 1) * 512]
                    S.op("dve", (lambda e, hv=hv, dv=dv, b=banks[s]: e.scalar_tensor_tensor(
                        out=dv, in0=bank(b)[:TP, :], scalar=0.5, in1=hv, op0=ALU.mult, op1=ALU.add)),
                        reads=[("ps", banks[s]), (resname, s)], writes=[(dstname, s)])

        def tile(kind, q, t):
            if kind == "p":
                T, NS, TP = 512, 4, 128
                tok0 = q * SEQLEN + t * 512
                xsrc = xp[tok0:tok0 + 512, :].rearrange("(s p) d -> p s d", p=128)
                first, last = (t == 0), (t == NT - 1)
                gp0 = 4 * t
                ydst = yp
            else:
                T, NS, TP = 64, 1, 64
                tok0 = 0
                xsrc = xsm.rearrange("(s p) d -> p s d", p=64)
                first, last = False, True
                gp0 = 4
                ydst = ys
            if not prefetched:
                prefetch_x(kind, q, t)
            prefetched.clear()

            if kind == "p" and first:
                S.op("dve", lambda e: e.memset(Sf[:], 0.0), writes=[("Sf", hh) for hh in range(4)])
                S.op("dve", lambda e: e.memset(Sb[:, 0, :, :], 0.0), writes=[("Sb", 0, hh) for hh in range(4)])
            if kind == "s":
                S.op("sp", lambda e: e.dma_start(out=Sf[:], in_=st.rearrange("h d v -> d h v")),
                     writes=[("Sf", hh) for hh in range(4)], dma=True)
                S.op("act", lambda e: e.activation(out=Sb[:, 0, :, :], in_=Sf[:], func=AF.Copy),
                     reads=[("Sf", hh) for hh in range(4)], writes=[("Sb", 0, hh) for hh in range(4)])
                for blk in range(4):
                    S.op("pool", (lambda e, blk=blk: e.dma_start(out=vaug[:, blk, :, 0:64],
                                                                 in_=cv[blk * 128:(blk + 1) * 128, :].rearrange("p (h d) -> p h d", h=16))),
                         writes=[("vaug", blk, 0), ("vaug", blk, 1)], dma=True)
                    pbuf = blk % 2
                    S.op("pool", (lambda e, blk=blk, pbuf=pbuf: e.dma_start(out=xb[:, pbuf, :], in_=ck[blk * 128:(blk + 1) * 128, :])),
                         writes=[("xb", pbuf)], dma=True)
                    b = ps_alloc("A")
                    pb_ = bankb(b)

                    def fn(e, pb_=pb_, pbuf=pbuf):
                        ins = None
                        for k in range(8):
                            ins = e.transpose(out=pb_[:, k * 128:(k + 1) * 128], in_=xb[:, pbuf, k * 128:(k + 1) * 128],
                                              identity=identb[:, :])
                        return ins
                    S.op("pe", fn, reads=[("xb", pbuf), "identb"], writes=[("ps", b)])
                    S.op("act", (lambda e, pb_=pb_, blk=blk: e.activation(
                        out=kbT[:, :, blk * 128:(blk + 1) * 128], in_=pb_.rearrange("p (k t) -> p k t", k=8), func=AF.Copy)),
                        reads=[("ps", b)], writes=[("kbT", blk, c_) for c_ in range(8)])

            mark("load")
            ffn(T, NS, TP, 0, "w1i", "w1o", res=h2, resname="h2", norm_done=True)

            dump("h1", h[:TP, 0:NS, :], [("h", s) for s in range(NS)])
            mark("ffn1")
            while late_init:
                late_init.pop(0)()
            norm_T(T, NS, TP, 1, xnT, "xnT")
            nkeys = [("xnT", s) for s in range(NS)]
            S.enter("ebT")
            S.enter("enbT")
            S.enter("la")
            S.enter("dke")
            b = ps_alloc("A")
            mmgroup(bank(b)[0:16, 0:T], [(wfa[:, k, :], xnT[:, k, 0:T]) for k in range(8)], nkeys + ["wfa"], [("ps", b)])
            S.op("act", (lambda e, b=b: e.activation(out=faT[0:16, 0:T], in_=bank(b)[0:16, 0:T], func=AF.Copy)),
                 reads=[("ps", b)], writes=["faT"])
            def m1_A(s):
                b = ps_alloc("A")
                mmgroup(bank(b)[:TP, :], [(faT[0:17, s * 128:s * 128 + TP], wgate[0:17, :])], ["faT", "wgate"], [("ps", b)])
                tb = s % 2
                S.op("act", (lambda e, b=b, tb=tb: e.activation(out=tmpf[:TP, tb, :], in_=bank(b)[:TP, :], func=AF.Exp, scale=-1.0)),
                     reads=[("ps", b)], writes=[("tmpf", tb)])
                S.op("act", (lambda e, s=s, tb=tb: e.activation(out=la[:TP, s, :], in_=tmpf[:TP, tb, :], func=AF.Ln, bias=1.0)),
                     reads=[("tmpf", tb)], writes=[("la", s)])
            def m1_B(s):
                b1 = ps_alloc("A")

                def fnc(e, s=s, b1=b1):
                    ins = None
                    for hh in range(4):
                        ins = e.matmul(bank(b1)[:, hh * 128:hh * 128 + TP], lhsT=la[:TP, s, hh * 128:(hh + 1) * 128],
                                       rhs=TRI[:TP, :TP], start=True, stop=True)
                    return ins
                S.op("pe", fnc, reads=[("la", s), "constf"], writes=[("ps", b1)])
                src3 = bank(b1).rearrange("p (h t) -> p h t", h=4)[:, :, 0:TP]
                S.op("act", (lambda e, s=s, src3=src3: e.activation(out=ebT[:, :, s * 128:s * 128 + TP], in_=src3,
                                                                    func=AF.Exp, scale=-1.0 / 16)),
                     reads=[("ps", b1)], writes=[("ebT", s)])
                S.op("act", (lambda e, s=s, src3=src3: e.activation(out=enbT[:, :, s * 128:s * 128 + TP], in_=src3,
                                                                    func=AF.Exp, scale=1.0 / 16)),
                     reads=[("ps", b1)], writes=[("enbT", s)])
                b2 = ps_alloc("A")
                mmgroup(bank(b2)[:TP, :], [(TRIC[:TP, :TP], la[:TP, s, :])], [("la", s), "constf"], [("ps", b2)])
                S.op("act", (lambda e, s=s, b2=b2: e.activation(out=dke[:TP, s, :], in_=bank(b2)[:TP, :], func=AF.Exp,
                                                                scale=-1.0 / 16)),
                     reads=[("ps", b2)], writes=[("dke", s)])
                srcd = bank(b1).rearrange("p (h t) -> p h t", h=4)[:, :, TP - 1:TP]
                S.op("act", (lambda e, s=s, srcd=srcd: e.activation(out=dec[:, :, s:s + 1], in_=srcd,
                                                                    func=AF.Exp, scale=-1.0 / 16)),
                     reads=[("ps", b1)], writes=[("dec", s)])
            S.enter("va")
            S.enter("rr")
            S.enter("qbT")
            ebkeys = [("ebT", s) for s in range(NS)]
            enkeys = [("enbT", s) for s in range(NS)]
            def proj_qa():
                w, kw = wk("win", 0, 512)
                for hh in range(4):
                    b = ps_alloc("A")
                    mmgroup(bank(b)[:, 0:T], [(w[:, k, hh * 128:(hh + 1) * 128], xnT[:, k, 0:T]) for k in range(8)],
                            nkeys + [kw], [("ps", b)])
                    S.op("dve", (lambda e, b=b, hh=hh: e.scalar_tensor_tensor(
                        out=qdT[:, hh, 0:T], in0=bank(b)[:, 0:T], scalar=128.0 ** -0.5, in1=ebT[:, hh, 0:T],
                        op0=ALU.mult, op1=ALU.mult)), reads=[("ps", b)] + ebkeys, writes=[("qdT", hh)])
            def proj_ka():
                w, kw = wk("win", 512, 512)
                for hh in range(4):
                    b = ps_alloc("A")
                    mmgroup(bank(b)[:, 0:T], [(w[:, k, hh * 128:(hh + 1) * 128], xnT[:, k, 0:T]) for k in range(8)],
                            nkeys + [kw], [("ps", b)])
                    S.op("dve", (lambda e, b=b, hh=hh: e.tensor_tensor(out=kiT[:, hh, 0:T], in0=bank(b)[:, 0:T],
                                                                       in1=enbT[:, hh, 0:T], op=ALU.mult)),
                         reads=[("ps", b)] + enkeys, writes=[("kiT", hh)])
                for s in range(NS):
                    b = ps_alloc("A")
                    mmgroup(bank(b)[:TP, :], [(xnT[:, k, s * 128:s * 128 + TP], w[:, k, :]) for k in range(8)],
                            [("xnT", s), kw], [("ps", b)])
                    S.op("dve", (lambda e, b=b, s=s: e.tensor_tensor(
                        out=kend[:TP, s, :], in0=bank(b)[:TP, :], in1=dke[:TP, s, :], op=ALU.mult)),
                        reads=[("ps", b), ("dke", s)], writes=[("kend", s)])
            def proj_va(half):
                w, kw = wk("win", 1024 + half * 512, 512)
                for s in range(NS):
                    b = ps_alloc("A")
                    mmgroup(bank(b)[:TP, :], [(xnT[:, k, s * 128:s * 128 + TP], w[:, k, :]) for k in range(8)],
                            [("xnT", s), kw], [("ps", b)])
                    S.op("dve", (lambda e, b=b, s=s, half=half: e.tensor_copy(
                        out=va[:TP, s, half * 512:(half + 1) * 512], in_=bank(b)[:TP, :])),
                        reads=[("ps", b)], writes=[("va", s, half)])
            def proj_ra(half):
                w, kw = wk("win", 2048 + half * 512, 512)
                for s in range(NS):
                    b = ps_alloc("A")
                    mmgroup(bank(b)[:TP, :], [(xnT[:, k, s * 128:s * 128 + TP], w[:, k, :]) for k in range(8)],
                            [("xnT", s), kw], [("ps", b)])
                    S.op("act", (lambda e, b=b, s=s, half=half: e.activation(
                        out=rr[:TP, s, half * 512:(half + 1) * 512], in_=bank(b)[:TP, :], func=AF.Silu)),
                        reads=[("ps", b)], writes=[("rr", s, half)])
            def proj_qb_kb_vb():
                for half in range(2):
                    w, kw = wk("win", 3088 + half * 512, 512)
                    for c in range(4):
                        b = ps_alloc("A")
                        mmgroup(bank(b)[:, 0:T], [(w[:, k, c * 128:(c + 1) * 128], xnT[:, k, 0:T]) for k in range(8)],
                                nkeys + [kw], [("ps", b)])
                        S.op("act", (lambda e, b=b, c=c, half=half: e.activation(out=qbT[:, half * 4 + c, 0:T], in_=bank(b)[:, 0:T],
                                                                                 func=AF.Copy)),
                             reads=[("ps", b)], writes=[("qbT", half * 4 + c)])
                mark("m2d")
                slots = [(gp0 + s) % 8 for s in range(NS)]
                kpos0 = slots[0] * 128
                for half in range(2):
                    w, kw = wk("win", 4112 + half * 512, 512)
                    for c in range(4):
                        b = ps_alloc("A")
                        mmgroup(bank(b)[:, 0:T], [(w[:, k, c * 128:(c + 1) * 128], xnT[:, k, 0:T]) for k in range(8)],
                                nkeys + [kw], [("ps", b)])
                        S.op("act", (lambda e, b=b, c=c, half=half: e.activation(out=kbT[:, half * 4 + c, kpos0:kpos0 + T],
                                                                                 in_=bank(b)[:, 0:T], func=AF.Copy)),
                             reads=[("ps", b)], writes=[("kbT", sl, half * 4 + c) for sl in slots])
                    if last:
                        for s in range(NS):
                            b = ps_alloc("A")
                            mmgroup(bank(b)[:TP, :], [(xnT[:, k, s * 128:s * 128 + TP], w[:, k, :]) for k in range(8)],
                                    [("xnT", s), kw], [("ps", b)])
                            ob_ = (2 * s + half) % 2
                            S.op("act", (lambda e, b=b, ob_=ob_: e.activation(out=ystage[:TP, ob_, :], in_=bank(b)[:TP, :], func=AF.Copy)),
                                 reads=[("ps", b)], writes=[("ystage", ob_)])
                            if kind == "p":
                                dst = kpo[q * 512 + s * 128:q * 512 + s * 128 + 128, half * 512:(half + 1) * 512]
                            else:
                                dst = kso[:, half * 512:(half + 1) * 512]
                            S.op("pool", (lambda e, dst=dst, ob_=ob_: e.dma_start(out=dst, in_=ystage[:TP, ob_, :])),
                                 reads=[("ystage", ob_)], dma=True)
                mark("m2e")
                for half in range(2):
                    w, kw = wk("win", 5136 + half * 512, 512)
                    for s in range(NS):
                        b = ps_alloc("A")
                        mmgroup(bank(b)[:TP, :], [(xnT[:, k, s * 128:s * 128 + TP], w[:, k, :]) for k in range(8)],
                                [("xnT", s), kw], [("ps", b)])
                        sl = slots[s]
                        S.op("act", (lambda e, b=b, sl=sl, half=half: e.activation(
                            out=vaug[:TP, sl, half * 8:(half + 1) * 8, 0:64],
                            in_=bank(b)[:TP, :].rearrange("p (h d) -> p h d", h=8), func=AF.Copy)),
                            reads=[("ps", b)], writes=[("vaug", sl, half)])
                        if last:
                            ob_ = (2 * s + half) % 2
                            S.op("dve", (lambda e, b=b, ob_=ob_: e.tensor_copy(out=ystage[:TP, ob_, :], in_=bank(b)[:TP, :])),
                                 reads=[("ps", b), ("vaug", sl, half)], writes=[("ystage", ob_)])
                            if kind == "p":
                                dst = vpo[q * 512 + s * 128:q * 512 + s * 128 + 128, half * 512:(half + 1) * 512]
                            else:
                                dst = vso[:, half * 512:(half + 1) * 512]
                            S.op("pool", (lambda e, dst=dst, ob_=ob_: e.dma_start(out=dst, in_=ystage[:TP, ob_, :])),
                                 reads=[("ystage", ob_)], dma=True)
            for s in range(NS):
                m1_A(s)
            proj_va(0)
            for s in range(0, min(2, NS)):
                m1_B(s)
            proj_va(1)
            for s in range(2, NS):
                m1_B(s)
            proj_ra(0)
            proj_ra(1)
            proj_qb_kb_vb()
            S.enter("qdT")
            S.enter("kiT")
            proj_qa()
            proj_ka()
            dump("qdT", qdT[:, :, 0:T], [("qdT", hh) for hh in range(4)])
            dump("kiT", kiT[:, :, 0:T], [("kiT", hh) for hh in range(4)])

            mark("m2")
            S.enter("oa")
            S.enter("oaT")
            gla_tail = []
            gla_epi = []
            S.enter("Pbg")
            sqrt_table_prefetch()
            sbase = (4 * t) if kind == "p" else 0
            info = {}

            def partA_all(s):
                rd = (sbase + s) % 3
                wr = (sbase + s + 1) % 3
                for hh in range(4):
                    bi = ps_alloc("A")
                    mmgroup(bank(bi)[:, 0:256], [(kend[:TP, s, hh * 128:(hh + 1) * 128], va[:TP, s, hh * 256:(hh + 1) * 256])],
                            [("kend", s), ("va", s, hh // 2)], [("ps", bi)])
                    bs = ps_alloc("A")
                    mmgroup(bank(bs)[:TP, 0:TP], [(kiT[:, hh, s * 128:s * 128 + TP], qdT[:, hh, s * 128:s * 128 + TP])],
                            [("kiT", hh), ("qdT", hh)], [("ps", bs)])
                    pslot = (hh, 1 + s % 2)
                    S.op("dve", (lambda e, bs=bs, pslot=pslot: e.tensor_tensor(
                        out=Pb[:TP, pslot[0], pslot[1], 0:TP], in0=bank(bs)[:TP, 0:TP], in1=CMASK[:TP, :TP], op=ALU.mult)),
                        reads=[("ps", bs), "constf"], writes=[("Pbg", pslot)])
                    S.op("dve", (lambda e, bi=bi, hh=hh: e.scalar_tensor_tensor(
                        out=Sf[:, hh, :], in0=Sf[:, hh, :], scalar=dec[:, hh, s:s + 1],
                        in1=bank(bi)[:, 0:256], op0=ALU.mult, op1=ALU.add)),
                        reads=[("Sf", hh), ("dec", s), ("ps", bi)], writes=[("Sf", hh)])
                    S.op("act", (lambda e, hh=hh, wr=wr: e.activation(out=Sb[:, wr, hh, :], in_=Sf[:, hh, :], func=AF.Copy)),
                         reads=[("Sf", hh)], writes=[("Sb", wr, hh)])
                info[s] = rd

            def partB_all(s):
                rd = info[s]
                bo = [ps_alloc("B"), ps_alloc("B")]
                for hh in range(4):
                    obank = bank(bo[hh // 2])
                    ocol = (hh % 2) * 256
                    pslot = (hh, 1 + s % 2)

                    def fno(e, obank=obank, ocol=ocol, hh=hh, pslot=pslot):
                        e.matmul(obank[:TP, ocol:ocol + 256], lhsT=qdT[:, hh, s * 128:s * 128 + TP],
                                 rhs=Sb[:, rd, hh, :], start=True, stop=False)
                        return e.matmul(obank[:TP, ocol:ocol + 256], lhsT=Pb[:TP, pslot[0], pslot[1], 0:TP],
                                        rhs=va[:TP, s, hh * 256:(hh + 1) * 256], start=False, stop=True)
                    S.op("pe", fno, reads=[("qdT", hh), ("Pbg", pslot), ("va", s, hh // 2), ("Sb", rd, hh)],
                         writes=[("ps", bo[hh // 2])])
                for hh in range(4):
                    obank = bank(bo[hh // 2])
                    ocol = (hh % 2) * 256
                    tb = hh % 2
                    S.op("act", (lambda e, hh=hh, obank=obank, ocol=ocol, tb=tb: e.activation(
                        out=tmpf[:TP, tb, 0:256], in_=obank[:TP, ocol:ocol + 256], func=AF.Square,
                        accum_out=stat[:TP, 8 + hh:9 + hh])),
                        reads=[("ps", bo[hh // 2])], writes=[("tmpf", tb), ("stat", 8 + hh)])
                S.op("act", lambda e: e.activation(out=stat[:TP, 12:16], in_=stat[:TP, 8:12], func=AF.Sqrt,
                                                   scale=1.0 / 256, bias=epsb[:TP, 0:1]),
                     reads=[("stat", 8 + hh) for hh in range(4)] + ["epsb"], writes=[("stat", 12 + hh) for hh in range(4)])
                S.op("dve", lambda e: e.reciprocal(out=stat[:TP, 12:16], in_=stat[:TP, 12:16]),
                     reads=[("stat", 12 + hh) for hh in range(4)], writes=[("stat", 12 + hh) for hh in range(4)])
                for hh in range(4):
                    obank = bank(bo[hh // 2])
                    ocol = (hh % 2) * 256
                    S.op("dve", (lambda e, hh=hh, obank=obank, ocol=ocol: e.scalar_tensor_tensor(
                        out=oa[:TP, s, hh * 256:(hh + 1) * 256], in0=obank[:TP, ocol:ocol + 256],
                        scalar=stat[:TP, 12 + hh:13 + hh], in1=rr[:TP, s, hh * 256:(hh + 1) * 256],
                        op0=ALU.mult, op1=ALU.mult)),
                        reads=[("ps", bo[hh // 2]), ("stat", 12 + hh), ("rr", s, hh // 2)], writes=[("oa", s, hh)])

            def gtail(s):
                transpose_sub(oa[:TP, s, :], TP, oaT, s, 3, [("oa", s, hh) for hh in range(4)], ("oaT", s), pool="A")

            partA_all(0)
            for s in range(1, NS):
                partA_all(s)
                partB_all(s - 1)
                if s >= 2:
                    gtail(s - 2)
            partB_all(NS - 1)
            if NS >= 2:
                gtail(NS - 2)
            gla_tail.append(lambda: gtail(NS - 1))

            if last:
                sdst = spo[q] if kind == "p" else sso
                S.op("pool", lambda e: e.dma_start(out=sdst.rearrange("h d v -> d h v"), in_=Sf[:]),
                     reads=[("Sf", hh) for hh in range(4)], dma=True)

            mark("m3")
            S.enter("ob")
            S.enter("obT")
            S.enter("Pb")
            accs = {}

            def att_sc(s, hd):
                gp = gp0 + s
                blks = [i for i in range(5) if gp - 4 + i >= 0]
                c = hd // 2
                pb0 = (hd % 2) * 64
                pbuf = (s * 16 + hd) % 4
                bA = ps_alloc("C")
                bB = ps_alloc("C")
                cst = [i for i in blks if i < 3]

                def fns(e):
                    ins = None
                    for i in blks:
                        sl = (gp - 4 + i) % 8
                        nk = TP if i == 4 else 128
                        outp = bank(bA)[:nk, i * 128:i * 128 + TP] if i < 3 else bank(bB)[:nk, (i - 3) * 128:(i - 3) * 128 + TP]
                        ins = e.matmul(outp, lhsT=kbT[pb0:pb0 + 64, c, sl * 128:sl * 128 + nk],
                                       rhs=qbT[pb0:pb0 + 64, c, s * 128:s * 128 + TP], start=True, stop=True)
                    return ins
                S.op("pe", fns, reads=[("qbT", c)] + [("kbT", (gp - 4 + i) % 8, c) for i in blks],
                     writes=[("ps", bA), ("ps", bB)])
                tb = (s * 16 + hd) % 2
                S.op("dve", lambda e: e.scalar_tensor_tensor(
                    out=tmpf[:, tb, 0:256].rearrange("p (b q) -> p b q", b=2)[:, :, 0:TP],
                    in0=bank(bB)[:, 0:256].rearrange("p (b q) -> p b q", b=2)[:, :, 0:TP], scalar=0.125,
                    in1=biasT[:, hd, :, 0:TP], op0=ALU.mult, op1=ALU.add),
                    reads=[("ps", bB), "biasT"], writes=[("tmpf", tb)])
                if cst:
                    i0 = cst[0]
                    ncst = len(cst)
                    S.op("act", lambda e: e.activation(
                        out=Pb[:, pbuf, i0:i0 + ncst, 0:TP],
                        in_=bank(bA)[:, i0 * 128:(i0 + ncst) * 128].rearrange("p (b q) -> p b q", b=ncst)[:, :, 0:TP],
                        func=AF.Exp, scale=0.125, bias=cbs[:, hd:hd + 1]),
                        reads=[("ps", bA), "cbs"], writes=[("Pb", pbuf, 0)])
                i3 = 0 if 3 in blks else 1
                S.op("act", lambda e: e.activation(
                    out=Pb[:, pbuf, 3 + i3:5, 0:TP],
                    in_=tmpf[:, tb, 0:256].rearrange("p (b q) -> p b q", b=2)[:, i3:2, 0:TP], func=AF.Exp),
                    reads=[("tmpf", tb)], writes=[("Pb", pbuf, 1)])
                if 0 in blks and TP == 128:
                    S.op("dve", lambda e: e.memset(Pb[0:64, pbuf, 0, 64:128], 0.0), writes=[("Pb", pbuf, 0)])

            GRP = [(0, 7), (7, 7), (14, 2)]

            def att_pv(s, hd):
                gp = gp0 + s
                blks = [i for i in range(5) if gp - 4 + i >= 0]
                pbuf = (s * 16 + hd) % 4
                g = 0 if hd < 7 else (1 if hd < 14 else 2)
                hd0, nh = GRP[g]
                if hd == hd0:
                    accs[(s, g)] = ps_alloc("D")
                ab = accs[(s, g)]
                acc = bank(ab)
                acol = (hd - hd0) * 65

                def fnv(e):
                    ins = None
                    for n_, i in enumerate(blks):
                        sl = (gp - 4 + i) % 8
                        nk = TP if i == 4 else 128
                        ins = e.matmul(acc[:TP, acol:acol + 65], lhsT=Pb[:nk, pbuf, i, 0:TP], rhs=vaug[:nk, sl, hd, :],
                                       start=(n_ == 0), stop=(n_ == len(blks) - 1))
                    return ins
                S.op("pe", fnv, reads=[("Pb", pbuf, 0), ("Pb", pbuf, 1)] + [("vaug", (gp - 4 + i) % 8, hd // 8) for i in blks],
                     writes=[("ps", ab)])
                if hd == hd0 + nh - 1:
                    a3 = acc[:TP, 0:nh * 65].rearrange("p (h d) -> p h d", h=nh)
                    rb = g % 2
                    S.op("dve", lambda e: e.reciprocal(out=rec[:TP, rb, 0:nh], in_=a3[:, :, 64]),
                         reads=[("ps", ab)], writes=[("rec", rb)])
                    S.op("dve", lambda e: e.tensor_tensor(
                        out=ob[:TP, s, hd0 * 64:(hd0 + nh) * 64].rearrange("p (h d) -> p h d", h=nh), in0=a3[:, :, 0:64],
                        in1=rec[:TP, rb, 0:nh].unsqueeze(2).broadcast_to([TP, nh, 64]), op=ALU.mult),
                        reads=[("ps", ab), ("rec", rb)], writes=[("ob", s, g)])

            items = [(s, hd) for s in range(NS) for hd in range(16)]
            DEPTH = 2
            if gla_epi:
                gla_epi.pop()()
            for i, (s, hd) in enumerate(items):
                att_sc(s, hd)
                if i == 6 and gla_tail:
                    gla_tail.pop()()
                if i >= DEPTH:
                    ps_, ph_ = items[i - DEPTH]
                    att_pv(ps_, ph_)
                    if ph_ == 15 and False:
                        pass
                if hd == 3 and s > 0:
                    transpose_sub(ob[:TP, s - 1, :], TP, obT, s - 1, None, [("ob", s - 1, g) for g in range(3)], ("obT", s - 1))
            for j in range(len(items) - DEPTH, len(items)):
                att_pv(*items[j])
            if gla_tail:
                gla_tail.pop()()
            ob_tail = [lambda: transpose_sub(ob[:TP, NS - 1, :], TP, obT, NS - 1, None, [("ob", NS - 1, g) for g in range(3)],
                                             ("obT", NS - 1))]

            mark("m4")
            oakeys = [("oaT", s) for s in range(NS)]
            obkeys = [("obT", s) for s in range(NS)]
            for mp in range(4):
                w, kw = wk2("win", 6160 + mp * 256, "wbg", mp * 256, 256)
                if mp == 0:
                    oakeys_ = list(oakeys)
                for c in range(2):
                    m = mp * 2 + c
                    b1 = ps_alloc("A")
                    b2 = ps_alloc("A")
                    mmgroup(bank(b1)[:, 0:T], [(w[:, k, c * 128:(c + 1) * 128], xnT[:, k, 0:T]) for k in range(8)],
                            nkeys + [kw], [("ps", b1)])
                    mmgroup(bank(b2)[:, 0:T], [(w[:, k, 256 + c * 128:256 + (c + 1) * 128], oaT[:, k, 0:T]) for k in range(8)],
                            oakeys + [kw], [("ps", b2)])
                    tb = m % 2
                    S.op("act", (lambda e, b1=b1, tb=tb: e.activation(out=tmpf[:, tb, 0:T], in_=bank(b1)[:, 0:T], func=AF.Sigmoid)),
                         reads=[("ps", b1)], writes=[("tmpf", tb)])
                    S.op("dve", (lambda e, b2=b2, tb=tb, m=m: e.tensor_tensor(out=mixT[:, m, 0:T], in0=bank(b2)[:, 0:T],
                                                                              in1=tmpf[:, tb, 0:T], op=ALU.mult)),
                         reads=[("ps", b2), ("tmpf", tb)], writes=[("mixT", m)])
                if ob_tail:
                    ob_tail.pop()()
                w, kw = wk2("win", 7184 + mp * 256, "wba", mp * 256, 256)
                for c in range(2):
                    m = mp * 2 + c
                    b1 = ps_alloc("A")
                    b2 = ps_alloc("A")
                    mmgroup(bank(b1)[:, 0:T], [(w[:, k, c * 128:(c + 1) * 128], xnT[:, k, 0:T]) for k in range(8)],
                            nkeys + [kw], [("ps", b1)])
                    mmgroup(bank(b2)[:, 0:T], [(w[:, k, 256 + c * 128:256 + (c + 1) * 128], obT[:, k, 0:T]) for k in range(8)],
                            obkeys + [kw], [("ps", b2)])
                    tb = m % 2
                    S.op("act", (lambda e, b1=b1, tb=tb: e.activation(out=tmpf[:, tb, 0:T], in_=bank(b1)[:, 0:T], func=AF.Sigmoid)),
                         reads=[("ps", b1)], writes=[("tmpf", tb)])
                    S.op("dve", (lambda e, b2=b2, tb=tb: e.tensor_tensor(out=tmpf[:, tb, 0:T], in0=bank(b2)[:, 0:T],
                                                                         in1=tmpf[:, tb, 0:T], op=ALU.mult)),
                         reads=[("ps", b2), ("tmpf", tb)], writes=[("tmpf", tb)])
                    S.op("dve", (lambda e, tb=tb, m=m: e.tensor_tensor(out=mixT[:, m, 0:T], in0=tmpf[:, tb, 0:T],
                                                                       in1=mixT[:, m, 0:T], op=ALU.add)),
                         reads=[("tmpf", tb), ("mixT", m)], writes=[("mixT", m)])
            sqrt_table_prefetch()
            mkeys = [("mixT", m) for m in range(8)]
            for half in range(2):
                w, kw = wk("wo", half * 512, 512)
                for s in range(NS):
                    b = ps_alloc("A")
                    mmgroup(bank(b)[:TP, :], [(mixT[:, k, s * 128:s * 128 + TP], w[:, k, :]) for k in range(8)],
                            mkeys + [kw], [("ps", b)])
                    hv = h[:TP, s, half * 512:(half + 1) * 512]
                    S.op("dve", (lambda e, hv=hv, b=b: e.tensor_tensor(out=hv, in0=bank(b)[:TP, :], in1=hv, op=ALU.add)),
                         reads=[("ps", b), ("h", s)], writes=[("h", s)])
            dump("hmix", h[:TP, 0:NS, :], [("h", s) for s in range(NS)])

            mark("m5")
            S.enter("hout")
            nxt = next_tile.get((kind, q, t))
            hookA = (lambda: prefetch_x(*nxt, part="load")) if nxt is not None else None
            hookB = (lambda: (prefetch_x(*nxt, part="apply"), prefetched.append(1))) if nxt is not None else None
            ffn(T, NS, TP, 2, "w2i", "w2o", dst=hout, dstname="hout", mid_hook=hookB, pre_hook=hookA)
            mark("ffn2")

            def final_sub(s, TP=TP, ydst=ydst, tok0=tok0):
                col = 16 + 2 * (s % 2)
                rstd_of(hout[:TP, s, :], TP, D, col, ("tmpf", s % 2), tmpf[:TP, s % 2, :].bitcast(BF16), [("hout", s)])
                for half in range(2):
                    yb_ = half
                    S.op("dve", (lambda e, half=half, yb_=yb_: e.scalar_tensor_tensor(
                        out=ystage[:TP, yb_, :], in0=hout[:TP, s, half * 512:(half + 1) * 512], scalar=stat[:TP, col + 1:col + 2],
                        in1=gnfb[:TP, half * 512:(half + 1) * 512], op0=ALU.mult, op1=ALU.mult)),
                        reads=[("hout", s), ("stat", col + 1), "gnfb"], writes=[("ystage", yb_)])
                    dst_ = ydst[tok0 + s * 128:tok0 + s * 128 + TP, half * 512:(half + 1) * 512]
                    S.op("pool", (lambda e, dst_=dst_, yb_=yb_: e.dma_start(out=dst_, in_=ystage[:TP, yb_, :])),
                         reads=[("ystage", yb_)], dma=True)
            for s in range(NS):
                pending_final.append(lambda s=s, f=final_sub: f(s))

        pending_final = []
        prefetched = []
        order = [("p", q_, t_) for q_ in range(NSEQ) for t_ in range(NT)] + [("s", 0, 0)]
        next_tile = {order[i]: order[i + 1] for i in range(len(order) - 1)}

        def prefetch_x(kind, q, t, part="all"):
            if kind == "p":
                T, NS, TP = 512, 4, 128
                tok0 = q * SEQLEN + t * 512
                xsrc = xp[tok0:tok0 + 512, :].rearrange("(s p) d -> p s d", p=128)
            else:
                T, NS, TP = 64, 1, 64
                xsrc = xsm.rearrange("(s p) d -> p s d", p=64)
            if part in ("all", "load"):
                S.enter("h2")
                for s in range(NS):
                    S.op("sp", (lambda e, s=s: e.dma_start(out=h2[:TP, s, :], in_=xsrc[:, s, :])), writes=[("h2", s)], dma=True)
                for s in range(NS):
                    S.op("dve", (lambda e, s=s: e.scalar_tensor_tensor(
                        out=junk[:TP, 1, :], in0=h2[:TP, s, :], scalar=1.0, in1=h2[:TP, s, :],
                        op0=ALU.mult, op1=ALU.mult, accum_out=stat[:TP, 24 + s:25 + s])),
                        reads=[("h2", s)], writes=[("junk", 1), ("stat", 24 + s)])
                S.op("act", lambda e: e.activation(out=stat[:TP, 28:28 + NS], in_=stat[:TP, 24:24 + NS], func=AF.Sqrt,
                                                   scale=1.0 / D, bias=epsb[:TP, 0:1]),
                     reads=[("stat", 24 + s) for s in range(NS)] + ["epsb"], writes=[("stat", 28 + s) for s in range(NS)])
                S.op("dve", lambda e: e.reciprocal(out=stat[:TP, 28:28 + NS], in_=stat[:TP, 28:28 + NS]),
                     reads=[("stat", 28 + s) for s in range(NS)], writes=[("stat", 28 + s) for s in range(NS)])
            if part in ("all", "apply"):
                norm_apply(NS, TP, 0, xnT, "xnT", h2, "h2", scol=28)
        for q in range(NSEQ):
            for t in range(NT):
                tile("p", q, t)
        tile("s", 0, 0)
        while pending_final:
            pending_final.pop(0)()

    ws.planning = True
    try:
        emit_all(NullSched())
    except StopEmit:
        pass
    ws.planning = False
    ntiles = NSEQ * NT + 1
    if stop_at is None:
        assert len(ws.plan) % ntiles == 0
        ws.nbt = len(ws.plan) // ntiles
        for j in range(len(ws.plan)):
            assert ws.plan[j][1] == ws.plan[j % ws.nbt][1]
    else:
        ws.nbt = len(ws.plan)
    wsc_holder["t"] = nc.dram_tensor("wsc", [ws.nbt, 128, 4096], BF16, kind="Internal").ap()
    S = Sched()
    try:
        emit_all(S)
    except StopEmit:
        pass
    outs = [i for i, o in enumerate(S.ops) if o["dma"] and o["eng"] == "pool"]
    S.barrier("pool", outs)
    S.analyze()
    with contextlib.ExitStack() as stk:
        sems = {e: stk.enter_context(nc.semaphore("s_" + e)) for e in COMPUTE}
        dsems = {q_: [stk.enter_context(nc.semaphore("d_%s%d" % (q_, i))) for i in range(NDMASEM)] for q_ in QUEUES}
        block = stk.enter_context(nc.Block())
        S.emit(sems, dsems, block)
    return nc, dbg_out, S


def host_consts(attn_rel_bias):
    tab = np.asarray(attn_rel_bias, dtype=np.float32)[0]
    ext = np.concatenate([tab, np.repeat(tab[:, 256:257], 127, axis=1)], axis=1)
    p = np.arange(128)[:, None]
    qq = np.arange(128)[None, :]
    biasT = np.empty((128, 16, 2, 128), np.float32)
    biasT[:, :, 0, :] = np.transpose(ext[:, 256 + qq - p], (1, 0, 2))
    blkA = np.transpose(ext[:, 128 + qq - p], (1, 0, 2)).copy()
    blkA[64:128, :, 0:64] = -30000.0
    biasT[:, :, 1, :] = blkA
    cb = np.broadcast_to(tab[None, :, 256], (128, 16)).astype(np.float32).copy()
    consts = np.zeros((128, 4, 128), np.float32)
    consts[:, 0, :] = np.eye(128, dtype=np.float32)
    j = np.arange(128)[:, None]
    i = np.arange(128)[None, :]
    same = np.ones((128, 128), dtype=bool)
    consts[:, 1, :] = (same & (j <= i)).astype(np.float32)
    consts[:, 2, :] = (same & (j > i)).astype(np.float32)
    consts[:, 3, :] = (same & (j <= i)).astype(np.float32)
    return biasT, cb, consts


def make_common(norm_ffn1, w_ffn1_in, w_ffn1_out, norm_mix, w_in, w_gla_gate, b_gla_gate, gla_norm, attn_rel_bias,
                w_branch_gla, w_branch_att, w_out, norm_ffn2, w_ffn2_in, w_ffn2_out, norm_final):
    f = lambda a: np.ascontiguousarray(np.asarray(a, dtype=np.float32))
    biasT, cb, consts = host_consts(attn_rel_bias)
    gains = np.stack([f(norm_ffn1)[0], f(norm_mix)[0], f(norm_ffn2)[0], f(gla_norm)[0]], axis=0)
    gains = np.ascontiguousarray(gains.reshape(4, 8, 128).transpose(2, 0, 1))
    return dict(
        w1i=f(w_ffn1_in)[0], w1o=f(w_ffn1_out)[0], win=f(w_in)[0], wbg=f(w_branch_gla)[0], wba=f(w_branch_att)[0],
        wo=f(w_out)[0], w2i=f(w_ffn2_in)[0], w2o=f(w_ffn2_out)[0],
        wgate=np.ascontiguousarray(np.concatenate([f(w_gla_gate)[0], f(b_gla_gate)], axis=0)),
        gains=gains, gnf=f(norm_final).reshape(1, D), biasT=biasT, cb=cb, consts=consts)


_CACHE = {}


def kernel(x_prompt, x_sample, cache_att_k, cache_att_v, state_gla, norm_ffn1, w_ffn1_in, w_ffn1_out, norm_mix, w_in,
           w_gla_gate, b_gla_gate, gla_norm, attn_rel_bias, w_branch_gla, w_branch_att, w_out, norm_ffn2, w_ffn2_in,
           w_ffn2_out, norm_final):
    x_prompt = np.asarray(x_prompt, dtype=np.float32)
    x_sample = np.asarray(x_sample, dtype=np.float32)
    cache_att_k = np.asarray(cache_att_k, dtype=np.float32)
    cache_att_v = np.asarray(cache_att_v, dtype=np.float32)
    state_gla = np.asarray(state_gla, dtype=np.float32)
    B, SEQ, _ = x_prompt.shape
    NB = x_sample.shape[0]
    n = NB
    NSEQ = B // n
    key = (NSEQ, SEQ)
    if key not in _CACHE:
        _CACHE[key] = build(NSEQ, SEQ)[0]
    nc = _CACHE[key]
    common = make_common(norm_ffn1, w_ffn1_in, w_ffn1_out, norm_mix, w_in, w_gla_gate, b_gla_gate, gla_norm, attn_rel_bias,
                         w_branch_gla, w_branch_att, w_out, norm_ffn2, w_ffn2_in, w_ffn2_out, norm_final)
    in_maps = []
    for c in range(n):
        m = dict(common)
        m["xp"] = np.ascontiguousarray(x_prompt[c * NSEQ:(c + 1) * NSEQ].reshape(NSEQ * SEQ, D))
        m["xs"] = np.ascontiguousarray(x_sample[c])
        m["ck"] = np.ascontiguousarray(cache_att_k[0, c].reshape(512, D))
        m["cv"] = np.ascontiguousarray(cache_att_v[0, c].reshape(512, D))
        m["st"] = np.ascontiguousarray(state_gla[0, c])
        in_maps.append(m)
    res = run_bass_kernel_spmd(nc, in_maps, core_ids=list(range(n)))
    R = res.results
    y_prompt = np.stack([R[c]["yp"].reshape(NSEQ, SEQ, D) for c in range(n)]).reshape(B, SEQ, D)
    y_sample = np.stack([R[c]["ys"] for c in range(n)])
    kp = np.stack([R[c]["kpo"].reshape(NSEQ, 512, 16, 64) for c in range(n)]).reshape(1, B, 512, 16, 64)
    vp = np.stack([R[c]["vpo"].reshape(NSEQ, 512, 16, 64) for c in range(n)]).reshape(1, B, 512, 16, 64)
    sp = np.stack([R[c]["spo"] for c in range(n)]).reshape(1, B, 4, 128, 256)
    ks = np.stack([R[c]["kso"].reshape(64, 16, 64) for c in range(n)])[None]
    vs = np.stack([R[c]["vso"].reshape(64, 16, 64) for c in range(n)])[None]
    ss = np.stack([R[c]["sso"] for c in range(n)])[None]
    f = lambda a: np.ascontiguousarray(a, dtype=np.float32)
    return (f(y_prompt), f(y_sample), f(kp), f(vp), f(sp), f(ks), f(vs), f(ss))
```

```python
import contextlib
import numpy as np
import concourse.bass as bass
import concourse.mybir as mybir
from concourse.bass_utils import run_bass_kernel_spmd

F32 = mybir.dt.float32
BF16 = mybir.dt.bfloat16
AF = mybir.ActivationFunctionType
ALU = mybir.AluOpType

D = 1024
DFF = 2816
PW = 8208
EPS = 1e-6
NCORES = 8

COMPUTE = ("pe", "act", "dve")
QUEUES = ("sp", "pool")
NDMASEM = 12

OVERLAPS = {
    "gT": ["ebT", "enbT", "la", "oaT", "obT", "qdT", "kiT"],
    "ebT": ["gT", "oaT"], "oaT": ["gT", "ebT"],
    "enbT": ["gT", "obT"], "obT": ["gT", "enbT"],
    "la": ["gT", "qdT", "kiT"], "qdT": ["gT", "la"], "kiT": ["gT", "la"],
    "dke": ["oa", "h2"], "oa": ["dke", "h2"], "va": ["ob", "h2"], "ob": ["va", "h2"], "h2": ["dke", "oa", "va", "ob"],
    "hout": ["rr", "qbT"], "rr": ["hout"], "qbT": ["hout"],
    "Pb": ["Pbg"], "Pbg": ["Pb"],
}


def kname(k):
    return k[0] if isinstance(k, tuple) else k


class Sched:
    def __init__(self):
        self.ops = []
        self.last_w = {}
        self.readers = {}
        self.nseq = {e: 0 for e in COMPUTE + QUEUES}
        self.pending = {}

    def enter(self, name):
        deps = set()
        others = set(OVERLAPS[name])
        for k, v in self.last_w.items():
            if kname(k) in others:
                deps.add(v)
        for k, v in self.readers.items():
            if kname(k) in others:
                deps.update(v)
        best = {}
        keep = set()
        for d in deps:
            o = self.ops[d]
            if o["dma"]:
                keep.add(d)
            else:
                if o["eng"] not in best or self.ops[best[o["eng"]]]["seq"] < o["seq"]:
                    best[o["eng"]] = d
        keep.update(best.values())
        self.pending[name] = keep

    def op(self, eng, fn, reads=(), writes=(), dma=False):
        idx = len(self.ops)
        raw = set()
        war = set()
        for r in reads:
            if r in self.last_w:
                raw.add(self.last_w[r])
            if kname(r) == "ps":
                for rd in self.readers.get(r, ()):
                    war.add(rd)
        for w in writes:
            if w in self.last_w:
                raw.add(self.last_w[w])
            for rd in self.readers.get(w, ()):
                war.add(rd)
            p = self.pending.get(kname(w))
            if p:
                raw.update(p)
        for r in reads:
            self.readers.setdefault(r, []).append(idx)
        for w in writes:
            self.last_w[w] = idx
            self.readers[w] = []
        seq = self.nseq[eng]
        self.nseq[eng] += 1
        self.ops.append(dict(eng=eng, fn=fn, raw=raw, war=war - raw, dma=dma, seq=seq,
                             sig=False, waits=[], idx=idx))
        return idx

    def barrier(self, eng, deps):
        idx = len(self.ops)
        seq = self.nseq[eng]
        self.nseq[eng] += 1
        self.ops.append(dict(eng=eng, fn=None, raw=set(deps), war=set(), dma=False, seq=seq,
                             sig=False, waits=[], idx=idx))
        return idx

    def analyze(self):
        ops = self.ops
        seen = {e: {x: -1 for x in COMPUTE} for e in COMPUTE + QUEUES}
        seen_dma = {e: set() for e in COMPUTE + QUEUES}
        dma_count = {q: 0 for q in QUEUES}
        dma_hist = {q: [] for q in QUEUES}
        for o in ops:
            e = o["eng"]
            deps = [(d, True) for d in sorted(o["raw"])] + [(d, False) for d in sorted(o["war"])]
            if o["dma"]:
                n = dma_count[e]
                dma_count[e] += 1
                o["dsem"] = n % NDMASEM
                o["dval"] = 16 * (n // NDMASEM + 1)
                if n >= NDMASEM:
                    deps.append((dma_hist[e][n - NDMASEM], True))
                dma_hist[e].append(o["idx"])
            for d, is_raw in deps:
                a = ops[d]
                if a["dma"]:
                    if d in seen_dma[e]:
                        continue
                    o["waits"].append(d)
                    seen_dma[e].add(d)
                else:
                    ae = a["eng"]
                    if a["fn"] is None:
                        continue
                    if ae == e and ae == "pe":
                        continue
                    if seen[e][ae] >= a["seq"]:
                        continue
                    o["waits"].append(d)
                    a["sig"] = True
                for x in COMPUTE:
                    if a["vc"][x] > seen[e][x]:
                        seen[e][x] = a["vc"][x]
            vc = dict(seen[e])
            if e in COMPUTE:
                vc[e] = o["seq"]
            o["vc"] = vc
        cnt = {e: 0 for e in COMPUTE}
        for o in ops:
            if o["eng"] in COMPUTE and o["sig"]:
                cnt[o["eng"]] += 1
                o["sval"] = cnt[o["eng"]]

    def emit(self, sems, dsems, block):
        ops = self.ops
        per = {e: [o for o in ops if o["eng"] == e] for e in COMPUTE + QUEUES}

        def run(engname, eng):
            for o in per[engname]:
                for d in o["waits"]:
                    a = ops[d]
                    if a["dma"]:
                        eng.wait_ge(dsems[a["eng"]][a["dsem"]], a["dval"])
                    else:
                        eng.wait_ge(sems[a["eng"]], a["sval"])
                if o["fn"] is None:
                    continue
                ins = o["fn"](eng)
                if o["dma"]:
                    ins.then_inc(dsems[engname][o["dsem"]], 16)
                elif o["sig"]:
                    ins.then_inc(sems[engname], 1)

        block.tensor(lambda eng: run("pe", eng))
        block.scalar(lambda eng: run("act", eng))
        block.vector(lambda eng: run("dve", eng))
        block.gpsimd(lambda eng: run("pool", eng))
        block.sync(lambda eng: run("sp", eng))


class NullSched:
    def op(self, *a, **k):
        return 0

    def enter(self, name):
        pass


NSLOT = 3
LOOKAHEAD = 2
DIRECT0 = False


class WStream:
    def __init__(self, ring):
        self.ring = ring
        self.plan = []
        self.rearr = []
        self.planning = True
        self.issued = 0
        self.taken = 0
        self.S = None

    def view(self, slot, shape):
        a, b = shape
        return self.ring[:, slot, 0:a * b].rearrange("p (a b) -> p a b", a=a)


class StopEmit(Exception):
    pass


def build(NSEQ, SEQLEN, debug=(), stop_at=None):
    nc = bass.Bass("TRN2", target_bir_lowering=False)
    NTOK = NSEQ * SEQLEN
    NT = SEQLEN // 512

    def din(name, shape, dt=F32):
        return nc.dram_tensor(name, shape, dt, kind="ExternalInput").ap()

    def dout(name, shape):
        return nc.dram_tensor(name, shape, F32, kind="ExternalOutput").ap()

    def dscr(name, shape):
        return nc.dram_tensor(name, shape, BF16, kind="Internal").ap()

    xp = din("xp", [NTOK, D])
    xsm = din("xs", [64, D])
    ck = din("ck", [512, D])
    cv = din("cv", [512, D])
    st = din("st", [4, 128, 256])
    wsrc = dict(
        w1i=din("w1i", [D, 2 * DFF]), w1o=din("w1o", [DFF, D]), win=din("win", [D, PW]),
        wbg=din("wbg", [D, D]), wba=din("wba", [D, D]), wo=din("wo", [D, D]),
        w2i=din("w2i", [D, 2 * DFF]), w2o=din("w2o", [DFF, D]), wgate=din("wgate", [17, 512]))
    gains_d = din("gains", [128, 4, 8])
    gnf_d = din("gnf", [1, D])
    biasT_d = din("biasT", [128, 16, 2, 128])
    cb_d = din("cb", [128, 16])
    consts_d = din("consts", [128, 4, 128])

    yp = dout("yp", [NTOK, D])
    ys = dout("ys", [64, D])
    kpo = dout("kpo", [NSEQ * 512, D])
    vpo = dout("vpo", [NSEQ * 512, D])
    spo = dout("spo", [NSEQ, 4, 128, 256])
    kso = dout("kso", [64, D])
    vso = dout("vso", [64, D])
    sso = dout("sso", [4, 128, 256])
    dbg_out = {}

    wfa_b = dscr("wfa_b", [D, 16])
    wgate_b = dscr("wgate_b", [17, 512])
    wsc_holder = {}

    def sb(name, shape, dt):
        return nc.alloc_sbuf_tensor("s_" + name, shape, dt)

    h = sb("h", [128, 4, D], F32)
    xnT = sb("xnT", [128, 8, 512], BF16)
    mixT = sb("mixT", [128, 8, 512], BF16)
    UG = sb("UG", [128, 12288], BF16)
    gT = UG[:, 0:22 * 512].rearrange("p (j t) -> p j t", j=22)
    ebT = UG[:, 0:4096].bitcast(F32).rearrange("p (h t) -> p h t", h=4)
    oaT = UG[:, 0:4096].rearrange("p (k t) -> p k t", k=8)
    enbT = UG[:, 4096:8192].bitcast(F32).rearrange("p (h t) -> p h t", h=4)
    obT = UG[:, 4096:8192].rearrange("p (k t) -> p k t", k=8)
    la = UG[:, 8192:12288].bitcast(F32).rearrange("p (s d) -> p s d", s=4)
    qdT = UG[:, 8192:10240].rearrange("p (h t) -> p h t", h=4)
    kiT = UG[:, 10240:12288].rearrange("p (h t) -> p h t", h=4)
    U45 = sb("U45", [128, 8192], BF16)
    dke = U45[:, 0:4096].bitcast(F32).rearrange("p (s d) -> p s d", s=4)
    oa = U45[:, 0:4096].rearrange("p (s d) -> p s d", s=4)
    va = U45[:, 4096:8192].rearrange("p (s d) -> p s d", s=4)
    ob = U45[:, 4096:8192].rearrange("p (s d) -> p s d", s=4)
    h2 = U45[:, :].bitcast(F32).rearrange("p (s d) -> p s d", s=4)
    U67 = sb("U67", [128, 8192], BF16)
    rr = U67[:, 0:4096].rearrange("p (s d) -> p s d", s=4)
    qbT = U67[:, 4096:8192].rearrange("p (k t) -> p k t", k=8)
    hout = U67[:, :].bitcast(F32).rearrange("p (s d) -> p s d", s=4)
    kend = sb("kend", [128, 4, 512], BF16)
    faT = sb("faT", [32, 512], BF16)
    Pb = sb("Pb", [128, 4, 5, 128], BF16)
    tmpf = sb("tmpf", [128, 2, 512], F32)
    xb = sb("xb", [128, 2, D], BF16)
    junk = sb("junk", [128, 2, D], BF16)
    ystage = sb("ystage", [128, 2, 512], F32)
    kbT = sb("kbT", [128, 8, 1024], BF16)
    vaug = sb("vaug", [128, 8, 16, 65], BF16)
    Sf = sb("Sf", [128, 4, 256], F32)
    Sb = sb("Sb", [128, 3, 4, 256], BF16)
    dec = sb("dec", [128, 4, 8], F32)
    biasT = sb("biasT", [128, 16, 2, 128], F32)
    cbs = sb("cbs", [128, 16], F32)
    constf = sb("constf", [128, 4, 128], F32)
    identb = sb("identb", [128, 128], BF16)
    gains = sb("gains", [128, 4, 8], F32)
    gnfb = sb("gnfb", [128, D], F32)
    epsb = sb("epsb", [128, 1], F32)
    stat = sb("stat", [128, 40], F32)
    rec = sb("rec", [128, 2, 8], F32)
    wfa = sb("wfa", [128, 8, 16], BF16)
    wgate = sb("wgate", [32, 512], BF16)
    ring = sb("ring", [128, NSLOT, 4096], BF16)
    ps = nc.alloc_psum_tensor("ps", [128, 8, 512], F32)

    TRI = constf[:, 1, :]
    TRIC = constf[:, 2, :]
    CMASK = constf[:, 3, :]

    ws = WStream(ring)
    POOLS = {"A": [0, 1, 2, 3], "B": [4, 5, 6, 7], "C": [0, 1, 2, 3, 4, 5], "D": [6, 7]}
    psrr = {k: 0 for k in POOLS}

    def ps_alloc(pool):
        i = psrr[pool]
        psrr[pool] = (i + 1) % len(POOLS[pool])
        return POOLS[pool][i]

    def bank(b):
        return ps[:, b, :]

    def bankb(b):
        return ps[:, b, :].bitcast(BF16)

    dumps = []

    def emit_all(S):
        for k_ in psrr:
            psrr[k_] = 0

        marks = {}
        late_init = []

        def mark(name):
            marks[name] = marks.get(name, 0) + 1
            if stop_at is not None and stop_at == "%s:%d" % (name, marks[name]):
                raise StopEmit()

        def dump(name, ap, key):
            if name not in debug or isinstance(S, NullSched) or name in dbg_out:
                return
            shape = list(ap.shape)
            t = nc.dram_tensor("dbg_" + name, shape, ap.dtype, kind="ExternalOutput").ap()
            dbg_out[name] = t
            S.op("pool", lambda e: e.dma_start(out=t, in_=ap), reads=key, dma=True)

        S.op("pool", lambda e: e.dma_start(out=wfa_b, in_=wsrc["win"][:, 3072:3088]), writes=["wfa_b"], dma=True)
        S.op("pool", lambda e: e.dma_start(out=wgate_b, in_=wsrc["wgate"]), writes=["wgate_b"], dma=True)
        if not ws.planning and not DIRECT0:
            wsc = wsc_holder["t"]
            for j in range(ws.nbt):
                parts_j, shape_j = ws.plan[j]
                for (name_j, reg_j, fn_j), rearr_j in zip(parts_j, ws.rearr[j]):
                    src_j = reg_j[1](wsrc[name_j]).rearrange(rearr_j, p=128)
                    dst_j = fn_j(wsc[j])
                    S.op("pool", (lambda e, d=dst_j, s_=src_j: e.dma_start(out=d, in_=s_)), writes=[("wsc", j)], dma=True)

        S.op("sp", lambda e: e.dma_start(out=constf[:], in_=consts_d), writes=["constf"], dma=True)
        S.op("sp", lambda e: e.dma_start(out=gains[:], in_=gains_d), writes=["gains"], dma=True)
        S.op("sp", lambda e: e.dma_start(out=gnfb[:], in_=gnf_d.partition_broadcast(128)), writes=["gnfb"], dma=True)
        late_init.append(lambda: S.op("sp", lambda e: e.dma_start(out=biasT[:], in_=biasT_d), writes=["biasT"], dma=True))
        late_init.append(lambda: S.op("sp", lambda e: e.dma_start(out=cbs[:], in_=cb_d), writes=["cbs"], dma=True))
        S.op("dve", lambda e: e.tensor_copy(out=identb[:], in_=constf[:, 0, :]), reads=["constf"], writes=["identb"])
        S.op("dve", lambda e: e.memset(epsb[:], EPS), writes=["epsb"])
        S.op("dve", lambda e: e.memset(faT[:], 1.0), writes=["faT"])
        S.op("dve", lambda e: e.memset(vaug[:], 1.0), writes=[("vaug", sl_, hf_) for sl_ in range(8) for hf_ in range(2)])
        S.op("sp", lambda e: e.dma_start(out=wfa[:], in_=wfa_b.rearrange("(k p) n -> p k n", p=128)),
             reads=["wfa_b"], writes=["wfa"], dma=True)
        S.op("sp", lambda e: e.dma_start(out=wgate[0:17, :], in_=wgate_b), reads=["wgate_b"], writes=["wgate"], dma=True)

        mark("init")
        def getw(parts, shape):
            if ws.planning:
                ws.plan.append(([(n_, r_, f_) for (n_, r_, f_, _) in parts], shape))
                ws.rearr.append([x[3] for x in parts])
                return ws.view(0, shape), ("wring", 0)
            i = ws.taken
            assert ws.plan[i][1] == shape, (i, ws.plan[i][1], shape)
            wsc = wsc_holder["t"]
            while ws.issued < min(len(ws.plan), i + LOOKAHEAD + 1):
                j = ws.issued
                parts_j, shape_j = ws.plan[j]
                slot = j % NSLOT
                jb = j % ws.nbt
                n_el = shape_j[0] * shape_j[1]
                if j < ws.nbt and DIRECT0:
                    for (name_j, reg_j, fn_j), rearr_j in zip(parts_j, ws.rearr[j]):
                        src_j = reg_j[1](wsrc[name_j]).rearrange(rearr_j, p=128)
                        dst_j = fn_j(ring[:, slot, :])
                        S.op("pool", (lambda e, d=dst_j, s_=src_j: e.dma_start(out=d, in_=s_)), writes=[("wring", slot)], dma=True)
                    S.op("sp", (lambda e, slot=slot, jb=jb, n_el=n_el: e.dma_start(out=wsc[jb][:, 0:n_el], in_=ring[:, slot, 0:n_el])),
                         reads=[("wring", slot)], writes=[("wsc", jb)], dma=True)
                else:
                    S.op("sp", (lambda e, slot=slot, jb=jb, n_el=n_el: e.dma_start(out=ring[:, slot, 0:n_el], in_=wsc[jb][:, 0:n_el])),
                         reads=[("wsc", jb)], writes=[("wring", slot)], dma=True)
                ws.issued += 1
            ws.taken += 1
            slot = i % NSLOT
            return ws.view(slot, shape), ("wring", slot)

        def v3(base, a_, b_):
            return base[:, 0:a_ * b_].rearrange("p (a b) -> p a b", a=a_)

        def colreg(c0, n):
            return ((0, c0), lambda t: t[:, c0:c0 + n])

        def wk(name, c0, ncols):
            return getw([(name, colreg(c0, ncols), lambda base: v3(base, 8, ncols), "(k p) n -> p k n")], (8, ncols))

        def wk2(name1, c1, name2, c2, n):
            return getw([(name1, colreg(c1, n), lambda base: v3(base, 8, 2 * n)[:, :, 0:n], "(k p) n -> p k n"),
                         (name2, colreg(c2, n), lambda base: v3(base, 8, 2 * n)[:, :, n:2 * n], "(k p) n -> p k n")],
                        (8, 2 * n))

        def wj(name, j0, nj, c0):
            reg = ((j0, c0), lambda t: t[j0 * 128:(j0 + nj) * 128, c0:c0 + 512])
            return getw([(name, reg, lambda base: v3(base, nj, 512), "(j p) n -> p j n")], (nj, 512))

        def mmgroup(out_ap, pairs, reads, writes, start=True, stop=True):
            def fn(e):
                n = len(pairs)
                ins = None
                for i, (l, r) in enumerate(pairs):
                    ins = e.matmul(out_ap, lhsT=l, rhs=r, start=(start and i == 0), stop=(stop and i == n - 1))
                return ins
            return S.op("pe", fn, reads=reads, writes=writes)

        def rstd_of(src_ap, TP, n, col, junk_key, junk_ap, reads):
            S.op("act", lambda e: e.activation(out=junk_ap, in_=src_ap, func=AF.Square, accum_out=stat[:TP, col:col + 1]),
                 reads=reads, writes=[junk_key, ("stat", col)])
            S.op("act", lambda e: e.activation(out=stat[:TP, col + 1:col + 2], in_=stat[:TP, col:col + 1], func=AF.Sqrt,
                                               scale=1.0 / n, bias=epsb[:TP, 0:1]),
                 reads=[("stat", col), "epsb"], writes=[("stat", col + 1)])
            S.op("dve", lambda e: e.reciprocal(out=stat[:TP, col + 1:col + 2], in_=stat[:TP, col + 1:col + 2]),
                 reads=[("stat", col + 1)], writes=[("stat", col + 1)])

        def sqrt_table_prefetch():
            S.op("act", lambda e: e.activation(out=stat[0:1, 32:33], in_=epsb[0:1, 0:1], func=AF.Sqrt),
                 reads=["epsb"], writes=[("stat", 32)])

        def transpose_sub(src_ap, TP, dstT, s, gidx, reads, dstkey, pool="A"):
            b = ps_alloc(pool)
            pb_ = bankb(b)

            def fn(e):
                ins = None
                for k in range(8):
                    ins = e.transpose(out=pb_[:, k * 128:k * 128 + TP], in_=src_ap[:, k * 128:(k + 1) * 128],
                                      identity=identb[:TP, :TP])
                return ins
            S.op("pe", fn, reads=list(reads) + ["identb"], writes=[("ps", b)])
            src3 = pb_.rearrange("p (k t) -> p k t", k=8)[:, :, 0:TP]
            dst3 = dstT[:, :, s * 128:s * 128 + TP]
            if gidx is None:
                S.op("act", lambda e: e.activation(out=dst3, in_=src3, func=AF.Copy),
                     reads=[("ps", b)], writes=[dstkey])
            else:
                g3 = gains[:, gidx, :].unsqueeze(2).broadcast_to([128, 8, TP])
                S.op("dve", lambda e: e.tensor_tensor(out=dst3, in0=src3, in1=g3, op=ALU.mult),
                     reads=[("ps", b), "gains"], writes=[dstkey])

        def norm_stats(NS, TP, src, srcname):
            for s in range(NS):
                if s < 2:
                    S.op("act", (lambda e, s=s: e.activation(out=junk[:TP, 0, :], in_=src[:TP, s, :], func=AF.Square,
                                                             accum_out=stat[:TP, s:s + 1])),
                         reads=[(srcname, s)], writes=[("junk", 0), ("stat", s)])
                else:
                    S.op("dve", (lambda e, s=s: e.scalar_tensor_tensor(
                        out=junk[:TP, 1, :], in0=src[:TP, s, :], scalar=1.0, in1=src[:TP, s, :],
                        op0=ALU.mult, op1=ALU.mult, accum_out=stat[:TP, s:s + 1])),
                        reads=[(srcname, s)], writes=[("junk", 1), ("stat", s)])
            S.op("act", lambda e: e.activation(out=stat[:TP, 4:4 + NS], in_=stat[:TP, 0:NS], func=AF.Sqrt,
                                               scale=1.0 / D, bias=epsb[:TP, 0:1]),
                 reads=[("stat", s) for s in range(NS)] + ["epsb"], writes=[("stat", 4 + s) for s in range(NS)])
            S.op("dve", lambda e: e.reciprocal(out=stat[:TP, 4:4 + NS], in_=stat[:TP, 4:4 + NS]),
                 reads=[("stat", 4 + s) for s in range(NS)], writes=[("stat", 4 + s) for s in range(NS)])

        def norm_apply(NS, TP, gidx, dstT, dstname, src, srcname, scol=4):
            for s in range(NS):
                pbuf = s % 2
                S.op("act", (lambda e, s=s, pbuf=pbuf: e.activation(
                    out=xb[:TP, pbuf, :], in_=src[:TP, s, :], func=AF.Copy, scale=stat[:TP, scol + s:scol + s + 1])),
                    reads=[(srcname, s), ("stat", scol + s)], writes=[("xb", pbuf)])
                transpose_sub(xb[:TP, pbuf, :], TP, dstT, s, gidx, [("xb", pbuf)], (dstname, s))

        def norm_T(T, NS, TP, gidx, dstT, dstname, src=None, srcname="h"):
            src = h if src is None else src
            norm_stats(NS, TP, src, srcname)
            norm_apply(NS, TP, gidx, dstT, dstname, src, srcname)

        def ffn(T, NS, TP, gidx, win_name, wout_name, dst=None, dstname="h", res=None, resname="h", norm_done=False,
                mid_hook=None, pre_hook=None):
            dst = h if dst is None else dst
            res = h if res is None else res
            S.enter("gT")
            if not norm_done:
                norm_T(T, NS, TP, gidx, xnT, "xnT")
            if pre_hook is not None:
                pre_hook()
            xkeys = [("xnT", s) for s in range(NS)]
            dump("xnT_" + win_name, xnT[:, :, 0:T], xkeys)
            for blk in range(11):
                if pending_final and blk >= 1:
                    pending_final.pop(0)()
                w, kw = wk2(win_name, blk * 256, win_name, DFF + blk * 256, 256)
                for c in range(2):
                    j = blk * 2 + c
                    bg = ps_alloc("A")
                    bu = ps_alloc("A")
                    mmgroup(bank(bg)[:, 0:T], [(w[:, k, c * 128:(c + 1) * 128], xnT[:, k, 0:T]) for k in range(8)],
                            xkeys + [kw], [("ps", bg)])
                    mmgroup(bank(bu)[:, 0:T], [(w[:, k, 256 + c * 128:256 + (c + 1) * 128], xnT[:, k, 0:T]) for k in range(8)],
                            xkeys + [kw], [("ps", bu)])
                    tb = j % 2
                    S.op("act", (lambda e, bg=bg, tb=tb: e.activation(out=tmpf[:, tb, 0:T], in_=bank(bg)[:, 0:T], func=AF.Silu)),
                         reads=[("ps", bg)], writes=[("tmpf", tb)])
                    S.op("dve", (lambda e, bu=bu, tb=tb, j=j: e.tensor_tensor(out=gT[:, j, 0:T], in0=bank(bu)[:, 0:T],
                                                                              in1=tmpf[:, tb, 0:T], op=ALU.mult)),
                         reads=[("ps", bu), ("tmpf", tb)], writes=[("gT", j)])
            sqrt_table_prefetch()
            dump("gT_" + win_name, gT[:, :, 0:T], [("gT", j) for j in range(22)])
            for half in range(2):
                if half == 1 and mid_hook is not None:
                    mid_hook()
                banks = [ps_alloc("B" if half == 0 else "A") for _ in range(NS)]
                for jb in range(3):
                    j0 = jb * 8
                    nj = 8 if jb < 2 else 6
                    w, kw = wj(wout_name, j0, nj, half * 512)
                    for s in range(NS):
                        mmgroup(bank(banks[s])[:TP, :],
                                [(gT[:, j0 + jj, s * 128:s * 128 + TP], w[:, jj, :]) for jj in range(nj)],
                                [("gT", j0 + jj) for jj in range(nj)] + [kw], [("ps", banks[s])],
                                start=(jb == 0), stop=(jb == 2))
                for s in range(NS):
                    hv = res[:TP, s, half * 512:(half + 1) * 512]
                    dv = dst[:TP, s, half * 512:(half + 1) * 512]
                    S.op("dve", (lambda e, hv=hv, dv=dv, b=banks[s]: e.scalar_tensor_tensor(
                        out=dv, in0=bank(b)[:TP, :], scalar=0.5, in1=hv, op0=ALU.mult, op1=ALU.add)),
                        reads=[("ps", banks[s]), (resname, s)], writes=[(dstname, s)])

        def tile(kind, q, t):
            if kind == "p":
                T, NS, TP = 512, 4, 128
                tok0 = q * SEQLEN + t * 512
                xsrc = xp[tok0:tok0 + 512, :].rearrange("(s p) d -> p s d", p=128)
                first, last = (t == 0), (t == NT - 1)
                gp0 = 4 * t
                ydst = yp
            else:
                T, NS, TP = 64, 1, 64
                tok0 = 0
                xsrc = xsm.rearrange("(s p) d -> p s d", p=64)
                first, last = False, True
                gp0 = 4
                ydst = ys
            if not prefetched:
                prefetch_x(kind, q, t)
            prefetched.clear()

            if kind == "p" and first:
                S.op("dve", lambda e: e.memset(Sf[:], 0.0), writes=[("Sf", hh) for hh in range(4)])
                S.op("dve", lambda e: e.memset(Sb[:, 0, :, :], 0.0), writes=[("Sb", 0, hh) for hh in range(4)])
            if kind == "s":
                S.op("sp", lambda e: e.dma_start(out=Sf[:], in_=st.rearrange("h d v -> d h v")),
                     writes=[("Sf", hh) for hh in range(4)], dma=True)
                S.op("act", lambda e: e.activation(out=Sb[:, 0, :, :], in_=Sf[:], func=AF.Copy),
                     reads=[("Sf", hh) for hh in range(4)], writes=[("Sb", 0, hh) for hh in range(4)])
                for blk in range(4):
                    S.op("pool", (lambda e, blk=blk: e.dma_start(out=vaug[:, blk, :, 0:64],
                                                                 in_=cv[blk * 128:(blk + 1) * 128, :].rearrange("p (h d) -> p h d", h=16))),
                         writes=[("vaug", blk, 0), ("vaug", blk, 1)], dma=True)
                    pbuf = blk % 2
                    S.op("pool", (lambda e, blk=blk, pbuf=pbuf: e.dma_start(out=xb[:, pbuf, :], in_=ck[blk * 128:(blk + 1) * 128, :])),
                         writes=[("xb", pbuf)], dma=True)
                    b = ps_alloc("A")
                    pb_ = bankb(b)

                    def fn(e, pb_=pb_, pbuf=pbuf):
                        ins = None
                        for k in range(8):
                            ins = e.transpose(out=pb_[:, k * 128:(k + 1) * 128], in_=xb[:, pbuf, k * 128:(k + 1) * 128],
                                              identity=identb[:, :])
                        return ins
                    S.op("pe", fn, reads=[("xb", pbuf), "identb"], writes=[("ps", b)])
                    S.op("act", (lambda e, pb_=pb_, blk=blk: e.activation(
                        out=kbT[:, :, blk * 128:(blk + 1) * 128], in_=pb_.rearrange("p (k t) -> p k t", k=8), func=AF.Copy)),
                        reads=[("ps", b)], writes=[("kbT", blk, c_) for c_ in range(8)])

            mark("load")
            ffn(T, NS, TP, 0, "w1i", "w1o", res=h2, resname="h2", norm_done=True)

            dump("h1", h[:TP, 0:NS, :], [("h", s) for s in range(NS)])
            mark("ffn1")
            while late_init:
                late_init.pop(0)()
            norm_T(T, NS, TP, 1, xnT, "xnT")
            nkeys = [("xnT", s) for s in range(NS)]
            S.enter("ebT")
            S.enter("enbT")
            S.enter("la")
            S.enter("dke")
            b = ps_alloc("A")
            mmgroup(bank(b)[0:16, 0:T], [(wfa[:, k, :], xnT[:, k, 0:T]) for k in range(8)], nkeys + ["wfa"], [("ps", b)])
            S.op("act", (lambda e, b=b: e.activation(out=faT[0:16, 0:T], in_=bank(b)[0:16, 0:T], func=AF.Copy)),
                 reads=[("ps", b)], writes=["faT"])
            def m1_A(s):
                b = ps_alloc("A")
                mmgroup(bank(b)[:TP, :], [(faT[0:17, s * 128:s * 128 + TP], wgate[0:17, :])], ["faT", "wgate"], [("ps", b)])
                tb = s % 2
                S.op("act", (lambda e, b=b, tb=tb: e.activation(out=tmpf[:TP, tb, :], in_=bank(b)[:TP, :], func=AF.Exp, scale=-1.0)),
                     reads=[("ps", b)], writes=[("tmpf", tb)])
                S.op("act", (lambda e, s=s, tb=tb: e.activation(out=la[:TP, s, :], in_=tmpf[:TP, tb, :], func=AF.Ln, bias=1.0)),
                     reads=[("tmpf", tb)], writes=[("la", s)])
            def m1_B(s):
                b1 = ps_alloc("A")

                def fnc(e, s=s, b1=b1):
                    ins = None
                    for hh in range(4):
                        ins = e.matmul(bank(b1)[:, hh * 128:hh * 128 + TP], lhsT=la[:TP, s, hh * 128:(hh + 1) * 128],
                                       rhs=TRI[:TP, :TP], start=True, stop=True)
                    return ins
                S.op("pe", fnc, reads=[("la", s), "constf"], writes=[("ps", b1)])
                src3 = bank(b1).rearrange("p (h t) -> p h t", h=4)[:, :, 0:TP]
                S.op("act", (lambda e, s=s, src3=src3: e.activation(out=ebT[:, :, s * 128:s * 128 + TP], in_=src3,
                                                                    func=AF.Exp, scale=-1.0 / 16)),
                     reads=[("ps", b1)], writes=[("ebT", s)])
                S.op("act", (lambda e, s=s, src3=src3: e.activation(out=enbT[:, :, s * 128:s * 128 + TP], in_=src3,
                                                                    func=AF.Exp, scale=1.0 / 16)),
                     reads=[("ps", b1)], writes=[("enbT", s)])
                b2 = ps_alloc("A")
                mmgroup(bank(b2)[:TP, :], [(TRIC[:TP, :TP], la[:TP, s, :])], [("la", s), "constf"], [("ps", b2)])
                S.op("act", (lambda e, s=s, b2=b2: e.activation(out=dke[:TP, s, :], in_=bank(b2)[:TP, :], func=AF.Exp,
                                                                scale=-1.0 / 16)),
                     reads=[("ps", b2)], writes=[("dke", s)])
                srcd = bank(b1).rearrange("p (h t) -> p h t", h=4)[:, :, TP - 1:TP]
                S.op("act", (lambda e, s=s, srcd=srcd: e.activation(out=dec[:, :, s:s + 1], in_=srcd,
                                                                    func=AF.Exp, scale=-1.0 / 16)),
                     reads=[("ps", b1)], writes=[("dec", s)])
            S.enter("va")
            S.enter("rr")
            S.enter("qbT")
            ebkeys = [("ebT", s) for s in range(NS)]
            enkeys = [("enbT", s) for s in range(NS)]
            def proj_qa():
                w, kw = wk("win", 0, 512)
                for hh in range(4):
                    b = ps_alloc("A")
                    mmgroup(bank(b)[:, 0:T], [(w[:, k, hh * 128:(hh + 1) * 128], xnT[:, k, 0:T]) for k in range(8)],
                            nkeys + [kw], [("ps", b)])
                    S.op("dve", (lambda e, b=b, hh=hh: e.scalar_tensor_tensor(
                        out=qdT[:, hh, 0:T], in0=bank(b)[:, 0:T], scalar=128.0 ** -0.5, in1=ebT[:, hh, 0:T],
                        op0=ALU.mult, op1=ALU.mult)), reads=[("ps", b)] + ebkeys, writes=[("qdT", hh)])
            def proj_ka():
                w, kw = wk("win", 512, 512)
                for hh in range(4):
                    b = ps_alloc("A")
                    mmgroup(bank(b)[:, 0:T], [(w[:, k, hh * 128:(hh + 1) * 128], xnT[:, k, 0:T]) for k in range(8)],
                            nkeys + [kw], [("ps", b)])
                    S.op("dve", (lambda e, b=b, hh=hh: e.tensor_tensor(out=kiT[:, hh, 0:T], in0=bank(b)[:, 0:T],
                                                                       in1=enbT[:, hh, 0:T], op=ALU.mult)),
                         reads=[("ps", b)] + enkeys, writes=[("kiT", hh)])
                for s in range(NS):
                    b = ps_alloc("A")
                    mmgroup(bank(b)[:TP, :], [(xnT[:, k, s * 128:s * 128 + TP], w[:, k, :]) for k in range(8)],
                            [("xnT", s), kw], [("ps", b)])
                    S.op("dve", (lambda e, b=b, s=s: e.tensor_tensor(
                        out=kend[:TP, s, :], in0=bank(b)[:TP, :], in1=dke[:TP, s, :], op=ALU.mult)),
                        reads=[("ps", b), ("dke", s)], writes=[("kend", s)])
            def proj_va(half):
                w, kw = wk("win", 1024 + half * 512, 512)
                for s in range(NS):
                    b = ps_alloc("A")
                    mmgroup(bank(b)[:TP, :], [(xnT[:, k, s * 128:s * 128 + TP], w[:, k, :]) for k in range(8)],
                            [("xnT", s), kw], [("ps", b)])
                    S.op("dve", (lambda e, b=b, s=s, half=half: e.tensor_copy(
                        out=va[:TP, s, half * 512:(half + 1) * 512], in_=bank(b)[:TP, :])),
                        reads=[("ps", b)], writes=[("va", s, half)])
            def proj_ra(half):
                w, kw = wk("win", 2048 + half * 512, 512)
                for s in range(NS):
                    b = ps_alloc("A")
                    mmgroup(bank(b)[:TP, :], [(xnT[:, k, s * 128:s * 128 + TP], w[:, k, :]) for k in range(8)],
                            [("xnT", s), kw], [("ps", b)])
                    S.op("act", (lambda e, b=b, s=s, half=half: e.activation(
                        out=rr[:TP, s, half * 512:(half + 1) * 512], in_=bank(b)[:TP, :], func=AF.Silu)),
                        reads=[("ps", b)], writes=[("rr", s, half)])
            def proj_qb_kb_vb():
                for half in range(2):
                    w, kw = wk("win", 3088 + half * 512, 512)
                    for c in range(4):
                        b = ps_alloc("A")
                        mmgroup(bank(b)[:, 0:T], [(w[:, k, c * 128:(c + 1) * 128], xnT[:, k, 0:T]) for k in range(8)],
                                nkeys + [kw], [("ps", b)])
                        if c % 2 == 0:
                            S.op("act", (lambda e, b=b, c=c, half=half: e.activation(out=qbT[:, half * 4 + c, 0:T], in_=bank(b)[:, 0:T],
                                                                                     func=AF.Copy)),
                                 reads=[("ps", b)], writes=[("qbT", half * 4 + c)])
                        else:
                            S.op("dve", (lambda e, b=b, c=c, half=half: e.tensor_copy(out=qbT[:, half * 4 + c, 0:T], in_=bank(b)[:, 0:T])),
                                 reads=[("ps", b)], writes=[("qbT", half * 4 + c)])
                mark("m2d")
                slots = [(gp0 + s) % 8 for s in range(NS)]
                kpos0 = slots[0] * 128
                for half in range(2):
                    w, kw = wk("win", 4112 + half * 512, 512)
                    for c in range(4):
                        b = ps_alloc("A")
                        mmgroup(bank(b)[:, 0:T], [(w[:, k, c * 128:(c + 1) * 128], xnT[:, k, 0:T]) for k in range(8)],
                                nkeys + [kw], [("ps", b)])
                        if c % 2 == 0:
                            S.op("act", (lambda e, b=b, c=c, half=half: e.activation(out=kbT[:, half * 4 + c, kpos0:kpos0 + T],
                                                                                     in_=bank(b)[:, 0:T], func=AF.Copy)),
                                 reads=[("ps", b)], writes=[("kbT", sl, half * 4 + c) for sl in slots])
                        else:
                            S.op("dve", (lambda e, b=b, c=c, half=half: e.tensor_copy(out=kbT[:, half * 4 + c, kpos0:kpos0 + T],
                                                                                      in_=bank(b)[:, 0:T])),
                                 reads=[("ps", b)], writes=[("kbT", sl, half * 4 + c) for sl in slots])
                    if last:
                        for s in range(NS):
                            b = ps_alloc("A")
                            mmgroup(bank(b)[:TP, :], [(xnT[:, k, s * 128:s * 128 + TP], w[:, k, :]) for k in range(8)],
                                    [("xnT", s), kw], [("ps", b)])
                            ob_ = (2 * s + half) % 2
                            S.op("act", (lambda e, b=b, ob_=ob_: e.activation(out=ystage[:TP, ob_, :], in_=bank(b)[:TP, :], func=AF.Copy)),
                                 reads=[("ps", b)], writes=[("ystage", ob_)])
                            if kind == "p":
                                dst = kpo[q * 512 + s * 128:q * 512 + s * 128 + 128, half * 512:(half + 1) * 512]
                            else:
                                dst = kso[:, half * 512:(half + 1) * 512]
                            S.op("pool", (lambda e, dst=dst, ob_=ob_: e.dma_start(out=dst, in_=ystage[:TP, ob_, :])),
                                 reads=[("ystage", ob_)], dma=True)
                mark("m2e")
                for half in range(2):
                    w, kw = wk("win", 5136 + half * 512, 512)
                    for s in range(NS):
                        b = ps_alloc("A")
                        mmgroup(bank(b)[:TP, :], [(xnT[:, k, s * 128:s * 128 + TP], w[:, k, :]) for k in range(8)],
                                [("xnT", s), kw], [("ps", b)])
                        sl = slots[s]
                        S.op("act", (lambda e, b=b, sl=sl, half=half: e.activation(
                            out=vaug[:TP, sl, half * 8:(half + 1) * 8, 0:64],
                            in_=bank(b)[:TP, :].rearrange("p (h d) -> p h d", h=8), func=AF.Copy)),
                            reads=[("ps", b)], writes=[("vaug", sl, half)])
                        if last:
                            ob_ = (2 * s + half) % 2
                            S.op("dve", (lambda e, b=b, ob_=ob_: e.tensor_copy(out=ystage[:TP, ob_, :], in_=bank(b)[:TP, :])),
                                 reads=[("ps", b), ("vaug", sl, half)], writes=[("ystage", ob_)])
                            if kind == "p":
                                dst = vpo[q * 512 + s * 128:q * 512 + s * 128 + 128, half * 512:(half + 1) * 512]
                            else:
                                dst = vso[:, half * 512:(half + 1) * 512]
                            S.op("pool", (lambda e, dst=dst, ob_=ob_: e.dma_start(out=dst, in_=ystage[:TP, ob_, :])),
                                 reads=[("ystage", ob_)], dma=True)
            for s in range(NS):
                m1_A(s)
            proj_va(0)
            for s in range(0, min(2, NS)):
                m1_B(s)
            proj_va(1)
            for s in range(2, NS):
                m1_B(s)
            proj_ra(0)
            proj_ra(1)
            proj_qb_kb_vb()
            S.enter("qdT")
            S.enter("kiT")
            proj_qa()
            proj_ka()
            dump("qdT", qdT[:, :, 0:T], [("qdT", hh) for hh in range(4)])
            dump("kiT", kiT[:, :, 0:T], [("kiT", hh) for hh in range(4)])

            mark("m2")
            S.enter("oa")
            S.enter("oaT")
            gla_tail = []
            gla_epi = []
            S.enter("Pbg")
            sqrt_table_prefetch()
            sbase = (4 * t) if kind == "p" else 0
            info = {}

            def partA_all(s):
                rd = (sbase + s) % 3
                wr = (sbase + s + 1) % 3
                for hh in range(4):
                    bi = ps_alloc("A")
                    mmgroup(bank(bi)[:, 0:256], [(kend[:TP, s, hh * 128:(hh + 1) * 128], va[:TP, s, hh * 256:(hh + 1) * 256])],
                            [("kend", s), ("va", s, hh // 2)], [("ps", bi)])
                    bs = ps_alloc("A")
                    mmgroup(bank(bs)[:TP, 0:TP], [(kiT[:, hh, s * 128:s * 128 + TP], qdT[:, hh, s * 128:s * 128 + TP])],
                            [("kiT", hh), ("qdT", hh)], [("ps", bs)])
                    pslot = (hh, 1 + s % 2)
                    S.op("dve", (lambda e, bs=bs, pslot=pslot: e.tensor_tensor(
                        out=Pb[:TP, pslot[0], pslot[1], 0:TP], in0=bank(bs)[:TP, 0:TP], in1=CMASK[:TP, :TP], op=ALU.mult)),
                        reads=[("ps", bs), "constf"], writes=[("Pbg", pslot)])
                    S.op("dve", (lambda e, bi=bi, hh=hh: e.scalar_tensor_tensor(
                        out=Sf[:, hh, :], in0=Sf[:, hh, :], scalar=dec[:, hh, s:s + 1],
                        in1=bank(bi)[:, 0:256], op0=ALU.mult, op1=ALU.add)),
                        reads=[("Sf", hh), ("dec", s), ("ps", bi)], writes=[("Sf", hh)])
                    S.op("act", (lambda e, hh=hh, wr=wr: e.activation(out=Sb[:, wr, hh, :], in_=Sf[:, hh, :], func=AF.Copy)),
                         reads=[("Sf", hh)], writes=[("Sb", wr, hh)])
                info[s] = rd

            def partB_all(s):
                rd = info[s]
                bo = [ps_alloc("B"), ps_alloc("B")]
                for hh in range(4):
                    obank = bank(bo[hh // 2])
                    ocol = (hh % 2) * 256
                    pslot = (hh, 1 + s % 2)

                    def fno(e, obank=obank, ocol=ocol, hh=hh, pslot=pslot):
                        e.matmul(obank[:TP, ocol:ocol + 256], lhsT=qdT[:, hh, s * 128:s * 128 + TP],
                                 rhs=Sb[:, rd, hh, :], start=True, stop=False)
                        return e.matmul(obank[:TP, ocol:ocol + 256], lhsT=Pb[:TP, pslot[0], pslot[1], 0:TP],
                                        rhs=va[:TP, s, hh * 256:(hh + 1) * 256], start=False, stop=True)
                    S.op("pe", fno, reads=[("qdT", hh), ("Pbg", pslot), ("va", s, hh // 2), ("Sb", rd, hh)],
                         writes=[("ps", bo[hh // 2])])
                for hh in range(4):
                    obank = bank(bo[hh // 2])
                    ocol = (hh % 2) * 256
                    tb = hh % 2
                    S.op("act", (lambda e, hh=hh, obank=obank, ocol=ocol, tb=tb: e.activation(
                        out=tmpf[:TP, tb, 0:256], in_=obank[:TP, ocol:ocol + 256], func=AF.Square,
                        accum_out=stat[:TP, 8 + hh:9 + hh])),
                        reads=[("ps", bo[hh // 2])], writes=[("tmpf", tb), ("stat", 8 + hh)])
                S.op("act", lambda e: e.activation(out=stat[:TP, 12:16], in_=stat[:TP, 8:12], func=AF.Sqrt,
                                                   scale=1.0 / 256, bias=epsb[:TP, 0:1]),
                     reads=[("stat", 8 + hh) for hh in range(4)] + ["epsb"], writes=[("stat", 12 + hh) for hh in range(4)])
                S.op("dve", lambda e: e.reciprocal(out=stat[:TP, 12:16], in_=stat[:TP, 12:16]),
                     reads=[("stat", 12 + hh) for hh in range(4)], writes=[("stat", 12 + hh) for hh in range(4)])
                for hh in range(4):
                    obank = bank(bo[hh // 2])
                    ocol = (hh % 2) * 256
                    S.op("dve", (lambda e, hh=hh, obank=obank, ocol=ocol: e.scalar_tensor_tensor(
                        out=oa[:TP, s, hh * 256:(hh + 1) * 256], in0=obank[:TP, ocol:ocol + 256],
                        scalar=stat[:TP, 12 + hh:13 + hh], in1=rr[:TP, s, hh * 256:(hh + 1) * 256],
                        op0=ALU.mult, op1=ALU.mult)),
                        reads=[("ps", bo[hh // 2]), ("stat", 12 + hh), ("rr", s, hh // 2)], writes=[("oa", s, hh)])

            def gtail(s):
                transpose_sub(oa[:TP, s, :], TP, oaT, s, 3, [("oa", s, hh) for hh in range(4)], ("oaT", s), pool="A")

            partA_all(0)
            for s in range(1, NS):
                partA_all(s)
                partB_all(s - 1)
                if s >= 2:
                    gtail(s - 2)
            partB_all(NS - 1)
            if NS >= 2:
                gtail(NS - 2)
            gla_tail.append(lambda: gtail(NS - 1))

            if last:
                sdst = spo[q] if kind == "p" else sso
                S.op("pool", lambda e: e.dma_start(out=sdst.rearrange("h d v -> d h v"), in_=Sf[:]),
                     reads=[("Sf", hh) for hh in range(4)], dma=True)

            mark("m3")
            S.enter("ob")
            S.enter("obT")
            S.enter("Pb")
            accs = {}

            def att_sc(s, hd):
                gp = gp0 + s
                blks = [i for i in range(5) if gp - 4 + i >= 0]
                c = hd // 2
                pb0 = (hd % 2) * 64
                pbuf = (s * 16 + hd) % 4
                bA = ps_alloc("C")
                bB = ps_alloc("C")
                cst = [i for i in blks if i < 3]

                def fns(e):
                    ins = None
                    for i in blks:
                        sl = (gp - 4 + i) % 8
                        nk = TP if i == 4 else 128
                        outp = bank(bA)[:nk, i * 128:i * 128 + TP] if i < 3 else bank(bB)[:nk, (i - 3) * 128:(i - 3) * 128 + TP]
                        ins = e.matmul(outp, lhsT=kbT[pb0:pb0 + 64, c, sl * 128:sl * 128 + nk],
                                       rhs=qbT[pb0:pb0 + 64, c, s * 128:s * 128 + TP], start=True, stop=True)
                    return ins
                S.op("pe", fns, reads=[("qbT", c)] + [("kbT", (gp - 4 + i) % 8, c) for i in blks],
                     writes=[("ps", bA), ("ps", bB)])
                tb = (s * 16 + hd) % 2
                S.op("dve", lambda e: e.scalar_tensor_tensor(
                    out=tmpf[:, tb, 0:256].rearrange("p (b q) -> p b q", b=2)[:, :, 0:TP],
                    in0=bank(bB)[:, 0:256].rearrange("p (b q) -> p b q", b=2)[:, :, 0:TP], scalar=0.125,
                    in1=biasT[:, hd, :, 0:TP], op0=ALU.mult, op1=ALU.add),
                    reads=[("ps", bB), "biasT"], writes=[("tmpf", tb)])
                if cst:
                    i0 = cst[0]
                    ncst = len(cst)
                    S.op("act", lambda e: e.activation(
                        out=Pb[:, pbuf, i0:i0 + ncst, 0:TP],
                        in_=bank(bA)[:, i0 * 128:(i0 + ncst) * 128].rearrange("p (b q) -> p b q", b=ncst)[:, :, 0:TP],
                        func=AF.Exp, scale=0.125, bias=cbs[:, hd:hd + 1]),
                        reads=[("ps", bA), "cbs"], writes=[("Pb", pbuf, 0)])
                i3 = 0 if 3 in blks else 1
                S.op("act", lambda e: e.activation(
                    out=Pb[:, pbuf, 3 + i3:5, 0:TP],
                    in_=tmpf[:, tb, 0:256].rearrange("p (b q) -> p b q", b=2)[:, i3:2, 0:TP], func=AF.Exp),
                    reads=[("tmpf", tb)], writes=[("Pb", pbuf, 1)])
                if 0 in blks and TP == 128:
                    S.op("dve", lambda e: e.memset(Pb[0:64, pbuf, 0, 64:128], 0.0), writes=[("Pb", pbuf, 0)])

            GRP = [(0, 7), (7, 7), (14, 2)]

            def att_pv(s, hd):
                gp = gp0 + s
                blks = [i for i in range(5) if gp - 4 + i >= 0]
                pbuf = (s * 16 + hd) % 4
                g = 0 if hd < 7 else (1 if hd < 14 else 2)
                hd0, nh = GRP[g]
                if hd == hd0:
                    accs[(s, g)] = ps_alloc("D")
                ab = accs[(s, g)]
                acc = bank(ab)
                acol = (hd - hd0) * 65

                def fnv(e):
                    ins = None
                    for n_, i in enumerate(blks):
                        sl = (gp - 4 + i) % 8
                        nk = TP if i == 4 else 128
                        ins = e.matmul(acc[:TP, acol:acol + 65], lhsT=Pb[:nk, pbuf, i, 0:TP], rhs=vaug[:nk, sl, hd, :],
                                       start=(n_ == 0), stop=(n_ == len(blks) - 1))
                    return ins
                S.op("pe", fnv, reads=[("Pb", pbuf, 0), ("Pb", pbuf, 1)] + [("vaug", (gp - 4 + i) % 8, hd // 8) for i in blks],
                     writes=[("ps", ab)])
                if hd == hd0 + nh - 1:
                    a3 = acc[:TP, 0:nh * 65].rearrange("p (h d) -> p h d", h=nh)
                    rb = g % 2
                    S.op("dve", lambda e: e.reciprocal(out=rec[:TP, rb, 0:nh], in_=a3[:, :, 64]),
                         reads=[("ps", ab)], writes=[("rec", rb)])
                    S.op("dve", lambda e: e.tensor_tensor(
                        out=ob[:TP, s, hd0 * 64:(hd0 + nh) * 64].rearrange("p (h d) -> p h d", h=nh), in0=a3[:, :, 0:64],
                        in1=rec[:TP, rb, 0:nh].unsqueeze(2).broadcast_to([TP, nh, 64]), op=ALU.mult),
                        reads=[("ps", ab), ("rec", rb)], writes=[("ob", s, g)])

            items = [(s, hd) for s in range(NS) for hd in range(16)]
            DEPTH = 2
            if gla_epi:
                gla_epi.pop()()
            for i, (s, hd) in enumerate(items):
                att_sc(s, hd)
                if i == 6 and gla_tail:
                    gla_tail.pop()()
                if i >= DEPTH:
                    ps_, ph_ = items[i - DEPTH]
                    att_pv(ps_, ph_)
                    if ph_ == 15 and False:
                        pass
                if hd == 3 and s > 0:
                    transpose_sub(ob[:TP, s - 1, :], TP, obT, s - 1, None, [("ob", s - 1, g) for g in range(3)], ("obT", s - 1))
            for j in range(len(items) - DEPTH, len(items)):
                att_pv(*items[j])
            if gla_tail:
                gla_tail.pop()()
            ob_tail = [lambda: transpose_sub(ob[:TP, NS - 1, :], TP, obT, NS - 1, None, [("ob", NS - 1, g) for g in range(3)],
                                             ("obT", NS - 1))]

            mark("m4")
            oakeys = [("oaT", s) for s in range(NS)]
            obkeys = [("obT", s) for s in range(NS)]
            for mp in range(4):
                w, kw = wk2("win", 6160 + mp * 256, "wbg", mp * 256, 256)
                if mp == 0:
                    oakeys_ = list(oakeys)
                for c in range(2):
                    m = mp * 2 + c
                    b1 = ps_alloc("A")
                    b2 = ps_alloc("A")
                    mmgroup(bank(b1)[:, 0:T], [(w[:, k, c * 128:(c + 1) * 128], xnT[:, k, 0:T]) for k in range(8)],
                            nkeys + [kw], [("ps", b1)])
                    mmgroup(bank(b2)[:, 0:T], [(w[:, k, 256 + c * 128:256 + (c + 1) * 128], oaT[:, k, 0:T]) for k in range(8)],
                            oakeys + [kw], [("ps", b2)])
                    tb = m % 2
                    S.op("act", (lambda e, b1=b1, tb=tb: e.activation(out=tmpf[:, tb, 0:T], in_=bank(b1)[:, 0:T], func=AF.Sigmoid)),
                         reads=[("ps", b1)], writes=[("tmpf", tb)])
                    S.op("dve", (lambda e, b2=b2, tb=tb, m=m: e.tensor_tensor(out=mixT[:, m, 0:T], in0=bank(b2)[:, 0:T],
                                                                              in1=tmpf[:, tb, 0:T], op=ALU.mult)),
                         reads=[("ps", b2), ("tmpf", tb)], writes=[("mixT", m)])
                if ob_tail:
                    ob_tail.pop()()
                w, kw = wk2("win", 7184 + mp * 256, "wba", mp * 256, 256)
                for c in range(2):
                    m = mp * 2 + c
                    b1 = ps_alloc("A")
                    b2 = ps_alloc("A")
                    mmgroup(bank(b1)[:, 0:T], [(w[:, k, c * 128:(c + 1) * 128], xnT[:, k, 0:T]) for k in range(8)],
                            nkeys + [kw], [("ps", b1)])
                    mmgroup(bank(b2)[:, 0:T], [(w[:, k, 256 + c * 128:256 + (c + 1) * 128], obT[:, k, 0:T]) for k in range(8)],
                            obkeys + [kw], [("ps", b2)])
                    tb = m % 2
                    S.op("act", (lambda e, b1=b1, tb=tb: e.activation(out=tmpf[:, tb, 0:T], in_=bank(b1)[:, 0:T], func=AF.Sigmoid)),
                         reads=[("ps", b1)], writes=[("tmpf", tb)])
                    S.op("dve", (lambda e, b2=b2, tb=tb: e.tensor_tensor(out=tmpf[:, tb, 0:T], in0=bank(b2)[:, 0:T],
                                                                         in1=tmpf[:, tb, 0:T], op=ALU.mult)),
                         reads=[("ps", b2), ("tmpf", tb)], writes=[("tmpf", tb)])
                    S.op("dve", (lambda e, tb=tb, m=m: e.tensor_tensor(out=mixT[:, m, 0:T], in0=tmpf[:, tb, 0:T],
                                                                       in1=mixT[:, m, 0:T], op=ALU.add)),
                         reads=[("tmpf", tb), ("mixT", m)], writes=[("mixT", m)])
            sqrt_table_prefetch()
            mkeys = [("mixT", m) for m in range(8)]
            for half in range(2):
                w, kw = wk("wo", half * 512, 512)
                for s in range(NS):
                    b = ps_alloc("A")
                    mmgroup(bank(b)[:TP, :], [(mixT[:, k, s * 128:s * 128 + TP], w[:, k, :]) for k in range(8)],
                            mkeys + [kw], [("ps", b)])
                    hv = h[:TP, s, half * 512:(half + 1) * 512]
                    S.op("dve", (lambda e, hv=hv, b=b: e.tensor_tensor(out=hv, in0=bank(b)[:TP, :], in1=hv, op=ALU.add)),
                         reads=[("ps", b), ("h", s)], writes=[("h", s)])
            dump("hmix", h[:TP, 0:NS, :], [("h", s) for s in range(NS)])

            mark("m5")
            S.enter("hout")
            nxt = next_tile.get((kind, q, t))
            hookA = (lambda: prefetch_x(*nxt, part="load")) if nxt is not None else None
            hookB = (lambda: (prefetch_x(*nxt, part="apply"), prefetched.append(1))) if nxt is not None else None
            ffn(T, NS, TP, 2, "w2i", "w2o", dst=hout, dstname="hout", mid_hook=hookB, pre_hook=hookA)
            mark("ffn2")

            def final_stats(NS=NS, TP=TP):
                for s in range(NS):
                    S.op("act", (lambda e, s=s: e.activation(out=tmpf[:TP, s % 2, :].bitcast(BF16), in_=hout[:TP, s, :],
                                                             func=AF.Square, accum_out=stat[:TP, 16 + s:17 + s])),
                         reads=[("hout", s)], writes=[("tmpf", s % 2), ("stat", 16 + s)])
                S.op("act", lambda e: e.activation(out=stat[:TP, 20:20 + NS], in_=stat[:TP, 16:16 + NS], func=AF.Sqrt,
                                                   scale=1.0 / D, bias=epsb[:TP, 0:1]),
                     reads=[("stat", 16 + s) for s in range(NS)] + ["epsb"], writes=[("stat", 20 + s) for s in range(NS)])
                S.op("dve", lambda e: e.reciprocal(out=stat[:TP, 20:20 + NS], in_=stat[:TP, 20:20 + NS]),
                     reads=[("stat", 20 + s) for s in range(NS)], writes=[("stat", 20 + s) for s in range(NS)])

            def final_sub(s, TP=TP, ydst=ydst, tok0=tok0):
                col = 20 + s
                for half in range(2):
                    yb_ = half
                    S.op("dve", (lambda e, half=half, yb_=yb_: e.scalar_tensor_tensor(
                        out=ystage[:TP, yb_, :], in0=hout[:TP, s, half * 512:(half + 1) * 512], scalar=stat[:TP, col:col + 1],
                        in1=gnfb[:TP, half * 512:(half + 1) * 512], op0=ALU.mult, op1=ALU.mult)),
                        reads=[("hout", s), ("stat", col), "gnfb"], writes=[("ystage", yb_)])
                    dst_ = ydst[tok0 + s * 128:tok0 + s * 128 + TP, half * 512:(half + 1) * 512]
                    S.op("pool", (lambda e, dst_=dst_, yb_=yb_: e.dma_start(out=dst_, in_=ystage[:TP, yb_, :])),
                         reads=[("ystage", yb_)], dma=True)
            pending_final.append(final_stats)
            for s in range(NS):
                pending_final.append(lambda s=s, f=final_sub: f(s))

        pending_final = []
        prefetched = []
        order = [("p", q_, t_) for q_ in range(NSEQ) for t_ in range(NT)] + [("s", 0, 0)]
        next_tile = {order[i]: order[i + 1] for i in range(len(order) - 1)}

        def prefetch_x(kind, q, t, part="all"):
            if kind == "p":
                T, NS, TP = 512, 4, 128
                tok0 = q * SEQLEN + t * 512
                xsrc = xp[tok0:tok0 + 512, :].rearrange("(s p) d -> p s d", p=128)
            else:
                T, NS, TP = 64, 1, 64
                xsrc = xsm.rearrange("(s p) d -> p s d", p=64)
            if part in ("all", "load"):
                S.enter("h2")
                for s in range(NS):
                    S.op("sp", (lambda e, s=s: e.dma_start(out=h2[:TP, s, :], in_=xsrc[:, s, :])), writes=[("h2", s)], dma=True)
                for s in range(NS):
                    S.op("dve", (lambda e, s=s: e.scalar_tensor_tensor(
                        out=junk[:TP, 1, :], in0=h2[:TP, s, :], scalar=1.0, in1=h2[:TP, s, :],
                        op0=ALU.mult, op1=ALU.mult, accum_out=stat[:TP, 24 + s:25 + s])),
                        reads=[("h2", s)], writes=[("junk", 1), ("stat", 24 + s)])
                S.op("act", lambda e: e.activation(out=stat[:TP, 28:28 + NS], in_=stat[:TP, 24:24 + NS], func=AF.Sqrt,
                                                   scale=1.0 / D, bias=epsb[:TP, 0:1]),
                     reads=[("stat", 24 + s) for s in range(NS)] + ["epsb"], writes=[("stat", 28 + s) for s in range(NS)])
                S.op("dve", lambda e: e.reciprocal(out=stat[:TP, 28:28 + NS], in_=stat[:TP, 28:28 + NS]),
                     reads=[("stat", 28 + s) for s in range(NS)], writes=[("stat", 28 + s) for s in range(NS)])
            if part in ("all", "apply"):
                norm_apply(NS, TP, 0, xnT, "xnT", h2, "h2", scol=28)
        for q in range(NSEQ):
            for t in range(NT):
                tile("p", q, t)
        tile("s", 0, 0)
        while pending_final:
            pending_final.pop(0)()

    ws.planning = True
    try:
        emit_all(NullSched())
    except StopEmit:
        pass
    ws.planning = False
    ntiles = NSEQ * NT + 1
    if stop_at is None:
        assert len(ws.plan) % ntiles == 0
        ws.nbt = len(ws.plan) // ntiles
        for j in range(len(ws.plan)):
            assert ws.plan[j][1] == ws.plan[j % ws.nbt][1]
    else:
        ws.nbt = len(ws.plan)
    wsc_holder["t"] = nc.dram_tensor("wsc", [ws.nbt, 128, 4096], BF16, kind="Internal").ap()
    S = Sched()
    try:
        emit_all(S)
    except StopEmit:
        pass
    outs = [i for i, o in enumerate(S.ops) if o["dma"] and o["eng"] == "pool"]
    S.barrier("pool", outs)
    S.analyze()
    with contextlib.ExitStack() as stk:
        sems = {e: stk.enter_context(nc.semaphore("s_" + e)) for e in COMPUTE}
        dsems = {q_: [stk.enter_context(nc.semaphore("d_%s%d" % (q_, i))) for i in range(NDMASEM)] for q_ in QUEUES}
        block = stk.enter_context(nc.Block())
        S.emit(sems, dsems, block)
    return nc, dbg_out, S


def host_consts(attn_rel_bias):
    tab = np.asarray(attn_rel_bias, dtype=np.float32)[0]
    ext = np.concatenate([tab, np.repeat(tab[:, 256:257], 127, axis=1)], axis=1)
    p = np.arange(128)[:, None]
    qq = np.arange(128)[None, :]
    biasT = np.empty((128, 16, 2, 128), np.float32)
    biasT[:, :, 0, :] = np.transpose(ext[:, 256 + qq - p], (1, 0, 2))
    blkA = np.transpose(ext[:, 128 + qq - p], (1, 0, 2)).copy()
    blkA[64:128, :, 0:64] = -30000.0
    biasT[:, :, 1, :] = blkA
    cb = np.broadcast_to(tab[None, :, 256], (128, 16)).astype(np.float32).copy()
    consts = np.zeros((128, 4, 128), np.float32)
    consts[:, 0, :] = np.eye(128, dtype=np.float32)
    j = np.arange(128)[:, None]
    i = np.arange(128)[None, :]
    same = np.ones((128, 128), dtype=bool)
    consts[:, 1, :] = (same & (j <= i)).astype(np.float32)
    consts[:, 2, :] = (same & (j > i)).astype(np.float32)
    consts[:, 3, :] = (same & (j <= i)).astype(np.float32)
    return biasT, cb, consts


def make_common(norm_ffn1, w_ffn1_in, w_ffn1_out, norm_mix, w_in, w_gla_gate, b_gla_gate, gla_norm, attn_rel_bias,
                w_branch_gla, w_branch_att, w_out, norm_ffn2, w_ffn2_in, w_ffn2_out, norm_final):
    f = lambda a: np.ascontiguousarray(np.asarray(a, dtype=np.float32))
    biasT, cb, consts = host_consts(attn_rel_bias)
    gains = np.stack([f(norm_ffn1)[0], f(norm_mix)[0], f(norm_ffn2)[0], f(gla_norm)[0]], axis=0)
    gains = np.ascontiguousarray(gains.reshape(4, 8, 128).transpose(2, 0, 1))
    return dict(
        w1i=f(w_ffn1_in)[0], w1o=f(w_ffn1_out)[0], win=f(w_in)[0], wbg=f(w_branch_gla)[0], wba=f(w_branch_att)[0],
        wo=f(w_out)[0], w2i=f(w_ffn2_in)[0], w2o=f(w_ffn2_out)[0],
        wgate=np.ascontiguousarray(np.concatenate([f(w_gla_gate)[0], f(b_gla_gate)], axis=0)),
        gains=gains, gnf=f(norm_final).reshape(1, D), biasT=biasT, cb=cb, consts=consts)


_CACHE = {}


def kernel(x_prompt, x_sample, cache_att_k, cache_att_v, state_gla, norm_ffn1, w_ffn1_in, w_ffn1_out, norm_mix, w_in,
           w_gla_gate, b_gla_gate, gla_norm, attn_rel_bias, w_branch_gla, w_branch_att, w_out, norm_ffn2, w_ffn2_in,
           w_ffn2_out, norm_final):
    x_prompt = np.asarray(x_prompt, dtype=np.float32)
    x_sample = np.asarray(x_sample, dtype=np.float32)
    cache_att_k = np.asarray(cache_att_k, dtype=np.float32)
    cache_att_v = np.asarray(cache_att_v, dtype=np.float32)
    state_gla = np.asarray(state_gla, dtype=np.float32)
    B, SEQ, _ = x_prompt.shape
    NB = x_sample.shape[0]
    n = NB
    NSEQ = B // n
    key = (NSEQ, SEQ)
    if key not in _CACHE:
        _CACHE[key] = build(NSEQ, SEQ)[0]
    nc = _CACHE[key]
    common = make_common(norm_ffn1, w_ffn1_in, w_ffn1_out, norm_mix, w_in, w_gla_gate, b_gla_gate, gla_norm, attn_rel_bias,
                         w_branch_gla, w_branch_att, w_out, norm_ffn2, w_ffn2_in, w_ffn2_out, norm_final)
    in_maps = []
    for c in range(n):
        m = dict(common)
        m["xp"] = np.ascontiguousarray(x_prompt[c * NSEQ:(c + 1) * NSEQ].reshape(NSEQ * SEQ, D))
        m["xs"] = np.ascontiguousarray(x_sample[c])
        m["ck"] = np.ascontiguousarray(cache_att_k[0, c].reshape(512, D))
        m["cv"] = np.ascontiguousarray(cache_att_v[0, c].reshape(512, D))
        m["st"] = np.ascontiguousarray(state_gla[0, c])
        in_maps.append(m)
    res = run_bass_kernel_spmd(nc, in_maps, core_ids=list(range(n)))
    R = res.results
    y_prompt = np.stack([R[c]["yp"].reshape(NSEQ, SEQ, D) for c in range(n)]).reshape(B, SEQ, D)
    y_sample = np.stack([R[c]["ys"] for c in range(n)])
    kp = np.stack([R[c]["kpo"].reshape(NSEQ, 512, 16, 64) for c in range(n)]).reshape(1, B, 512, 16, 64)
    vp = np.stack([R[c]["vpo"].reshape(NSEQ, 512, 16, 64) for c in range(n)]).reshape(1, B, 512, 16, 64)
    sp = np.stack([R[c]["spo"] for c in range(n)]).reshape(1, B, 4, 128, 256)
    ks = np.stack([R[c]["kso"].reshape(64, 16, 64) for c in range(n)])[None]
    vs = np.stack([R[c]["vso"].reshape(64, 16, 64) for c in range(n)])[None]
    ss = np.stack([R[c]["sso"] for c in range(n)])[None]
    f = lambda a: np.ascontiguousarray(a, dtype=np.float32)
    return (f(y_prompt), f(y_sample), f(kp), f(vp), f(sp), f(ks), f(vs), f(ss))
```
